# Optimizing a Trainium2 kernel written in Bass

```python
import math
import jax, jax.numpy as jnp
from jax import lax
import numpy as np

D_MODEL = 1024
BATCH = 4
SEQ = 8192
DEPTH = 1

HEAD_DIM = 64
SWA_Q_HEADS = 8
SWA_KV_HEADS = 2
SWA_GROUP = SWA_Q_HEADS // SWA_KV_HEADS
WINDOW = 128
DIFF_HEADS = 4
DIFF_V_DIM = 2 * HEAD_DIM
SWA_WIDTH = SWA_Q_HEADS * HEAD_DIM
DIFF_WIDTH = DIFF_HEADS * DIFF_V_DIM
MIX_WIDTH = SWA_WIDTH + DIFF_WIDTH
QA_COLS = SWA_Q_HEADS * HEAD_DIM
KA_COLS = SWA_KV_HEADS * HEAD_DIM
VA_COLS = SWA_KV_HEADS * HEAD_DIM
QB_COLS = DIFF_HEADS * 2 * HEAD_DIM
KB_COLS = DIFF_HEADS * 2 * HEAD_DIM
VB_COLS = DIFF_HEADS * DIFF_V_DIM
IN_COLS = QA_COLS + KA_COLS + VA_COLS + QB_COLS + KB_COLS + VB_COLS
IN_SPLITS = (QA_COLS, QA_COLS + KA_COLS, QA_COLS + KA_COLS + VA_COLS,
             QA_COLS + KA_COLS + VA_COLS + QB_COLS,
             QA_COLS + KA_COLS + VA_COLS + QB_COLS + KB_COLS)
Q_BLOCK = 128
D_FF = 4 * D_MODEL
CONV_WIDTH = 3
EPS = 1e-6

kernel_name = "hymba_swa_sink_diffattn_alibi_convffn"


def rms_norm(x, g):
    xf = x.astype(jnp.float32)
    xf = xf * lax.rsqrt(jnp.mean(xf * xf, axis=-1, keepdims=True) + EPS)
    return (xf * g.astype(jnp.float32)).astype(x.dtype)


def alibi_slopes(n):
    def pow2(m):
        start = 2.0 ** (-8.0 / m)
        return [start ** (i + 1) for i in range(m)]
    if math.log2(n).is_integer():
        s = pow2(n)
    else:
        c = 2 ** int(math.floor(math.log2(n)))
        s = pow2(c) + pow2(2 * c)[0::2][: n - c]
    return np.array(sorted(s, reverse=True), dtype=np.float32)


def swa_sink_attention(q, k, v, sinks, slopes):
    B, S = q.shape[0], q.shape[1]
    nb = S // WINDOW
    qb = q.reshape(B, nb, WINDOW, SWA_KV_HEADS, SWA_GROUP, HEAD_DIM)
    kb = k.reshape(B, nb, WINDOW, SWA_KV_HEADS, HEAD_DIM)
    vb = v.reshape(B, nb, WINDOW, SWA_KV_HEADS, HEAD_DIM)
    pad = ((0, 0), (1, 0), (0, 0), (0, 0), (0, 0))
    kk = jnp.concatenate([jnp.pad(kb, pad)[:, :-1], kb], axis=2)
    vv = jnp.concatenate([jnp.pad(vb, pad)[:, :-1], vb], axis=2)
    scores = jnp.einsum("bnqhgd,bnkhd->bnhgqk", qb, kk).astype(jnp.float32)
    scores = scores * (HEAD_DIM ** -0.5)
    qpos = jnp.arange(WINDOW)[:, None] + WINDOW
    kpos = jnp.arange(2 * WINDOW)[None, :]
    dist = qpos - kpos
    valid = (dist >= 0) & (dist < WINDOW)
    abs_k = jnp.arange(nb)[:, None, None] * WINDOW + kpos[None] - WINDOW
    valid = valid[None] & (abs_k >= 0)
    sl = slopes.reshape(SWA_KV_HEADS, SWA_GROUP)[:, :, None, None]
    scores = scores - sl * dist.astype(jnp.float32)[None, None]
    scores = jnp.where(valid[None, :, None, None], scores, -jnp.inf)
    sink = jnp.broadcast_to(
        sinks.astype(jnp.float32).reshape(1, 1, SWA_KV_HEADS, SWA_GROUP, 1, 1),
        scores.shape[:-1] + (1,))
    p = jax.nn.softmax(jnp.concatenate([scores, sink], axis=-1), axis=-1)[..., :-1]
    out = jnp.einsum("bnhgqk,bnkhd->bnqhgd", p.astype(v.dtype), vv)
    return out.reshape(B, S, SWA_Q_HEADS * HEAD_DIM)


def diff_attention(q, k, v, lam, slopes, subln_g, lambda_init):
    B, S = q.shape[0], q.shape[1]
    nb = S // Q_BLOCK
    qb = q.reshape(B, nb, Q_BLOCK, DIFF_HEADS, 2, HEAD_DIM).transpose(1, 0, 2, 3, 4, 5)
    kpos = jnp.arange(S)

    def block(args):
        q_blk, i = args
        s = jnp.einsum("bqhmd,bkhmd->bhmqk", q_blk, k).astype(jnp.float32)
        s = s * (HEAD_DIM ** -0.5)
        dist = (i * Q_BLOCK + jnp.arange(Q_BLOCK))[:, None] - kpos[None, :]
        s = s - slopes[None, :, None, None, None] * dist.astype(jnp.float32)
        s = jnp.where(dist >= 0, s, -jnp.inf)
        p = jax.nn.softmax(s, axis=-1)
        p_diff = p[:, :, 0] - lam * p[:, :, 1]
        return jnp.einsum("bhqk,bkhe->bqhe", p_diff.astype(v.dtype), v)

    out = lax.map(block, (qb, jnp.arange(nb)))
    out = out.transpose(1, 0, 2, 3, 4).reshape(B, S, DIFF_HEADS, DIFF_V_DIM)
    out = rms_norm(out, subln_g) * (1.0 - lambda_init)
    return out.reshape(B, S, DIFF_HEADS * DIFF_V_DIM)


def conv_ffn(h, w_up, conv_w, conv_b, w_down):
    u = jnp.einsum("bsd,df->bsf", h, w_up)
    up = jnp.pad(u, ((0, 0), (CONV_WIDTH - 1, 0), (0, 0)))
    S = u.shape[1]
    c = (conv_w[0] * up[:, 0:S] + conv_w[1] * up[:, 1:S + 1]
         + conv_w[2] * up[:, 2:S + 2] + conv_b)
    g, val = jnp.split(c, 2, axis=-1)
    return jnp.einsum("bsf,fd->bsd", jax.nn.gelu(g, approximate=True) * val, w_down)


def setup_inputs(seed: int = 0) -> dict:
    key = jax.random.key(seed)
    ks = jax.random.split(key, 18)
    f32 = jnp.float32

    def nrm(k, shape, scale):
        return jax.random.normal(k, shape, f32) * scale

    def gain(k, shape):
        return 1.0 + 0.05 * jax.random.normal(k, shape, f32)

    return {
        "x": nrm(ks[0], (BATCH, SEQ, D_MODEL), 1.0),
        "attn_pre_g": gain(ks[1], (DEPTH, D_MODEL)),
        "w_in": nrm(ks[2], (DEPTH, D_MODEL, IN_COLS), D_MODEL ** -0.5),
        "swa_sinks": nrm(ks[3], (DEPTH, SWA_Q_HEADS), 0.5),
        "swa_out_g": gain(ks[4], (DEPTH, SWA_WIDTH)),
        "diff_lq1": nrm(ks[5], (DEPTH, HEAD_DIM), 0.1),
        "diff_lk1": nrm(ks[6], (DEPTH, HEAD_DIM), 0.1),
        "diff_lq2": nrm(ks[7], (DEPTH, HEAD_DIM), 0.1),
        "diff_lk2": nrm(ks[8], (DEPTH, HEAD_DIM), 0.1),
        "diff_subln_g": gain(ks[9], (DEPTH, DIFF_V_DIM)),
        "w_out": nrm(ks[10], (DEPTH, MIX_WIDTH, D_MODEL), MIX_WIDTH ** -0.5),
        "attn_post_g": gain(ks[11], (DEPTH, D_MODEL)),
        "ffn_pre_g": gain(ks[12], (DEPTH, D_MODEL)),
        "w_up": nrm(ks[13], (DEPTH, D_MODEL, 2 * D_FF), D_MODEL ** -0.5),
        "conv_w": nrm(ks[14], (DEPTH, CONV_WIDTH, 2 * D_FF), CONV_WIDTH ** -0.5),
        "conv_b": nrm(ks[15], (DEPTH, 2 * D_FF), 0.02),
        "w_down": nrm(ks[16], (DEPTH, D_FF, D_MODEL), D_FF ** -0.5),
        "ffn_post_g": gain(ks[17], (DEPTH, D_MODEL)),
    }


def reference(x, attn_pre_g, w_in, swa_sinks, swa_out_g, diff_lq1, diff_lk1, diff_lq2,
              diff_lk2, diff_subln_g, w_out, attn_post_g, ffn_pre_g, w_up, conv_w, conv_b,
              w_down, ffn_post_g):
    B, S = x.shape[0], x.shape[1]
    slopes = jnp.asarray(alibi_slopes(SWA_Q_HEADS + DIFF_HEADS))
    swa_slopes = slopes[:SWA_Q_HEADS]
    diff_slopes = slopes[SWA_Q_HEADS:]
    for layer in range(DEPTH):
        lambda_init = 0.8 - 0.6 * math.exp(-0.3 * layer)
        h = rms_norm(x, attn_pre_g[layer])
        proj = jnp.einsum("bsd,de->bse", h, w_in[layer])
        q_a, k_a, v_a, q_b, k_b, v_b = jnp.split(proj, IN_SPLITS, axis=-1)
        y_a = swa_sink_attention(
            q_a.reshape(B, S, SWA_Q_HEADS, HEAD_DIM),
            k_a.reshape(B, S, SWA_KV_HEADS, HEAD_DIM),
            v_a.reshape(B, S, SWA_KV_HEADS, HEAD_DIM),
            swa_sinks[layer], swa_slopes)
        y_a = rms_norm(y_a, swa_out_g[layer])
        lam = (jnp.exp(jnp.sum(diff_lq1[layer].astype(jnp.float32) * diff_lk1[layer].astype(jnp.float32)))
               - jnp.exp(jnp.sum(diff_lq2[layer].astype(jnp.float32) * diff_lk2[layer].astype(jnp.float32)))
               + lambda_init)
        y_b = diff_attention(
            q_b.reshape(B, S, DIFF_HEADS, 2, HEAD_DIM),
            k_b.reshape(B, S, DIFF_HEADS, 2, HEAD_DIM),
            v_b.reshape(B, S, DIFF_HEADS, DIFF_V_DIM),
            lam, diff_slopes, diff_subln_g[layer], lambda_init)
        mix = jnp.concatenate([y_a, y_b], axis=-1)
        x = x + rms_norm(jnp.einsum("bse,ed->bsd", mix, w_out[layer]), attn_post_g[layer])
        h2 = rms_norm(x, ffn_pre_g[layer])
        f = conv_ffn(h2, w_up[layer], conv_w[layer], conv_b[layer], w_down[layer])
        x = x + rms_norm(f, ffn_post_g[layer])
    return x
```

```python
import math
import numpy as np
import ml_dtypes
import concourse.bass as bass
import concourse.mybir as mybir
from concourse.bass_utils import run_bass_kernel_spmd

F32 = mybir.dt.float32
BF16 = mybir.dt.bfloat16
AF = mybir.ActivationFunctionType
ALU = mybir.AluOpType
AX = mybir.AxisListType

S = 8192
D = 1024
NEG = -30000.0
EPS = 1e-6
LAMBDA_INIT = 0.8 - 0.6 * math.exp(0.0)
NWSEL = 1216


class Buf:
    __slots__ = ("name", "w", "r")

    def __init__(self, name):
        self.name = name
        self.w = []
        self.r = []


class Chan:
    __slots__ = ("name", "sem", "count", "unit")

    def __init__(self, name, unit=16):
        self.name = name
        self.sem = None
        self.count = 0
        self.unit = unit


class Op:
    __slots__ = ("eng", "fn", "deps", "signal", "signo", "chan", "chval", "ndma", "pos", "sigto")

    def __init__(self, eng, fn):
        self.eng = eng
        self.fn = fn
        self.deps = []
        self.signal = False
        self.signo = 0
        self.chan = None
        self.chval = 0
        self.ndma = 0
        self.pos = 0
        self.sigto = {}


COMPUTE = ("pe", "act", "dve", "pool")


class Sched:
    def __init__(self):
        self.ops = {e: [] for e in ("pe", "act", "dve", "pool", "sp")}
        self.chans = []
        self.all_ops = []

    def chan(self, name, unit=16):
        c = Chan(name, unit)
        self.chans.append(c)
        return c

    def _add(self, op, reads, writes, pwrites):
        deps = []
        for b in reads:
            for w in b.w:
                deps.append((w, "raw"))
            if b.name.startswith("ps") or b.name.startswith("bank"):
                for r in b.r:
                    if r.eng != op.eng:
                        deps.append((r, "raw"))
        for b in writes:
            for w in b.w:
                deps.append((w, "waw"))
            for r in b.r:
                deps.append((r, "war"))
        for b in pwrites:
            for r in b.r:
                deps.append((r, "war"))
        best = {}
        for d, kind in deps:
            if d is op:
                continue
            if d.chan is None and d.eng == op.eng and op.chan is None:
                if op.eng == "pe":
                    continue
            if d.chan is not None:
                key = ("c", id(d.chan))
                if key not in best or best[key].chval < d.chval:
                    best[key] = d
            else:
                key = ("e", d.eng)
                if key not in best or best[key].pos < d.pos:
                    best[key] = d
        for d in best.values():
            op.deps.append(d)
            if d.chan is None:
                d.signal = True
                d.sigto[op.eng] = 0
        op.pos = len(self.ops[op.eng])
        for b in reads:
            b.r.append(op)
        for b in writes:
            b.w = [op]
            b.r = []
        for b in pwrites:
            b.w.append(op)
        self.ops[op.eng].append(op)
        self.all_ops.append(op)
        return op

    def op(self, eng, fn, reads=(), writes=(), pwrites=()):
        return self._add(Op(eng, fn), reads, writes, pwrites)

    def dma(self, eng, chan, fns, reads=(), writes=(), pwrites=()):
        if not isinstance(fns, (list, tuple)):
            fns = [fns]
        o = Op(eng, fns)
        o.chan = chan
        o.ndma = len(fns)
        chan.count += chan.unit * len(fns)
        o.chval = chan.count
        return self._add(o, reads, writes, pwrites)

    def barrier(self, bufs=()):
        lasts = []
        for e in COMPUTE:
            if self.ops[e]:
                cands = [o for o in self.ops[e] if o.chan is None and o.fn is not None]
                if cands:
                    lasts.append(cands[-1])
        chan_last = {}
        for o in self.all_ops:
            if o.chan is not None:
                chan_last[id(o.chan)] = o
        for e in ("pe", "act", "dve", "pool", "sp"):
            o = Op(e, None)
            for d in lasts:
                if d.eng != e:
                    o.deps.append(d)
                    d.signal = True
                    d.sigto[e] = 0
            for d in chan_last.values():
                o.deps.append(d)
            o.pos = len(self.ops[e])
            self.ops[e].append(o)
            self.all_ops.append(o)
        for b in bufs:
            b.w = []
            b.r = []

    def finalize(self):
        for e in COMPUTE:
            n = 0
            for o in self.ops[e]:
                if o.chan is None and o.signal:
                    n += 1
                    o.signo = n

    def emit(self, nc, block, sems):
        handles = {"pe": "tensor", "act": "scalar", "dve": "vector", "pool": "gpsimd", "sp": "sync"}
        sch = self

        def run(engname):
            def body(eng):
                seen = {}
                for o in sch.ops[engname]:
                    for d in o.deps:
                        if d.chan is not None:
                            key, sem, val = ("c", id(d.chan)), d.chan.sem, d.chval
                        else:
                            key, sem, val = ("e", d.eng), sems[d.eng], d.signo
                        if seen.get(key, 0) >= val:
                            continue
                        seen[key] = val
                        eng.wait_ge(sem, val)
                    if o.fn is None:
                        continue
                    if o.chan is not None:
                        for f in o.fn:
                            ins = f(eng)
                            if o.chan.unit == 16:
                                ins.then_inc(o.chan.sem, 16)
                            else:
                                ins.then_inc(o.chan.sem)
                    else:
                        ins = o.fn(eng)
                        if o.signal:
                            ins.then_inc(sems[engname], 1)
            return body

        block.tensor(run("pe"))
        block.scalar(run("act"))
        block.vector(run("dve"))
        block.gpsimd(run("pool"))
        block.sync(run("sp"))


class Arena:
    def __init__(self, ap_bf16, nbytes):
        self.ap = ap_bf16
        self.nbytes = nbytes
        self.off = 0

    def set(self, off):
        self.off = off

    def alloc(self, nbytes, dtype=BF16, shape=None):
        nbytes = (nbytes + 63) // 64 * 64
        assert self.off + nbytes <= self.nbytes, ("arena overflow", self.off, nbytes, self.nbytes)
        a = self.ap[:, self.off // 2:(self.off + nbytes) // 2]
        self.off += nbytes
        if dtype == F32:
            a = a.bitcast(F32)
        return a

    def f32(self, *shape):
        n = int(np.prod(shape))
        a = self.alloc(n * 4, F32)[:, 0:n]
        return _shape(a, shape)

    def bf(self, *shape):
        n = int(np.prod(shape))
        a = self.alloc(n * 2, BF16)[:, 0:n]
        return _shape(a, shape)


def _shape(a, shape):
    if len(shape) == 1:
        return a
    if len(shape) == 2:
        return a.rearrange("p (a b) -> p a b", a=shape[0])
    if len(shape) == 3:
        return a.rearrange("p (a b c) -> p a b c", a=shape[0], b=shape[1])
    raise ValueError(shape)


ARENA_BYTES = 207 * 1024
CA_N = 2576
CB_N = 3856


def build_program(debug=False, phases=3):
    nc = bass.Bass("TRN2", target_bir_lowering=False)
    sc = Sched()
    import os
    SKIP = set(os.environ.get("KSKIP", "").split(","))

    def dram_in(name, shape, dt=F32):
        return nc.dram_tensor(name, list(shape), dt, kind="ExternalInput").ap()

    x_seq = dram_in("x_seq", [S, D])
    x_own = dram_in("x_own", [4096, D])
    x_halo = dram_in("x_halo", [128, D])
    wsel_d = dram_in("wsel", [D, NWSEL])
    wout_d = dram_in("wout", [D, D])
    wup_d = dram_in("wup", [4096, 2048])
    wdn_d = dram_in("wdn", [2048, 2048])
    ca_d = dram_in("ca", [128, CA_N])
    cb_d = dram_in("cb", [128, CB_N])
    ch_d = dram_in("ch", [128, 512], BF16)
    out_d = nc.dram_tensor("out", [4096, D], F32, kind="ExternalOutput").ap()
    wup_bf = nc.dram_tensor("wup_bf", [4096, 2048], BF16)
    wdn_bf = nc.dram_tensor("wdn_bf", [2048, 2048], BF16)
    NCH = 8
    xin_t = [nc.dram_tensor("xch_in%d" % k, [1024, 512], BF16) for k in range(NCH)]
    xout_t = [nc.dram_tensor("xch_out%d" % k, [2048, 512], BF16) for k in range(NCH)]
    xin_c = [t.ap() for t in xin_t]
    xout_c = [t.ap().rearrange("(s t) c -> s t c", s=2) for t in xout_t]
    dbg = None
    if debug:
        dbg = nc.dram_tensor("dbg", [S, 512], BF16, kind="ExternalOutput").ap()

    ctx_arena = nc.sbuf_tensor("arena", [128, ARENA_BYTES // 2], BF16)
    ctx_ps = nc.psum_tensor("ps", [128, 4096], F32)
    arena_t = ctx_arena.__enter__()
    ps = ctx_ps.__enter__()
    A = Arena(arena_t[:, :], ARENA_BYTES)

    def rsqrt_ops(out, in_, inv_n, b_in, b_out, post_scale=1.0):
        sc.op("act", lambda e: e.activation(out=out, in_=in_, func=AF.Ln, bias=EPS, scale=inv_n),
              reads=[b_in], writes=[b_out])
        sc.op("act", lambda e: e.activation(out=out, in_=out, func=AF.Exp, bias=math.log(post_scale), scale=-0.5),
              reads=[b_out], writes=[b_out])

    def bank(b, n=1):
        return ps[:, 512 * b:512 * (b + n)]

    cH = A.bf(512)
    ident = cH[:, 0:128]
    maskT = cH[:, 128:256]
    sel0 = cH[:, 256:384]
    sel1 = cH[:, 384:512]
    uhalo = A.f32(64, 2)
    lam = A.f32(8)
    b_cH = Buf("cH")
    b_uhalo = Buf("uhalo")
    ch_c = sc.chan("consts")
    sc.dma("sp", ch_c, lambda e: e.dma_start(out=cH, in_=ch_d[:, :]), writes=[b_cH])

    ch_wconv = sc.chan("wconv")
    b_wupbf = Buf("wup_bf")
    b_wdnbf = Buf("wdn_bf")
    PACE = A.f32(2)
    b_pace = Buf("pace")
    conv_list = []
    for k in range(32):
        conv_list.append((lambda e, k=k: e.dma_start(out=wup_bf[128 * k:128 * (k + 1), :], in_=wup_d[128 * k:128 * (k + 1), :]),
                          b_wupbf))
    for k in range(16):
        conv_list.append((lambda e, k=k: e.dma_start(out=wdn_bf[128 * k:128 * (k + 1), :], in_=wdn_d[128 * k:128 * (k + 1), :]),
                          b_wdnbf))
    conv_pos = {"i": 0}

    def emit_conv(n, pace_buf=None):
        if conv_pos["i"] >= len(conv_list):
            return
        if pace_buf is not None:
            sc.op("pool", lambda e: e.memset(PACE, 0.0), reads=[pace_buf], writes=[b_pace])
        for _ in range(n):
            if conv_pos["i"] >= len(conv_list):
                return
            fn, buf = conv_list[conv_pos["i"]]
            conv_pos["i"] += 1
            sc.dma("pool", ch_wconv, fn, pwrites=[buf])

    G_END = A.off
    KbT = A.bf(2, S)
    QbT = A.bf(2, S)
    Vb = A.bf(64, 2, 130)
    cA = A.f32(CA_N)
    P12_END = A.off
    dbias = cA[:, 0:128].rearrange("p (h r) -> p h r", h=2)
    swab = cA[:, 128:1152].rearrange("p (k n) -> p k n", k=2)
    gpre = cA[:, 1152:2176]
    subg = cA[:, 2176:2304]
    sinks = cA[:, 2304:2308]
    lqk = cA[:, 2320:2576].rearrange("p (a n) -> p a n", a=4)
    b_cA = Buf("cA")
    ch_ca = sc.chan("ca")
    sc.dma("sp", ch_ca, lambda e: e.dma_start(out=cA, in_=ca_d[:, :]), writes=[b_cA])

    Wsel = A.bf(8, NWSEL)
    b_wsel = Buf("wsel")
    ch_ws = sc.chan("wsel")
    P1_BASE = A.off
    XT = [A.f32(4, 1024) for _ in range(2)]
    b_xt = [Buf("xt0"), Buf("xt1")]
    ch_xt = [sc.chan("xt0"), sc.chan("xt1")]
    HB = [A.bf(1024) for _ in range(2)]
    b_hb = [Buf("hb0"), Buf("hb1")]
    HT = [A.bf(8, 512) for _ in range(2)]
    b_ht = [Buf("ht0"), Buf("ht1")]
    QAT = [A.bf(2, 512) for _ in range(2)]
    b_qat = [Buf("qat0"), Buf("qat1")]
    KAT = [A.bf(512) for _ in range(2)]
    b_kat = [Buf("kat0"), Buf("kat1")]
    VA = [A.bf(4, 66) for _ in range(2)]
    b_va = [Buf("va0"), Buf("va1")]
    STMP = [A.f32(512) for _ in range(2)]
    b_stmp = [Buf("stmp0"), Buf("stmp1")]
    PTS = [[A.bf(512) for _ in range(2)] for _ in range(2)]
    b_pts = [[Buf("pts%d%d" % (a, b)) for b in range(2)] for a in range(2)]
    YAST = [A.bf(4, 256) for _ in range(2)]
    b_yast = [Buf("yast0"), Buf("yast1")]
    ch_yast = [sc.chan("yast0"), sc.chan("yast1")]
    SQJ = A.bf(1024)
    b_sqj = Buf("sqj")
    STAT = A.f32(64)
    ss1 = [STAT[:, 0:4], STAT[:, 4:8]]
    rstd1 = [STAT[:, 8:12], STAT[:, 12:16]]
    sinkexp = STAT[:, 16:20]
    den = [STAT[:, 20:24], STAT[:, 24:28]]
    rec = [STAT[:, 28:32], STAT[:, 32:36]]
    lsum = STAT[:, 40:44]
    b_ss1 = [Buf("ss1a"), Buf("ss1b")]
    b_rstd1 = [Buf("rstd1a"), Buf("rstd1b")]
    b_sinkexp = Buf("sinkexp")
    b_den = [Buf("den0"), Buf("den1")]
    b_rec = [Buf("rec0"), Buf("rec1")]
    b_lam = Buf("lam")
    b_lsum = Buf("lsum")
    LJ = A.f32(4, 64)
    P1_END = A.off

    b_xchin = [Buf("xch_in%d" % k) for k in range(8)]
    b_xchout = [Buf("xch_out%d" % k) for k in range(8)]
    b_kqv = Buf("kqv")
    if "memset4d" not in SKIP:
        sc.op("pool", lambda e: e.memset(Vb[:, :, :, 128:130], 1.0), pwrites=[b_kqv])
    sc.op("pool", lambda e: e.memset(VA[0][:, :, 64:66], 1.0), writes=[b_va[0]])
    sc.op("pool", lambda e: e.memset(VA[1][:, :, 64:66], 1.0), writes=[b_va[1]])
    if "wsel" not in SKIP:
        sc.dma("pool", ch_ws,
               [lambda e, c=c: e.dma_start(out=Wsel[:, c, :], in_=wsel_d[c * 128:(c + 1) * 128, :]) for c in range(8)],
               writes=[b_wsel])

    sc.op("act", lambda e: e.activation(out=sinkexp, in_=sinks, func=AF.Exp), reads=[b_cA], writes=[b_sinkexp])
    b_lj = Buf("lj")
    if "lam" not in SKIP:
      sc.op("dve", lambda e: e.tensor_tensor(out=LJ[:, 0:2, :], in0=lqk[:, 0:4:2, :], in1=lqk[:, 1:4:2, :], op=ALU.mult),
          reads=[b_cA], writes=[b_lj])
    sc.op("dve", lambda e: e.tensor_reduce(out=lsum[:, 0:2], in_=LJ[:, 0:2, :], axis=AX.X, op=ALU.add),
          reads=[b_lj], writes=[b_lsum])
    b_lexp = Buf("lexp")
    sc.op("act", lambda e: e.activation(out=lsum[:, 2:4], in_=lsum[:, 0:2], func=AF.Exp), reads=[b_lsum], writes=[b_lexp])
    sc.op("dve", lambda e: e.scalar_tensor_tensor(out=lam[:, 0:1], in0=lsum[:, 2:3], scalar=LAMBDA_INIT, in1=lsum[:, 3:4],
                                                   op0=ALU.add, op1=ALU.subtract),
          reads=[b_lexp], writes=[b_lam])

    psT = [bank(0).bitcast(BF16), bank(1).bitcast(BF16)]
    b_psT = [Buf("psT0"), Buf("psT1")]
    psF = [bank(2), bank(3)]
    b_psF = [Buf("psF0"), Buf("psF1")]
    psV = bank(4)
    b_psV = Buf("psV")
    psO = bank(5)
    b_psO = Buf("psO")
    psS = [bank(6), bank(7)]
    b_psS = [Buf("psS0"), Buf("psS1")]

    NT = S // 512

    def load_x(j):
        s = j % 2
        src = x_seq[512 * j:512 * (j + 1), :].rearrange("(i p) d -> p i d", p=128)
        sc.dma("sp", ch_xt[s], lambda e: e.dma_start(out=XT[s], in_=src), writes=[b_xt[s]])

    def stage_ss(j):
        s = j % 2
        for i in range(4):
            sc.op("act", lambda e, i=i: e.activation(out=SQJ, in_=XT[s][:, i, :], func=AF.Square,
                                                      accum_out=ss1[s][:, i:i + 1]),
                  reads=[b_xt[s]], writes=[b_sqj, b_ss1[s]] if i == 0 else [b_sqj], pwrites=[] if i == 0 else [b_ss1[s]])
        rsqrt_ops(rstd1[s], ss1[s], 1.0 / D, b_ss1[s], b_rstd1[s])

    cnt = {"hb": 0, "psT": 0, "psF": 0, "evac": 0}

    def evac(out, in_, reads, writes=(), pwrites=()):
        cnt["evac"] += 1
        use_act = False
        if "evacdve" in SKIP:
            use_act = False
        if "evacact" in SKIP:
            use_act = True
        if use_act:
            return sc.op("act", lambda e: e.copy(out=out, in_=in_), reads=reads, writes=writes, pwrites=pwrites)
        return sc.op("dve", lambda e: e.tensor_copy(out=out, in_=in_), reads=reads, writes=writes, pwrites=pwrites)

    nstate = {}

    def norm_a(j, i):
        s = j % 2
        hs = cnt["hb"] % 2
        cnt["hb"] += 1
        nstate[(j, i)] = hs
        sc.op("dve", lambda e: e.scalar_tensor_tensor(out=HB[hs], in0=XT[s][:, i, :], scalar=rstd1[s][:, i:i + 1],
                                                       in1=gpre, op0=ALU.mult, op1=ALU.mult),
              reads=[b_xt[s], b_rstd1[s], b_cA], writes=[b_hb[hs]])

    def norm_b(j, i):
        s = j % 2
        hs = nstate[(j, i)]
        tsl = cnt["psT"] % 2
        cnt["psT"] += 1
        for c in range(8):
            sc.op("pe", lambda e, c=c: e.transpose(out=psT[tsl][:, c * 128:(c + 1) * 128],
                                                   in_=HB[hs][:, c * 128:(c + 1) * 128], identity=ident),
                  reads=[b_hb[hs], b_cH], writes=[b_psT[tsl]] if c == 0 else (), pwrites=() if c == 0 else [b_psT[tsl]])
        evac(HT[s][:, :, i * 128:(i + 1) * 128], psT[tsl].rearrange("p (c t) -> p c t", c=8),
             reads=[b_psT[tsl]], writes=[b_ht[s]] if i == 0 else (), pwrites=() if i == 0 else [b_ht[s]])

    def stage_norm_block(j, i):
        norm_a(j, i)
        norm_b(j, i)

    def proj_fm(j, m):
        s = j % 2
        fs = cnt["psF"] % 2
        cnt["psF"] += 1
        for d in range(8):
            sc.op("pe", lambda e, d=d: e.matmul(psF[fs], lhsT=Wsel[:, d, m * 128:(m + 1) * 128], rhs=HT[s][:, d, :],
                                                start=(d == 0), stop=(d == 7)),
                  reads=[b_wsel, b_ht[s]], writes=[b_psF[fs]] if d == 0 else (), pwrites=() if d == 0 else [b_psF[fs]])
        cols = slice(512 * j, 512 * (j + 1))
        if m < 2:
            evac(QbT[:, m, cols], psF[fs], reads=[b_psF[fs]], pwrites=[b_kqv])
        elif m < 4:
            evac(KbT[:, m - 2, cols], psF[fs], reads=[b_psF[fs]], pwrites=[b_kqv])
        elif m < 6:
            evac(QAT[s][:, m - 4, :], psF[fs], reads=[b_psF[fs]],
                 writes=[b_qat[s]] if m == 4 else (), pwrites=() if m == 4 else [b_qat[s]])
        else:
            evac(KAT[s], psF[fs], reads=[b_psF[fs]], writes=[b_kat[s]])

    def proj_tm(j, i):
        s = j % 2
        for d in range(8):
            sc.op("pe", lambda e, d=d: e.matmul(psV[:, 0:320], lhsT=HT[s][:, d, i * 128:(i + 1) * 128],
                                                rhs=Wsel[:, d, 896:1216], start=(d == 0), stop=(d == 7)),
                  reads=[b_wsel, b_ht[s]], writes=[b_psV] if d == 0 else (), pwrites=() if d == 0 else [b_psV])
        sc.op("act", lambda e: e.copy(out=Vb[:, 4 * j + i, :, 0:128], in_=psV[:, 0:256].rearrange("p (h n) -> p h n", h=2)),
              reads=[b_psV], pwrites=[b_kqv])
        sc.op("act", lambda e: e.copy(out=VA[s][:, i, 0:64], in_=psV[:, 256:320]),
              reads=[b_psV], pwrites=[b_va[s]])

    swa_cnt = {"n": 0}

    sstate = {}

    def swa_a(j, i):
        s = j % 2
        n = 4 * j + i
        blocks = []
        if n > 0:
            blocks.append((0, (j if i > 0 else j - 1), (i - 1) % 4))
        blocks.append((1, j, i))
        pb = swa_cnt["n"] % 2
        swa_cnt["n"] += 1
        c0 = 0 if len(blocks) == 2 else 256
        firstw = [True, True]
        for (kbi, jj, ii) in blocks:
            ss_ = jj % 2
            for g in range(4):
                r0 = 64 * (g % 2)
                par = g % 2
                col = (kbi * 2 + g // 2) * 128
                sc.op("pe", lambda e, g=g, r0=r0, ss_=ss_, ii=ii, par=par, col=col: e.matmul(
                    psS[par][:, col:col + 128],
                    lhsT=KAT[ss_][r0:r0 + 64, ii * 128:(ii + 1) * 128],
                    rhs=QAT[s][r0:r0 + 64, g // 2, i * 128:(i + 1) * 128],
                    start=True, stop=True, tile_position=(r0, 0)),
                    reads=[b_kat[ss_], b_qat[s]], writes=[b_psS[par]] if firstw[par] else (),
                    pwrites=() if firstw[par] else [b_psS[par]])
                firstw[par] = False
        for par in range(2):
            sc.op("dve", lambda e, par=par: e.scalar_tensor_tensor(out=STMP[par][:, c0:512], in0=psS[par][:, c0:512], scalar=0.125,
                                                                    in1=swab[:, par, c0:512], op0=ALU.mult, op1=ALU.add),
                  reads=[b_psS[par], b_cA], writes=[b_stmp[par]])
            sc.op("act", lambda e, par=par: e.activation(out=PTS[par][pb][:, c0:512], in_=STMP[par][:, c0:512], func=AF.Exp),
                  reads=[b_stmp[par]], writes=[b_pts[par][pb]])
        sstate[(j, i)] = (blocks, pb)

    def swa_b(j, i):
        s = j % 2
        n = 4 * j + i
        blocks, pb = sstate[(j, i)]
        psO_v = psO[:, 0:264].rearrange("p (g n) -> p g n", g=4)
        first = True
        for g in range(4):
            for bi, (kbi, jj, ii) in enumerate(blocks):
                ss_ = jj % 2
                par = g % 2
                col = (kbi * 2 + g // 2) * 128
                sc.op("pe", lambda e, g=g, ss_=ss_, ii=ii, bi=bi, par=par, col=col: e.matmul(
                    psO_v[:, g, 0:65], lhsT=PTS[par][pb][:, col:col + 128], rhs=VA[ss_][:, ii, 0:65],
                    start=(bi == 0), stop=(bi == len(blocks) - 1)),
                    reads=[b_pts[par][pb], b_va[ss_]], writes=[b_psO] if first else (),
                    pwrites=() if first else [b_psO])
                first = False
        ds = n % 2
        sc.op("dve", lambda e: e.tensor_tensor(out=den[ds], in0=psO_v[:, :, 64], in1=sinkexp, op=ALU.add),
              reads=[b_psO, b_sinkexp], writes=[b_den[ds]])
        sc.op("dve", lambda e: e.reciprocal(out=rec[ds], in_=den[ds]), reads=[b_den[ds]], writes=[b_rec[ds]])
        sc.op("dve", lambda e: e.tensor_tensor(out=YAST[s][:, i, :].rearrange("p (g n) -> p g n", g=4),
                                               in0=psO_v[:, :, 0:64],
                                               in1=rec[ds].unsqueeze(2).to_broadcast([128, 4, 64]), op=ALU.mult),
              reads=[b_psO, b_rec[ds]], writes=[b_yast[s]] if i == 0 else (), pwrites=() if i == 0 else [b_yast[s]])

    def store_ya(j):
        s = j % 2
        r_ = 512 * (j % 2)
        dst = xin_c[j // 2][r_:r_ + 512, 0:256].rearrange("(i p) c -> p i c", p=128)
        sc.dma("sp", ch_yast[s], lambda e: e.dma_start(out=dst, in_=YAST[s]), reads=[b_yast[s]], pwrites=[b_xchin[j // 2]])

    NT_RUN = int(os.environ.get("KNT", NT))
    if "loadx" not in SKIP:
        load_x(0)
        load_x(1)
    if "ss" not in SKIP:
        stage_ss(0)
    if "norm" not in SKIP:
        for i in range(4):
            stage_norm_block(0, i)
    for j in range(NT_RUN):
        nxt = j + 1 < NT
        if nxt:
            stage_ss(j + 1)
        items = [("Na", 0), ("fm", 4), ("fm", 5), ("fm", 6), ("Nb", 0), ("Na", 1), ("tm", 0), ("Sa", 0), ("tm", 1),
                 ("Nb", 1), ("Na", 2), ("fm", 0), ("Sb", 0), ("Sa", 1), ("tm", 2), ("Nb", 2), ("Na", 3), ("fm", 1),
                 ("Sb", 1), ("Sa", 2), ("tm", 3), ("Nb", 3), ("fm", 2), ("Sb", 2), ("Sa", 3), ("fm", 3), ("Sb", 3)]
        for kind, a in items:
            if kind == "fm":
                proj_fm(j, a)
            elif kind == "tm":
                proj_tm(j, a)
            elif kind == "Sa":
                swa_a(j, a)
            elif kind == "Sb":
                swa_b(j, a)
            elif kind == "Na" and nxt:
                norm_a(j + 1, a)
            elif kind == "Nb" and nxt:
                norm_b(j + 1, a)
        if "store" not in SKIP:
            store_ya(j)
        if j >= 2:
            emit_conv(1, pace_buf=b_yast[j % 2])
        if j + 2 < NT:
            load_x(j + 2)

    sc.barrier()
    A.set(P12_END)
    HI_BASE = 143 * 1024
    if phases >= 3:
        A.set(HI_BASE)
        cB = A.f32(CB_N)
        swag = cB[:, 0:512].rearrange("p (s n) -> p s n", s=2)
        gpost = cB[:, 512:1536]
        gffn = cB[:, 1536:2560]
        gpost2 = cB[:, 2560:3584]
        convw = cB[:, 3584:3776].rearrange("p (c k) -> p c k", k=3)
        convb = cB[:, 3776:3840]
        selsc = cB[:, 3840:3842]
        b_cB = Buf("cB")
        ch_cb = sc.chan("cb")
        sc.dma("sp", ch_cb, lambda e: e.dma_start(out=cB, in_=cb_d[:, :]), writes=[b_cB])
        Wout = A.bf(8, 1024)
        b_wout = Buf("wout")
        ch_wo = sc.chan("wout")
        sc.dma("pool", ch_wo,
               [lambda e, c=c: e.dma_start(out=Wout[:, c, :], in_=wout_d[c * 128:(c + 1) * 128, :]) for c in range(8)],
               writes=[b_wout])
        XO = [A.f32(4, 1024) for _ in range(2)]
        b_xo = [Buf("xo0"), Buf("xo1")]
        ch_xo = [sc.chan("xo0"), sc.chan("xo1")]
        def load_xo(t):
            s = t % 2
            if t < 0:
                src = x_halo[:, :]
                sc.dma("sp", ch_xo[s], lambda e: e.dma_start(out=XO[s][:, 0, :], in_=src), writes=[b_xo[s]])
            else:
                src = x_own[512 * t:512 * (t + 1), :].rearrange("(i p) d -> p i d", p=128)
                sc.dma("sp", ch_xo[s], lambda e: e.dma_start(out=XO[s], in_=src), writes=[b_xo[s]])

        load_xo(-1)
        load_xo(0)
        assert A.off <= ARENA_BYTES
        A.set(P12_END)
    PT = [A.bf(2, 512) for _ in range(3)]
    b_pt = [Buf("pt%d" % k) for k in range(3)]
    T1 = A.f32(4, 128)
    T2 = A.f32(4, 128)
    YY = A.f32(4, 128)
    SQ2 = A.f32(4, 128)
    b_t1, b_t2, b_yy, b_sq2 = Buf("t1"), Buf("t2"), Buf("yy"), Buf("sq2")
    YBST = [A.bf(4, 2, 128) for _ in range(2)]
    b_ybst = [Buf("ybst0"), Buf("ybst1")]
    ch_ybst = [sc.chan("ybst0"), sc.chan("ybst1")]
    ST2 = A.f32(32)
    recs = ST2[:, 0:8].rearrange("p (m q) -> p m q", m=2)
    ss2 = ST2[:, 8:12]
    rs2 = ST2[:, 12:16]
    b_recs, b_ss2, b_rs2 = Buf("recs"), Buf("ss2"), Buf("rs2")

    assert A.off <= HI_BASE, A.off
    psS2 = [ps[:, 0:1024].rearrange("p (m q) -> p m q", m=2), ps[:, 1024:2048].rearrange("p (m q) -> p m q", m=2)]
    b_psS2 = [Buf("psS2a"), Buf("psS2b")]
    psO2 = [ps[:, 2048:3072].rearrange("p (q n) -> p q n", q=4), ps[:, 3072:4096].rearrange("p (q n) -> p q n", q=4)]
    b_psO2 = Buf("psO2")

    units = []
    for p in range(NT):
        for hh in range(2):
            for kb in range(4 * p + 4):
                units.append((p, hh, kb))

    def rec_S(u):
        p, hh, kb = units[u]
        sb = u % 2
        jd = kb - 4 * p
        q0 = 128 * max(jd, 0)
        first = [True]

        def w():
            if first[0]:
                first[0] = False
                return dict(writes=[b_psS2[sb]])
            return dict(pwrites=[b_psS2[sb]])
        for m in range(2):
            r0 = 64 * m
            kk = KbT[r0:r0 + 64, hh, kb * 128:(kb + 1) * 128]
            sc.op("pe", lambda e, m=m, r0=r0, kk=kk: e.matmul(
                psS2[sb][:, m, q0:512], lhsT=kk, rhs=QbT[r0:r0 + 64, hh, 512 * p + q0:512 * (p + 1)],
                start=True, stop=True, tile_position=(r0, 0)), reads=[b_kqv], **w())

    def rec_E(u):
        p, hh, kb = units[u]
        sb = u % 2
        tb = u % 3
        jd = kb - 4 * p
        q0 = 128 * max(jd, 0)
        rel = 4 * p + 3 - kb
        sc.op("act", lambda e: e.activation(out=PT[tb][:, :, q0:512], in_=psS2[sb][:, :, q0:512], func=AF.Exp,
                                            bias=dbias[:, hh, rel:rel + 1], scale=0.125),
              reads=[b_psS2[sb], b_cA], writes=[b_pt[tb]])
        if jd >= 0:
            sc.op("dve", lambda e: e.tensor_tensor(out=PT[tb][:, :, q0:q0 + 128], in0=PT[tb][:, :, q0:q0 + 128],
                                                   in1=maskT.unsqueeze(1).to_broadcast([128, 2, 128]), op=ALU.mult),
                  reads=[b_pt[tb], b_cH], writes=[b_pt[tb]])

    def rec_PV(u):
        p, hh, kb = units[u]
        tb = u % 3
        jd = kb - 4 * p
        first = True
        for m in range(2):
            for qs in range(max(jd, 0), 4):
                sc.op("pe", lambda e, m=m, qs=qs: e.matmul(
                    psO2[m][:, qs, 0:129], lhsT=PT[tb][:, m, qs * 128:(qs + 1) * 128], rhs=Vb[:, kb, hh, 0:129],
                    start=(kb == 0 and qs % 2 == 0), stop=(kb == 4 * p + qs), skip_group_check=True),
                    reads=[b_pt[tb], b_kqv], writes=[b_psO2] if (first and kb == 0) else (),
                    pwrites=() if (first and kb == 0) else [b_psO2])
                first = False
        if kb == 4 * p + 3:
            epilogue(p, hh)

    def epilogue(p, hh):
        ys = p % 2
        for m in range(2):
            sc.op("dve", lambda e, m=m: e.reciprocal(out=recs[:, m, :], in_=psO2[m][:, :, 128]),
                  reads=[b_psO2], writes=[b_recs] if m == 0 else (), pwrites=() if m == 0 else [b_recs])
        sc.op("dve", lambda e: e.tensor_scalar(out=recs[:, 1, :], in0=recs[:, 1, :], scalar1=lam[:, 0:1], scalar2=None,
                                                op0=ALU.mult), reads=[b_recs, b_lam], writes=[b_recs])
        sc.op("dve", lambda e: e.tensor_tensor(out=T1, in0=psO2[0][:, :, 0:128],
                                               in1=recs[:, 0, :].unsqueeze(2).to_broadcast([128, 4, 128]), op=ALU.mult),
              reads=[b_psO2, b_recs], writes=[b_t1])
        sc.op("dve", lambda e: e.tensor_tensor(out=T2, in0=psO2[1][:, :, 0:128],
                                               in1=recs[:, 1, :].unsqueeze(2).to_broadcast([128, 4, 128]), op=ALU.mult),
              reads=[b_psO2, b_recs], writes=[b_t2])
        sc.op("dve", lambda e: e.tensor_tensor(out=YY, in0=T1, in1=T2, op=ALU.subtract), reads=[b_t1, b_t2], writes=[b_yy])
        sc.op("dve", lambda e: e.tensor_tensor(out=SQ2, in0=YY, in1=YY, op=ALU.mult), reads=[b_yy], writes=[b_sq2])
        sc.op("dve", lambda e: e.tensor_reduce(out=ss2, in_=SQ2, axis=AX.X, op=ALU.add), reads=[b_sq2], writes=[b_ss2])
        rsqrt_ops(rs2, ss2, 1.0 / 128, b_ss2, b_rs2, post_scale=1.0 - LAMBDA_INIT)
        sc.op("dve", lambda e: e.tensor_tensor(out=YY, in0=YY, in1=rs2.unsqueeze(2).to_broadcast([128, 4, 128]), op=ALU.mult),
              reads=[b_yy, b_rs2], writes=[b_yy])
        sc.op("dve", lambda e: e.tensor_tensor(out=YBST[ys][:, :, hh, :], in0=YY,
                                               in1=subg.unsqueeze(1).to_broadcast([128, 4, 128]), op=ALU.mult),
              reads=[b_yy, b_cA], writes=[b_ybst[ys]] if hh == 0 else (), pwrites=() if hh == 0 else [b_ybst[ys]])
        if hh == 1:
            r_ = 512 * (p % 2)
            dst = xin_c[p // 2][r_:r_ + 512, 256:512].rearrange("(i p) c -> p i c", p=128)
            sc.dma("sp", ch_ybst[ys], lambda e: e.dma_start(out=dst, in_=YBST[ys].rearrange("p q h n -> p q (h n)")),
                   reads=[b_ybst[ys]], pwrites=[b_xchin[p // 2]])
            if p >= 3:
                emit_conv(3, pace_buf=b_ybst[ys])
            if p % 2 == 1 and phases >= 3:
                k_ = p // 2
                sc.dma("pool", ch_cc[k_], lambda e: e.collective_compute(
                    "AllGather", ALU.bypass, replica_groups=[[0, 1], [2, 3], [4, 5], [6, 7]],
                    ins=[xin_t[k_].ap().opt()], outs=[xout_t[k_].ap().opt()]),
                    reads=[b_xchin[k_]], writes=[b_xchout[k_]])

    ch_cc = [sc.chan("cc%d" % k, unit=1) for k in range(8)]
    if phases >= 2:
        NU = len(units)
        for u in range(NU + 1):
            if u < NU:
                rec_S(u)
                rec_E(u)
            if u >= 1:
                rec_PV(u - 1)

    if debug:
        ch_dbg = sc.chan("dbg")
        sc.dma("sp", ch_dbg, [lambda e, k=k: e.dma_start(out=dbg[512 * k:512 * (k + 1), :],
                                                         in_=xin_c[k // 2][512 * (k % 2):512 * (k % 2) + 512, :])
                              for k in range(16)], reads=b_xchin)

    emit_conv(100)
    if phases >= 3:
        sc.barrier()
        A.set(G_END)
        MIXC = [A.bf(2, 2, 512) for _ in range(2)]
        b_mixc = [Buf("mixc0"), Buf("mixc1")]
        ch_mixc = [sc.chan("mixc0"), sc.chan("mixc1")]
        MIXN = [A.bf(2, 512) for _ in range(2)]
        b_mixn = [Buf("mixn0"), Buf("mixn1")]
        MIXT = A.bf(8, 512)
        b_mixt = Buf("mixt")
        TMPS = A.bf(2, 512)
        b_tmps = Buf("tmps")
        H2B = [A.bf(1024) for _ in range(2)]
        b_h2b = [Buf("h2b0"), Buf("h2b1")]
        H2T = A.bf(8, 512)
        b_h2t = Buf("h2t")
        H2TH = A.bf(8, 128)
        b_h2th = Buf("h2th")
        WUP = [A.bf(8, 2, 128) for _ in range(3)]
        b_wup = [Buf("wup%d" % k) for k in range(3)]
        ch_wup = [sc.chan("wup%d" % k) for k in range(3)]
        UB = [A.f32(516) for _ in range(4)]
        b_ub = [Buf("ub%d" % k) for k in range(4)]
        CBUF = [A.f32(512) for _ in range(4)]
        b_cbuf = [Buf("cbuf%d" % k) for k in range(4)]
        GG = [A.f32(512) for _ in range(2)]
        b_gg = [Buf("gg0"), Buf("gg1")]
        AT = A.bf(32, 512)
        b_at = [Buf("at%d" % g) for g in range(8)]
        NWD = 4
        WD = [A.bf(4, 512) for _ in range(NWD)]
        b_wd = [Buf("wd%d" % k) for k in range(NWD)]
        ch_wd = [sc.chan("wd%d" % k) for k in range(NWD)]
        FT = A.f32(4, 1024)
        b_ft = Buf("ft")
        ch_out = sc.chan("out")
        SQ3 = A.bf(1024)
        b_sq3 = Buf("sq3")
        TMP3 = A.f32(512)
        b_tmp3 = Buf("tmp3")
        ST3 = A.f32(64)
        ssa = ST3[:, 0:2]
        rsa = ST3[:, 2:4]
        ssw = ST3[:, 4:12].rearrange("p (i h) -> p i h", h=2)
        rsw = ST3[:, 12:16]
        ssh = ST3[:, 16:20]
        rsh = ST3[:, 20:24]
        ssf = ST3[:, 24:32].rearrange("p (i h) -> p i h", h=2)
        rsf = ST3[:, 32:36]
        b_ssa, b_rsa, b_ssw, b_rsw = Buf("ssa"), Buf("rsa"), Buf("ssw"), Buf("rsw")
        b_ssh, b_rsh, b_ssf, b_rsf = Buf("ssh"), Buf("rsh"), Buf("ssf"), Buf("rsf")

        assert A.off <= HI_BASE, A.off
        psT3 = bank(0).bitcast(BF16)
        b_psT3 = Buf("psT3")
        psM = ps[:, 512:1536]
        b_psM = Buf("psM")
        psU = [bank(1), bank(2), bank(3)]
        b_psU = [b_psM, Buf("psU1"), Buf("psU2")]
        b_psU[1] = b_psM
        b_psU = [Buf("bank1"), Buf("bank2"), Buf("bank3")]
        psD = [bank(4), bank(5), bank(6), bank(7)]
        b_psD = [Buf("psD%d" % k) for k in range(4)]
        sc.op("pool", lambda e: e.memset(uhalo, 0.0), writes=[b_uhalo])

        cnt3 = {"mixc": 0, "h2b": 0, "wup": 0, "psu": 0, "ub": 0, "cb": 0, "gg": 0, "wd": 0}

        mstate = {}

        def mix_A(t, i):
            ms = cnt3["mixc"] % 2
            cnt3["mixc"] += 1
            mstate[(t, i)] = {"ms": ms}
            fns = []
            if t < 0:
                fns.append(lambda e: e.dma_start(out=MIXC[ms][:, 1, :, :],
                                                 in_=xout_c[3][:, 896:1024, :].rearrange("s p c -> p s c")))
                rd = [b_xchout[3]]
            else:
                r0_ = 512 * t + 128 * i
                k0_, rr_ = r0_ // 1024, r0_ % 1024
                fns.append(lambda e: e.dma_start(out=MIXC[ms][:, 0, :, :],
                                                 in_=xout_c[k0_][:, rr_:rr_ + 128, :].rearrange("s p c -> p s c")))
                fns.append(lambda e: e.dma_start(out=MIXC[ms][:, 1, :, :],
                                                 in_=xout_c[4 + k0_][:, rr_:rr_ + 128, :].rearrange("s p c -> p s c")))
                rd = [b_xchout[k0_], b_xchout[4 + k0_]]
            sc.dma("sp", ch_mixc[ms], fns, reads=rd, writes=[b_mixc[ms]])
            if t < 0:
                sc.op("dve", lambda e: e.tensor_scalar(out=MIXN[ms], in0=MIXC[ms][:, 1, :, :], scalar1=selsc[:, 1:2],
                                                        scalar2=None, op0=ALU.mult),
                      reads=[b_mixc[ms], b_cB], writes=[b_mixn[ms]])
            else:
                sc.op("dve", lambda e: e.tensor_scalar(out=TMPS, in0=MIXC[ms][:, 1, :, :], scalar1=selsc[:, 1:2],
                                                        scalar2=None, op0=ALU.mult),
                      reads=[b_mixc[ms], b_cB], writes=[b_tmps])
                sc.op("dve", lambda e: e.scalar_tensor_tensor(out=MIXN[ms], in0=MIXC[ms][:, 0, :, :], scalar=selsc[:, 0:1],
                                                               in1=TMPS, op0=ALU.mult, op1=ALU.add),
                      reads=[b_mixc[ms], b_cB, b_tmps], writes=[b_mixn[ms]])
            sc.op("act", lambda e: e.activation(out=SQ3[:, 0:512].rearrange("p (s n) -> p s n", s=2),
                                                in_=MIXN[ms][:, :, 0:256], func=AF.Square, accum_out=ssa[:, ms:ms + 1]),
                  reads=[b_mixn[ms]], writes=[b_sq3, b_ssa])
            rsqrt_ops(rsa[:, ms:ms + 1], ssa[:, ms:ms + 1], 1.0 / 512, b_ssa, b_rsa)
            sc.op("dve", lambda e: e.scalar_tensor_tensor(out=MIXN[ms][:, :, 0:256], in0=MIXN[ms][:, :, 0:256],
                                                           scalar=rsa[:, ms:ms + 1], in1=swag, op0=ALU.mult, op1=ALU.mult),
                  reads=[b_mixn[ms], b_rsa, b_cB], writes=[b_mixn[ms]])

        def mix_B(t, i):
            ms = mstate[(t, i)]["ms"]
            mixn_flat = MIXN[ms].rearrange("p s n -> p (s n)")
            for c in range(8):
                sc.op("pe", lambda e, c=c: e.transpose(out=psT3[:, c * 128:(c + 1) * 128],
                                                       in_=mixn_flat[:, c * 128:(c + 1) * 128], identity=ident),
                      reads=[b_mixn[ms], b_cH], writes=[b_psT3] if c == 0 else (), pwrites=() if c == 0 else [b_psT3])
            evac(MIXT[:, :, i * 128:(i + 1) * 128], psT3.rearrange("p (c t) -> p c t", c=8),
                 reads=[b_psT3], writes=[b_mixt])

        def mix_C(t, i):
            s = t % 2
            for half in range(2):
                for c in range(8):
                    sc.op("pe", lambda e, c=c, half=half: e.matmul(
                        psM[:, 512 * half:512 * (half + 1)], lhsT=MIXT[:, c, i * 128:(i + 1) * 128],
                        rhs=Wout[:, c, 512 * half:512 * (half + 1)], start=(c == 0), stop=(c == 7)),
                        reads=[b_mixt, b_wout], writes=[b_psU[half]] if c == 0 else (),
                        pwrites=() if c == 0 else [b_psU[half]])
                sc.op("act", lambda e, half=half: e.activation(out=SQ3[:, 0:512], in_=psM[:, 512 * half:512 * (half + 1)],
                                                               func=AF.Square, accum_out=ssw[:, i, half:half + 1]),
                      reads=[b_psU[half]], writes=[b_sq3, b_ssw] if half == 0 else [b_sq3],
                      pwrites=[] if half == 0 else [b_ssw])
            sc.op("dve", lambda e: e.tensor_tensor(out=rsw[:, i:i + 1], in0=ssw[:, i, 0:1], in1=ssw[:, i, 1:2], op=ALU.add),
                  reads=[b_ssw], writes=[b_rsw])
            rsqrt_ops(rsw[:, i:i + 1], rsw[:, i:i + 1], 1.0 / D, b_rsw, b_rsw)
            for half in range(2):
                hsl = slice(512 * half, 512 * (half + 1))
                sc.op("dve", lambda e, hsl=hsl: e.scalar_tensor_tensor(out=TMP3, in0=psM[:, hsl], scalar=rsw[:, i:i + 1],
                                                                        in1=gpost[:, hsl], op0=ALU.mult, op1=ALU.mult),
                      reads=[b_psU[half], b_rsw, b_cB], writes=[b_tmp3])
                sc.op("dve", lambda e, hsl=hsl: e.tensor_tensor(out=XO[s][:, i, hsl], in0=TMP3, in1=XO[s][:, i, hsl], op=ALU.add),
                      reads=[b_tmp3, b_xo[s]], pwrites=[b_xo[s]])
            sc.op("act", lambda e: e.activation(out=SQ3, in_=XO[s][:, i, :], func=AF.Square, accum_out=ssh[:, i:i + 1]),
                  reads=[b_xo[s]], writes=[b_sq3, b_ssh])
            rsqrt_ops(rsh[:, i:i + 1], ssh[:, i:i + 1], 1.0 / D, b_ssh, b_rsh)
            hs = cnt3["h2b"] % 2
            cnt3["h2b"] += 1
            mstate[(t, i)]["hs"] = hs
            sc.op("dve", lambda e: e.scalar_tensor_tensor(out=H2B[hs], in0=XO[s][:, i, :], scalar=rsh[:, i:i + 1],
                                                           in1=gffn, op0=ALU.mult, op1=ALU.mult),
                  reads=[b_xo[s], b_rsh, b_cB], writes=[b_h2b[hs]])

        def mix_D(t, i):
            hs = mstate[(t, i)]["hs"]
            for c in range(8):
                sc.op("pe", lambda e, c=c: e.transpose(out=psT3[:, c * 128:(c + 1) * 128],
                                                       in_=H2B[hs][:, c * 128:(c + 1) * 128], identity=ident),
                      reads=[b_h2b[hs], b_cH], writes=[b_psT3] if c == 0 else (), pwrites=() if c == 0 else [b_psT3])
            if t < 0:
                evac(H2TH, psT3.rearrange("p (c t) -> p c t", c=8), reads=[b_psT3], writes=[b_h2th])
            else:
                evac(H2T[:, :, i * 128:(i + 1) * 128], psT3.rearrange("p (c t) -> p c t", c=8),
                     reads=[b_psT3], writes=[b_h2t] if i == 0 else (), pwrites=() if i == 0 else [b_h2t])

        MIX_FN = {"A": mix_A, "B": mix_B, "C": mix_C, "D": mix_D}
        MIX_HOOKS = {0: ["A0", "A1", "B0"], 1: ["C0"], 3: ["B1", "A2"], 4: ["C1"], 5: ["D0"], 6: ["B2", "A3"],
                     7: ["C2"], 8: ["D1"], 9: ["B3"], 10: ["C3"], 11: ["D2"], 14: ["D3"]}

        wup_state = {"next": 0}
        wup_sched = []

        def load_wup(idx):
            if idx >= len(wup_sched):
                return
            _, pair = wup_sched[idx]
            k = idx % 3
            src = wup_bf[128 * pair:128 * (pair + 1), :]
            sc.dma("sp", ch_wup[k], lambda e: e.dma_start(out=WUP[k].rearrange("p c t n -> p (c t n)"), in_=src),
                   reads=[b_wupbf], writes=[b_wup[k]])

        wd_sched = []

        def load_wd(idx):
            if idx >= len(wd_sched):
                return
            _, half, grp = wd_sched[idx]
            k = idx % NWD
            r_ = (half * 8 + grp) * 128
            src = wdn_bf[r_:r_ + 128, :]
            sc.dma("sp", ch_wd[k], lambda e: e.dma_start(out=WD[k].rearrange("p f n -> p (f n)"), in_=src),
                   reads=[b_wdnbf], writes=[b_wd[k]])

        for t in range(8):
            for pair in range(32):
                wup_sched.append((t, pair))
        for t in range(8):
            for half in range(2):
                for grp in range(8):
                    wd_sched.append((t, half, grp))
        wup_idx = {"i": 0}
        wd_idx = {"i": 0}

        def ffn_up(t):
            psH = bank(0)
            for pair in range(32):
                idx = wup_idx["i"]
                wup_idx["i"] += 1
                load_wup(idx + 2)
                k = idx % 3
                cbs = []
                for tt in range(2):
                    fc = tt * 32 + pair
                    if t == 0:
                        for d in range(8):
                            sc.op("pe", lambda e, d=d, tt=tt, fc=fc, k=k: e.matmul(
                                psH[:, 2 * fc:2 * fc + 2], lhsT=WUP[k][:, d, tt, :], rhs=H2TH[:, d, 126:128],
                                start=(d == 0), stop=(d == 7)),
                                reads=[b_wup[k], b_h2th], writes=[b_psT3] if d == 0 else (),
                                pwrites=() if d == 0 else [b_psT3])
                        sc.op("act", lambda e, fc=fc: e.copy(out=uhalo[:, fc, :], in_=psH[:, 2 * fc:2 * fc + 2]),
                              reads=[b_psT3], pwrites=[b_uhalo])
                    pu = cnt3["psu"] % 3
                    cnt3["psu"] += 1
                    for d in range(8):
                        sc.op("pe", lambda e, d=d, tt=tt, pu=pu, k=k: e.matmul(
                            psU[pu], lhsT=WUP[k][:, d, tt, :], rhs=H2T[:, d, :],
                            start=(d == 0), stop=(d == 7)),
                            reads=[b_wup[k], b_h2t], writes=[b_psU[pu]] if d == 0 else (),
                            pwrites=() if d == 0 else [b_psU[pu]])
                    ub = cnt3["ub"] % 4
                    cnt3["ub"] += 1
                    sc.op("pool", lambda e, fc=fc, ub=ub: e.tensor_copy(out=UB[ub][:, 0:2], in_=uhalo[:, fc, :]),
                          reads=[b_uhalo], writes=[b_ub[ub]])
                    sc.op("act", lambda e, pu=pu, ub=ub: e.copy(out=UB[ub][:, 2:514], in_=psU[pu]),
                          reads=[b_psU[pu]], pwrites=[b_ub[ub]])
                    sc.op("pool", lambda e, fc=fc, ub=ub: e.tensor_copy(out=uhalo[:, fc, :], in_=UB[ub][:, 512:514]),
                          reads=[b_ub[ub]], pwrites=[b_uhalo])
                    cbi = cnt3["cb"] % 4
                    cnt3["cb"] += 1
                    cbs.append(cbi)
                    sc.op("act", lambda e, fc=fc, pu=pu, cbi=cbi: e.activation(
                        out=CBUF[cbi], in_=psU[pu], func=AF.Identity, bias=convb[:, fc:fc + 1], scale=convw[:, fc, 2:3]),
                        reads=[b_psU[pu], b_cB], writes=[b_cbuf[cbi]])
                    sc.op("dve", lambda e, fc=fc, ub=ub, cbi=cbi: e.scalar_tensor_tensor(
                        out=CBUF[cbi], in0=UB[ub][:, 1:513], scalar=convw[:, fc, 1:2], in1=CBUF[cbi],
                        op0=ALU.mult, op1=ALU.add), reads=[b_ub[ub], b_cB, b_cbuf[cbi]], writes=[b_cbuf[cbi]])
                    sc.op("dve", lambda e, fc=fc, ub=ub, cbi=cbi: e.scalar_tensor_tensor(
                        out=CBUF[cbi], in0=UB[ub][:, 0:512], scalar=convw[:, fc, 0:1], in1=CBUF[cbi],
                        op0=ALU.mult, op1=ALU.add), reads=[b_ub[ub], b_cB, b_cbuf[cbi]], writes=[b_cbuf[cbi]])
                gi = cnt3["gg"] % 2
                cnt3["gg"] += 1
                sc.op("act", lambda e, gi=gi, c0=cbs[0]: e.activation(out=GG[gi], in_=CBUF[c0], func=AF.Gelu_apprx_tanh),
                      reads=[b_cbuf[cbs[0]]], writes=[b_gg[gi]])
                sc.op("dve", lambda e, gi=gi, c1=cbs[1], pair=pair: e.tensor_tensor(out=AT[:, pair, :], in0=GG[gi], in1=CBUF[c1],
                                                                                     op=ALU.mult),
                      reads=[b_gg[gi], b_cbuf[cbs[1]]], writes=[b_at[pair // 4]] if pair % 4 == 0 else (),
                      pwrites=() if pair % 4 == 0 else [b_at[pair // 4]])

        def ffn_down(t, nxt=None):
            s = t % 2
            hook = {(0, 1): 0, (0, 5): 1, (1, 1): 2, (1, 5): 3}
            for half in range(2):
                for grp in range(8):
                    idx = wd_idx["i"]
                    wd_idx["i"] += 1
                    load_wd(idx + NWD - 1)
                    k = idx % NWD
                    for fi in range(4):
                        f = grp * 4 + fi
                        for blk in range(4):
                            sc.op("pe", lambda e, f=f, fi=fi, blk=blk, k=k: e.matmul(
                                psD[blk], lhsT=AT[:, f, blk * 128:(blk + 1) * 128], rhs=WD[k][:, fi, :],
                                start=(f == 0), stop=(f == 31)),
                                reads=[b_at[grp], b_wd[k]], writes=[b_psD[blk]] if f == 0 else (),
                                pwrites=() if f == 0 else [b_psD[blk]])
                    if nxt is not None:
                        for st in MIX_HOOKS.get(half * 8 + grp, ()):
                            MIX_FN[st[0]](nxt, int(st[1]))
                hsl = slice(512 * half, 512 * (half + 1))
                for blk in range(4):
                    sc.op("act", lambda e, blk=blk, hsl=hsl: e.copy(out=FT[:, blk, hsl], in_=psD[blk]),
                          reads=[b_psD[blk]], writes=[b_ft] if (half == 0 and blk == 0) else (),
                          pwrites=() if (half == 0 and blk == 0) else [b_ft])
                for blk in range(4):
                    sc.op("act", lambda e, blk=blk, hsl=hsl, half=half: e.activation(
                        out=SQ3[:, 0:512], in_=FT[:, blk, hsl], func=AF.Square, accum_out=ssf[:, blk, half:half + 1]),
                        reads=[b_ft], writes=[b_sq3, b_ssf] if (half == 0 and blk == 0) else [b_sq3],
                        pwrites=[] if (half == 0 and blk == 0) else [b_ssf])
            sc.op("dve", lambda e: e.tensor_tensor(out=rsf, in0=ssf[:, :, 0], in1=ssf[:, :, 1], op=ALU.add),
                  reads=[b_ssf], writes=[b_rsf])
            rsqrt_ops(rsf, rsf, 1.0 / D, b_rsf, b_rsf)
            for blk in range(4):
                sc.op("dve", lambda e, blk=blk: e.scalar_tensor_tensor(out=FT[:, blk, :], in0=FT[:, blk, :],
                                                                        scalar=rsf[:, blk:blk + 1], in1=gpost2,
                                                                        op0=ALU.mult, op1=ALU.mult),
                      reads=[b_ft, b_rsf, b_cB], pwrites=[b_ft])
                sc.op("dve", lambda e, blk=blk: e.tensor_tensor(out=FT[:, blk, :], in0=FT[:, blk, :], in1=XO[s][:, blk, :],
                                                                 op=ALU.add),
                      reads=[b_ft, b_xo[s]], pwrites=[b_ft])
            dst = out_d[512 * t:512 * (t + 1), :].rearrange("(i p) d -> p i d", p=128)
            sc.dma("sp", ch_out, lambda e: e.dma_start(out=dst, in_=FT), reads=[b_ft])

        load_wup(0)
        load_wup(1)
        for k_ in range(NWD - 1):
            load_wd(k_)
        for st in ("A-", "A0", "B-", "A1", "C-", "X1", "B0", "D-", "C0", "B1", "A2", "C1", "D0", "B2", "A3", "C2", "D1",
                   "B3", "C3", "D2", "D3"):
            if st == "X1":
                load_xo(1)
            elif st[1] == "-":
                MIX_FN[st[0]](-1, 0)
            else:
                MIX_FN[st[0]](0, int(st[1]))
        for t in range(8):
            ffn_up(t)
            ffn_down(t, nxt=(t + 1 if t + 1 < 8 else None))
            if t + 2 < 8:
                load_xo(t + 2)
    elif debug:
        pass

    fin = Op("sp", None)
    chan_last = {}
    for o in sc.all_ops:
        if o.chan is not None:
            chan_last[id(o.chan)] = o
    for d in chan_last.values():
        fin.deps.append(d)
    for e in COMPUTE:
        cands = [o for o in sc.ops[e] if o.chan is None and o.fn is not None]
        if cands:
            cands[-1].signal = True
            cands[-1].sigto["sp"] = 0
            fin.deps.append(cands[-1])
    sc.ops["sp"].append(fin)

    sc.finalize()

    sem_ctx = []
    sems = {}
    for e in COMPUTE:
        c = nc.semaphore("s_" + e)
        sems[e] = c.__enter__()
        sem_ctx.append(c)
    for ch in sc.chans:
        c = nc.semaphore("c_" + ch.name)
        ch.sem = c.__enter__()
        sem_ctx.append(c)
    with nc.Block() as block:
        sc.emit(nc, block, sems)
    for c in reversed(sem_ctx):
        c.__exit__(None, None, None)
    ctx_ps.__exit__(None, None, None)
    ctx_arena.__exit__(None, None, None)
    return nc


def _alibi_slopes(n):
    def pow2(m):
        start = 2.0 ** (-8.0 / m)
        return [start ** (i + 1) for i in range(m)]
    if math.log2(n).is_integer():
        s = pow2(n)
    else:
        c = 2 ** int(math.floor(math.log2(n)))
        s = pow2(c) + pow2(2 * c)[0::2][: n - c]
    return np.array(sorted(s, reverse=True), dtype=np.float32)


def _const_tables(r):
    slopes = _alibi_slopes(12).astype(np.float64)
    swa_sl = slopes[:8][4 * r:4 * r + 4]
    dif_sl = slopes[8:][2 * r:2 * r + 2]
    k = np.arange(128, dtype=np.float64)
    dbias = np.zeros((128, 2, 64), np.float64)
    for h in range(2):
        for rel in range(64):
            dbias[:, h, rel] = dif_sl[h] * (k - 127.0 - 128.0 * rel)
    q = np.arange(128, dtype=np.float64)
    swab = np.zeros((128, 2, 4, 128), np.float64)
    for g in range(4):
        dprev = q[None, :] + 128.0 - k[:, None]
        swab[:, 0, g, :] = np.where(k[:, None] > q[None, :], -swa_sl[g] * dprev, NEG)
        dcur = q[None, :] - k[:, None]
        swab[:, 1, g, :] = np.where(k[:, None] <= q[None, :], -swa_sl[g] * dcur, NEG)
    maskT = np.where(k[:, None] <= q[None, :], 1.0, 0.0)
    return dbias.astype(np.float32), swab.astype(np.float32), maskT.astype(np.float32)


def _rep(v):
    return np.broadcast_to(np.asarray(v, np.float32).reshape(1, -1), (128, v.size))


_CACHE = {}


def _get_program():
    if "nc" not in _CACHE:
        _CACHE["nc"] = build_program(debug=False)
    return _CACHE["nc"]


def make_in_maps(x, attn_pre_g, w_in, swa_sinks, swa_out_g, diff_lq1, diff_lk1, diff_lq2, diff_lk2, diff_subln_g,
                 w_out, attn_post_g, ffn_pre_g, w_up, conv_w, conv_b, w_down, ffn_post_g):
    f32 = np.float32
    x = np.asarray(x, f32)
    w_in = np.asarray(w_in, f32)[0]
    w_out = np.asarray(w_out, f32)[0]
    w_up = np.asarray(w_up, f32)[0]
    w_down = np.asarray(w_down, f32)[0]
    conv_w = np.asarray(conv_w, f32)[0]
    conv_b = np.asarray(conv_b, f32)[0]
    wup_h = w_up.reshape(8, 128, 2, 32, 128).transpose(3, 1, 0, 2, 4).reshape(4096, 2048)
    wup_h = np.ascontiguousarray(wup_h)
    wdn_h = w_down.reshape(8, 4, 128, 2, 512).transpose(3, 0, 2, 1, 4).reshape(2048, 2048)
    wdn_h = np.ascontiguousarray(wdn_h)
    cw = conv_w.T.reshape(64, 128, 3).transpose(1, 0, 2).reshape(128, 192)
    cbias = conv_b.reshape(64, 128).T
    perm = np.concatenate([np.arange(0, 256), np.arange(512, 768), np.arange(256, 512), np.arange(768, 1024)])
    wout_h = np.ascontiguousarray(w_out[perm, :])
    swag_full = np.asarray(swa_out_g, f32)[0]
    eye = np.eye(128, dtype=f32)
    in_maps = []
    for c in range(8):
        b, r = c // 2, c % 2
        dbias, swab, maskT = _const_tables(r)
        qb = w_in[:, 768 + 256 * r:768 + 256 * r + 256]
        kb = w_in[:, 1280 + 256 * r:1280 + 256 * r + 256]
        qa = w_in[:, 256 * r:256 * r + 256]
        ka = w_in[:, 512 + 64 * r:512 + 64 * r + 64]
        vb = w_in[:, 1792 + 256 * r:1792 + 256 * r + 256]
        va = w_in[:, 640 + 64 * r:640 + 64 * r + 64]
        wsel = np.ascontiguousarray(np.concatenate([qb, kb, qa, ka, ka, vb, va], axis=1))
        assert wsel.shape[1] == NWSEL
        ca = np.zeros((128, CA_N), f32)
        ca[:, 0:128] = dbias.reshape(128, 128)
        ca[:, 128:1152] = swab.reshape(128, 2, 2, 2, 128).transpose(0, 3, 1, 2, 4).reshape(128, 1024)
        ca[:, 1152:2176] = _rep(np.asarray(attn_pre_g, f32)[0])
        ca[:, 2176:2304] = _rep(np.asarray(diff_subln_g, f32)[0])
        ca[:, 2304:2308] = _rep(np.asarray(swa_sinks, f32)[0][4 * r:4 * r + 4])
        lq = np.concatenate([np.asarray(diff_lq1, f32)[0], np.asarray(diff_lk1, f32)[0],
                             np.asarray(diff_lq2, f32)[0], np.asarray(diff_lk2, f32)[0]])
        ca[:, 2320:2576] = _rep(lq)
        cbk = np.zeros((128, CB_N), f32)
        cbk[:, 0:512] = _rep(swag_full)
        cbk[:, 512:1536] = _rep(np.asarray(attn_post_g, f32)[0])
        cbk[:, 1536:2560] = _rep(np.asarray(ffn_pre_g, f32)[0])
        cbk[:, 2560:3584] = _rep(np.asarray(ffn_post_g, f32)[0])
        cbk[:, 3584:3776] = cw
        cbk[:, 3776:3840] = cbias
        cbk[:, 3840] = 1.0 - r
        cbk[:, 3841] = float(r)
        chh = np.zeros((128, 512), f32)
        chh[:, 0:128] = eye
        chh[:, 128:256] = maskT
        chh[:, 256:384] = eye * (1.0 - r)
        chh[:, 384:512] = eye * float(r)
        xs = x[b]
        x_own = np.ascontiguousarray(xs[4096 * r:4096 * r + 4096])
        x_halo = np.ascontiguousarray(xs[3968:4096]) if r == 1 else np.zeros((128, D), f32)
        in_maps.append({
            "x_seq": np.ascontiguousarray(xs), "x_own": x_own, "x_halo": x_halo,
            "wsel": wsel, "wout": wout_h, "wup": wup_h, "wdn": wdn_h,
            "ca": ca, "cb": cbk, "ch": chh.astype(ml_dtypes.bfloat16),
        })
    return in_maps


def kernel(**inputs):
    in_maps = make_in_maps(**inputs)
    nc = _get_program()
    res = run_bass_kernel_spmd(nc, in_maps, core_ids=list(range(8)))
    out = np.zeros((4, S, D), np.float32)
    for c in range(8):
        b, r = c // 2, c % 2
        out[b, 4096 * r:4096 * r + 4096] = np.asarray(res.results[c]["out"], np.float32)
    return out
```

```python
import math
import numpy as np
import ml_dtypes
import concourse.bass as bass
import concourse.mybir as mybir
from concourse.bass_utils import run_bass_kernel_spmd

F32 = mybir.dt.float32
BF16 = mybir.dt.bfloat16
AF = mybir.ActivationFunctionType
ALU = mybir.AluOpType
AX = mybir.AxisListType

S = 8192
D = 1024
NEG = -30000.0
EPS = 1e-6
LAMBDA_INIT = 0.8 - 0.6 * math.exp(0.0)
NWSEL = 1216


class Buf:
    __slots__ = ("name", "w", "r")

    def __init__(self, name):
        self.name = name
        self.w = []
        self.r = []


class Chan:
    __slots__ = ("name", "sem", "count", "unit")

    def __init__(self, name, unit=16):
        self.name = name
        self.sem = None
        self.count = 0
        self.unit = unit


class Op:
    __slots__ = ("eng", "fn", "deps", "signal", "signo", "chan", "chval", "ndma", "pos", "sigto")

    def __init__(self, eng, fn):
        self.eng = eng
        self.fn = fn
        self.deps = []
        self.signal = False
        self.signo = 0
        self.chan = None
        self.chval = 0
        self.ndma = 0
        self.pos = 0
        self.sigto = {}


COMPUTE = ("pe", "act", "dve", "pool")


class Sched:
    def __init__(self):
        self.ops = {e: [] for e in ("pe", "act", "dve", "pool", "sp")}
        self.chans = []
        self.all_ops = []

    def chan(self, name, unit=16):
        c = Chan(name, unit)
        self.chans.append(c)
        return c

    def _add(self, op, reads, writes, pwrites):
        deps = []
        for b in reads:
            for w in b.w:
                deps.append((w, "raw"))
            if b.name.startswith("ps") or b.name.startswith("bank"):
                for r in b.r:
                    if r.eng != op.eng:
                        deps.append((r, "raw"))
        for b in writes:
            for w in b.w:
                deps.append((w, "waw"))
            for r in b.r:
                deps.append((r, "war"))
        for b in pwrites:
            for r in b.r:
                deps.append((r, "war"))
        best = {}
        for d, kind in deps:
            if d is op:
                continue
            if d.chan is None and d.eng == op.eng and op.chan is None:
                if op.eng == "pe":
                    continue
            if d.chan is not None:
                key = ("c", id(d.chan))
                if key not in best or best[key].chval < d.chval:
                    best[key] = d
            else:
                key = ("e", d.eng)
                if key not in best or best[key].pos < d.pos:
                    best[key] = d
        for d in best.values():
            op.deps.append(d)
            if d.chan is None:
                d.signal = True
                d.sigto[op.eng] = 0
        op.pos = len(self.ops[op.eng])
        for b in reads:
            b.r.append(op)
        for b in writes:
            b.w = [op]
            b.r = []
        for b in pwrites:
            b.w.append(op)
        self.ops[op.eng].append(op)
        self.all_ops.append(op)
        return op

    def op(self, eng, fn, reads=(), writes=(), pwrites=()):
        return self._add(Op(eng, fn), reads, writes, pwrites)

    def dma(self, eng, chan, fns, reads=(), writes=(), pwrites=()):
        if not isinstance(fns, (list, tuple)):
            fns = [fns]
        o = Op(eng, fns)
        o.chan = chan
        o.ndma = len(fns)
        chan.count += chan.unit * len(fns)
        o.chval = chan.count
        return self._add(o, reads, writes, pwrites)

    def barrier(self, bufs=()):
        lasts = []
        for e in COMPUTE:
            if self.ops[e]:
                cands = [o for o in self.ops[e] if o.chan is None and o.fn is not None]
                if cands:
                    lasts.append(cands[-1])
        chan_last = {}
        for o in self.all_ops:
            if o.chan is not None:
                chan_last[id(o.chan)] = o
        for e in ("pe", "act", "dve", "pool", "sp"):
            o = Op(e, None)
            for d in lasts:
                if d.eng != e:
                    o.deps.append(d)
                    d.signal = True
                    d.sigto[e] = 0
            for d in chan_last.values():
                o.deps.append(d)
            o.pos = len(self.ops[e])
            self.ops[e].append(o)
            self.all_ops.append(o)
        for b in bufs:
            b.w = []
            b.r = []

    def finalize(self):
        for e in COMPUTE:
            n = 0
            for o in self.ops[e]:
                if o.chan is None and o.signal:
                    n += 1
                    o.signo = n

    def emit(self, nc, block, sems):
        handles = {"pe": "tensor", "act": "scalar", "dve": "vector", "pool": "gpsimd", "sp": "sync"}
        sch = self

        def run(engname):
            def body(eng):
                seen = {}
                for o in sch.ops[engname]:
                    for d in o.deps:
                        if d.chan is not None:
                            key, sem, val = ("c", id(d.chan)), d.chan.sem, d.chval
                        else:
                            key, sem, val = ("e", d.eng), sems[d.eng], d.signo
                        if seen.get(key, 0) >= val:
                            continue
                        seen[key] = val
                        eng.wait_ge(sem, val)
                    if o.fn is None:
                        continue
                    if o.chan is not None:
                        for f in o.fn:
                            ins = f(eng)
                            if o.chan.unit == 16:
                                ins.then_inc(o.chan.sem, 16)
                            else:
                                ins.then_inc(o.chan.sem)
                    else:
                        ins = o.fn(eng)
                        if o.signal:
                            ins.then_inc(sems[engname], 1)
            return body

        block.tensor(run("pe"))
        block.scalar(run("act"))
        block.vector(run("dve"))
        block.gpsimd(run("pool"))
        block.sync(run("sp"))


class Arena:
    def __init__(self, ap_bf16, nbytes):
        self.ap = ap_bf16
        self.nbytes = nbytes
        self.off = 0

    def set(self, off):
        self.off = off

    def alloc(self, nbytes, dtype=BF16, shape=None):
        nbytes = (nbytes + 63) // 64 * 64
        assert self.off + nbytes <= self.nbytes, ("arena overflow", self.off, nbytes, self.nbytes)
        a = self.ap[:, self.off // 2:(self.off + nbytes) // 2]
        self.off += nbytes
        if dtype == F32:
            a = a.bitcast(F32)
        return a

    def f32(self, *shape):
        n = int(np.prod(shape))
        a = self.alloc(n * 4, F32)[:, 0:n]
        return _shape(a, shape)

    def bf(self, *shape):
        n = int(np.prod(shape))
        a = self.alloc(n * 2, BF16)[:, 0:n]
        return _shape(a, shape)


def _shape(a, shape):
    if len(shape) == 1:
        return a
    if len(shape) == 2:
        return a.rearrange("p (a b) -> p a b", a=shape[0])
    if len(shape) == 3:
        return a.rearrange("p (a b c) -> p a b c", a=shape[0], b=shape[1])
    raise ValueError(shape)


ARENA_BYTES = 207 * 1024
CA_N = 2576
CB_N = 3856


def build_program(debug=False, phases=3):
    nc = bass.Bass("TRN2", target_bir_lowering=False)
    sc = Sched()
    import os
    SKIP = set(os.environ.get("KSKIP", "").split(","))

    def dram_in(name, shape, dt=F32):
        return nc.dram_tensor(name, list(shape), dt, kind="ExternalInput").ap()

    x_seq = dram_in("x_seq", [S, D])
    x_own = dram_in("x_own", [4096, D])
    x_halo = dram_in("x_halo", [128, D])
    wsel_d = dram_in("wsel", [D, NWSEL])
    wout_d = dram_in("wout", [D, D])
    wup_d = dram_in("wup", [4096, 2048])
    wdn_d = dram_in("wdn", [2048, 2048])
    ca_d = dram_in("ca", [128, CA_N])
    cb_d = dram_in("cb", [128, CB_N])
    ch_d = dram_in("ch", [128, 512], BF16)
    out_d = nc.dram_tensor("out", [4096, D], F32, kind="ExternalOutput").ap()
    wup_bf = nc.dram_tensor("wup_bf", [4096, 2048], BF16)
    wdn_bf = nc.dram_tensor("wdn_bf", [2048, 2048], BF16)
    NCH = 8
    xin_t = [nc.dram_tensor("xch_in%d" % k, [1024, 512], BF16) for k in range(NCH)]
    xout_t = [nc.dram_tensor("xch_out%d" % k, [2048, 512], BF16) for k in range(NCH)]
    xin_c = [t.ap() for t in xin_t]
    xout_c = [t.ap().rearrange("(s t) c -> s t c", s=2) for t in xout_t]
    dbg = None
    if debug:
        dbg = nc.dram_tensor("dbg", [S, 512], BF16, kind="ExternalOutput").ap()

    ctx_arena = nc.sbuf_tensor("arena", [128, ARENA_BYTES // 2], BF16)
    ctx_ps = nc.psum_tensor("ps", [128, 4096], F32)
    arena_t = ctx_arena.__enter__()
    ps = ctx_ps.__enter__()
    A = Arena(arena_t[:, :], ARENA_BYTES)

    def rsqrt_ops(out, in_, inv_n, b_in, b_out, post_scale=1.0):
        sc.op("act", lambda e: e.activation(out=out, in_=in_, func=AF.Ln, bias=EPS, scale=inv_n),
              reads=[b_in], writes=[b_out])
        sc.op("act", lambda e: e.activation(out=out, in_=out, func=AF.Exp, bias=math.log(post_scale), scale=-0.5),
              reads=[b_out], writes=[b_out])

    def bank(b, n=1):
        return ps[:, 512 * b:512 * (b + n)]

    cH = A.bf(512)
    ident = cH[:, 0:128]
    maskT = cH[:, 128:256]
    sel0 = cH[:, 256:384]
    sel1 = cH[:, 384:512]
    uhalo = A.f32(64, 2)
    lam = A.f32(8)
    b_cH = Buf("cH")
    b_uhalo = Buf("uhalo")
    ch_c = sc.chan("consts")
    sc.dma("sp", ch_c, lambda e: e.dma_start(out=cH, in_=ch_d[:, :]), writes=[b_cH])

    ch_wconv = sc.chan("wconv")
    b_wupbf = Buf("wup_bf")
    b_wdnbf = Buf("wdn_bf")
    PACE = A.f32(2)
    b_pace = Buf("pace")
    conv_list = []
    for k in range(32):
        conv_list.append((lambda e, k=k: e.dma_start(out=wup_bf[128 * k:128 * (k + 1), :], in_=wup_d[128 * k:128 * (k + 1), :]),
                          b_wupbf))
    for k in range(16):
        conv_list.append((lambda e, k=k: e.dma_start(out=wdn_bf[128 * k:128 * (k + 1), :], in_=wdn_d[128 * k:128 * (k + 1), :]),
                          b_wdnbf))
    conv_pos = {"i": 0}

    def emit_conv(n, pace_buf=None):
        if conv_pos["i"] >= len(conv_list):
            return
        if pace_buf is not None:
            sc.op("pool", lambda e: e.memset(PACE, 0.0), reads=[pace_buf], writes=[b_pace])
        for _ in range(n):
            if conv_pos["i"] >= len(conv_list):
                return
            fn, buf = conv_list[conv_pos["i"]]
            conv_pos["i"] += 1
            sc.dma("pool", ch_wconv, fn, pwrites=[buf])

    G_END = A.off
    KbT = A.bf(2, S)
    QbT = A.bf(2, S)
    Vb = A.bf(64, 2, 130)
    cA = A.f32(CA_N)
    P12_END = A.off
    dbias = cA[:, 0:128].rearrange("p (h r) -> p h r", h=2)
    swab = cA[:, 128:1152].rearrange("p (k n) -> p k n", k=2)
    gpre = cA[:, 1152:2176]
    subg = cA[:, 2176:2304]
    sinks = cA[:, 2304:2308]
    lqk = cA[:, 2320:2576].rearrange("p (a n) -> p a n", a=4)
    b_cA = Buf("cA")
    ch_ca = sc.chan("ca")
    sc.dma("sp", ch_ca, lambda e: e.dma_start(out=cA, in_=ca_d[:, :]), writes=[b_cA])

    Wsel = A.bf(8, NWSEL)
    b_wsel = Buf("wsel")
    ch_ws = sc.chan("wsel")
    P1_BASE = A.off
    XT = [A.f32(4, 1024) for _ in range(2)]
    b_xt = [Buf("xt0"), Buf("xt1")]
    ch_xt = [sc.chan("xt0"), sc.chan("xt1")]
    HB = [A.bf(1024) for _ in range(2)]
    b_hb = [Buf("hb0"), Buf("hb1")]
    HT = [A.bf(8, 512) for _ in range(2)]
    b_ht = [Buf("ht0"), Buf("ht1")]
    QAT = [A.bf(2, 512) for _ in range(2)]
    b_qat = [Buf("qat0"), Buf("qat1")]
    KAT = [A.bf(512) for _ in range(2)]
    b_kat = [Buf("kat0"), Buf("kat1")]
    VA = [A.bf(4, 66) for _ in range(2)]
    b_va = [Buf("va0"), Buf("va1")]
    STMP = [A.f32(512) for _ in range(2)]
    b_stmp = [Buf("stmp0"), Buf("stmp1")]
    PTS = [[A.bf(512) for _ in range(2)] for _ in range(2)]
    b_pts = [[Buf("pts%d%d" % (a, b)) for b in range(2)] for a in range(2)]
    YAST = [A.bf(4, 256) for _ in range(2)]
    b_yast = [Buf("yast0"), Buf("yast1")]
    ch_yast = [sc.chan("yast0"), sc.chan("yast1")]
    SQJ = A.bf(1024)
    b_sqj = Buf("sqj")
    STAT = A.f32(64)
    ss1 = [STAT[:, 0:4], STAT[:, 4:8]]
    rstd1 = [STAT[:, 8:12], STAT[:, 12:16]]
    sinkexp = STAT[:, 16:20]
    den = [STAT[:, 20:24], STAT[:, 24:28]]
    rec = [STAT[:, 28:32], STAT[:, 32:36]]
    lsum = STAT[:, 40:44]
    b_ss1 = [Buf("ss1a"), Buf("ss1b")]
    b_rstd1 = [Buf("rstd1a"), Buf("rstd1b")]
    b_sinkexp = Buf("sinkexp")
    b_den = [Buf("den0"), Buf("den1")]
    b_rec = [Buf("rec0"), Buf("rec1")]
    b_lam = Buf("lam")
    b_lsum = Buf("lsum")
    LJ = A.f32(4, 64)
    P1_END = A.off

    b_xchin = [Buf("xch_in%d" % k) for k in range(8)]
    b_xchout = [Buf("xch_out%d" % k) for k in range(8)]
    b_kqv = Buf("kqv")
    if "memset4d" not in SKIP:
        sc.op("pool", lambda e: e.memset(Vb[:, :, :, 128:130], 1.0), pwrites=[b_kqv])
    sc.op("pool", lambda e: e.memset(VA[0][:, :, 64:66], 1.0), writes=[b_va[0]])
    sc.op("pool", lambda e: e.memset(VA[1][:, :, 64:66], 1.0), writes=[b_va[1]])
    if "wsel" not in SKIP:
        sc.dma("pool", ch_ws,
               [lambda e, c=c: e.dma_start(out=Wsel[:, c, :], in_=wsel_d[c * 128:(c + 1) * 128, :]) for c in range(8)],
               writes=[b_wsel])

    sc.op("act", lambda e: e.activation(out=sinkexp, in_=sinks, func=AF.Exp), reads=[b_cA], writes=[b_sinkexp])
    b_lj = Buf("lj")
    if "lam" not in SKIP:
      sc.op("dve", lambda e: e.tensor_tensor(out=LJ[:, 0:2, :], in0=lqk[:, 0:4:2, :], in1=lqk[:, 1:4:2, :], op=ALU.mult),
          reads=[b_cA], writes=[b_lj])
    sc.op("dve", lambda e: e.tensor_reduce(out=lsum[:, 0:2], in_=LJ[:, 0:2, :], axis=AX.X, op=ALU.add),
          reads=[b_lj], writes=[b_lsum])
    b_lexp = Buf("lexp")
    sc.op("act", lambda e: e.activation(out=lsum[:, 2:4], in_=lsum[:, 0:2], func=AF.Exp), reads=[b_lsum], writes=[b_lexp])
    sc.op("dve", lambda e: e.scalar_tensor_tensor(out=lam[:, 0:1], in0=lsum[:, 2:3], scalar=LAMBDA_INIT, in1=lsum[:, 3:4],
                                                   op0=ALU.add, op1=ALU.subtract),
          reads=[b_lexp], writes=[b_lam])

    psT = [bank(0).bitcast(BF16), bank(1).bitcast(BF16)]
    b_psT = [Buf("psT0"), Buf("psT1")]
    psF = [bank(2), bank(3)]
    b_psF = [Buf("psF0"), Buf("psF1")]
    psV = bank(4)
    b_psV = Buf("psV")
    psO = bank(5)
    b_psO = Buf("psO")
    psS = [bank(6), bank(7)]
    b_psS = [Buf("psS0"), Buf("psS1")]

    NT = S // 512

    def load_x(j):
        s = j % 2
        src = x_seq[512 * j:512 * (j + 1), :].rearrange("(i p) d -> p i d", p=128)
        sc.dma("sp", ch_xt[s], lambda e: e.dma_start(out=XT[s], in_=src), writes=[b_xt[s]])

    def stage_ss(j):
        s = j % 2
        for i in range(4):
            sc.op("act", lambda e, i=i: e.activation(out=SQJ, in_=XT[s][:, i, :], func=AF.Square,
                                                      accum_out=ss1[s][:, i:i + 1]),
                  reads=[b_xt[s]], writes=[b_sqj, b_ss1[s]] if i == 0 else [b_sqj], pwrites=[] if i == 0 else [b_ss1[s]])
        rsqrt_ops(rstd1[s], ss1[s], 1.0 / D, b_ss1[s], b_rstd1[s])

    cnt = {"hb": 0, "psT": 0, "psF": 0, "evac": 0}

    def evac(out, in_, reads, writes=(), pwrites=()):
        cnt["evac"] += 1
        use_act = False
        if "evacdve" in SKIP:
            use_act = False
        if "evacact" in SKIP:
            use_act = True
        if use_act:
            return sc.op("act", lambda e: e.copy(out=out, in_=in_), reads=reads, writes=writes, pwrites=pwrites)
        return sc.op("dve", lambda e: e.tensor_copy(out=out, in_=in_), reads=reads, writes=writes, pwrites=pwrites)

    nstate = {}

    def norm_a(j, i):
        s = j % 2
        hs = cnt["hb"] % 2
        cnt["hb"] += 1
        nstate[(j, i)] = hs
        sc.op("dve", lambda e: e.scalar_tensor_tensor(out=HB[hs], in0=XT[s][:, i, :], scalar=rstd1[s][:, i:i + 1],
                                                       in1=gpre, op0=ALU.mult, op1=ALU.mult),
              reads=[b_xt[s], b_rstd1[s], b_cA], writes=[b_hb[hs]])

    def norm_b(j, i):
        s = j % 2
        hs = nstate[(j, i)]
        tsl = cnt["psT"] % 2
        cnt["psT"] += 1
        for c in range(8):
            sc.op("pe", lambda e, c=c: e.transpose(out=psT[tsl][:, c * 128:(c + 1) * 128],
                                                   in_=HB[hs][:, c * 128:(c + 1) * 128], identity=ident),
                  reads=[b_hb[hs], b_cH], writes=[b_psT[tsl]] if c == 0 else (), pwrites=() if c == 0 else [b_psT[tsl]])
        evac(HT[s][:, :, i * 128:(i + 1) * 128], psT[tsl].rearrange("p (c t) -> p c t", c=8),
             reads=[b_psT[tsl]], writes=[b_ht[s]] if i == 0 else (), pwrites=() if i == 0 else [b_ht[s]])

    def stage_norm_block(j, i):
        norm_a(j, i)
        norm_b(j, i)

    def proj_fm(j, m):
        s = j % 2
        fs = cnt["psF"] % 2
        cnt["psF"] += 1
        for d in range(8):
            sc.op("pe", lambda e, d=d: e.matmul(psF[fs], lhsT=Wsel[:, d, m * 128:(m + 1) * 128], rhs=HT[s][:, d, :],
                                                start=(d == 0), stop=(d == 7)),
                  reads=[b_wsel, b_ht[s]], writes=[b_psF[fs]] if d == 0 else (), pwrites=() if d == 0 else [b_psF[fs]])
        cols = slice(512 * j, 512 * (j + 1))
        if m < 2:
            evac(QbT[:, m, cols], psF[fs], reads=[b_psF[fs]], pwrites=[b_kqv])
        elif m < 4:
            evac(KbT[:, m - 2, cols], psF[fs], reads=[b_psF[fs]], pwrites=[b_kqv])
        elif m < 6:
            evac(QAT[s][:, m - 4, :], psF[fs], reads=[b_psF[fs]],
                 writes=[b_qat[s]] if m == 4 else (), pwrites=() if m == 4 else [b_qat[s]])
        else:
            evac(KAT[s], psF[fs], reads=[b_psF[fs]], writes=[b_kat[s]])

    def proj_tm(j, i):
        s = j % 2
        for d in range(8):
            sc.op("pe", lambda e, d=d: e.matmul(psV[:, 0:320], lhsT=HT[s][:, d, i * 128:(i + 1) * 128],
                                                rhs=Wsel[:, d, 896:1216], start=(d == 0), stop=(d == 7)),
                  reads=[b_wsel, b_ht[s]], writes=[b_psV] if d == 0 else (), pwrites=() if d == 0 else [b_psV])
        sc.op("act", lambda e: e.copy(out=Vb[:, 4 * j + i, :, 0:128], in_=psV[:, 0:256].rearrange("p (h n) -> p h n", h=2)),
              reads=[b_psV], pwrites=[b_kqv])
        sc.op("act", lambda e: e.copy(out=VA[s][:, i, 0:64], in_=psV[:, 256:320]),
              reads=[b_psV], pwrites=[b_va[s]])

    swa_cnt = {"n": 0}

    sstate = {}

    def swa_a(j, i):
        s = j % 2
        n = 4 * j + i
        blocks = []
        if n > 0:
            blocks.append((0, (j if i > 0 else j - 1), (i - 1) % 4))
        blocks.append((1, j, i))
        pb = swa_cnt["n"] % 2
        swa_cnt["n"] += 1
        c0 = 0 if len(blocks) == 2 else 256
        firstw = [True, True]
        for (kbi, jj, ii) in blocks:
            ss_ = jj % 2
            for g in range(4):
                r0 = 64 * (g % 2)
                par = g % 2
                col = (kbi * 2 + g // 2) * 128
                sc.op("pe", lambda e, g=g, r0=r0, ss_=ss_, ii=ii, par=par, col=col: e.matmul(
                    psS[par][:, col:col + 128],
                    lhsT=KAT[ss_][r0:r0 + 64, ii * 128:(ii + 1) * 128],
                    rhs=QAT[s][r0:r0 + 64, g // 2, i * 128:(i + 1) * 128],
                    start=True, stop=True, tile_position=(r0, 0)),
                    reads=[b_kat[ss_], b_qat[s]], writes=[b_psS[par]] if firstw[par] else (),
                    pwrites=() if firstw[par] else [b_psS[par]])
                firstw[par] = False
        for par in range(2):
            sc.op("dve", lambda e, par=par: e.scalar_tensor_tensor(out=STMP[par][:, c0:512], in0=psS[par][:, c0:512], scalar=0.125,
                                                                    in1=swab[:, par, c0:512], op0=ALU.mult, op1=ALU.add),
                  reads=[b_psS[par], b_cA], writes=[b_stmp[par]])
            sc.op("act", lambda e, par=par: e.activation(out=PTS[par][pb][:, c0:512], in_=STMP[par][:, c0:512], func=AF.Exp),
                  reads=[b_stmp[par]], writes=[b_pts[par][pb]])
        sstate[(j, i)] = (blocks, pb)

    def swa_b(j, i):
        s = j % 2
        n = 4 * j + i
        blocks, pb = sstate[(j, i)]
        psO_v = psO[:, 0:264].rearrange("p (g n) -> p g n", g=4)
        first = True
        for g in range(4):
            for bi, (kbi, jj, ii) in enumerate(blocks):
                ss_ = jj % 2
                par = g % 2
                col = (kbi * 2 + g // 2) * 128
                sc.op("pe", lambda e, g=g, ss_=ss_, ii=ii, bi=bi, par=par, col=col: e.matmul(
                    psO_v[:, g, 0:65], lhsT=PTS[par][pb][:, col:col + 128], rhs=VA[ss_][:, ii, 0:65],
                    start=(bi == 0), stop=(bi == len(blocks) - 1)),
                    reads=[b_pts[par][pb], b_va[ss_]], writes=[b_psO] if first else (),
                    pwrites=() if first else [b_psO])
                first = False
        ds = n % 2
        sc.op("dve", lambda e: e.tensor_tensor(out=den[ds], in0=psO_v[:, :, 64], in1=sinkexp, op=ALU.add),
              reads=[b_psO, b_sinkexp], writes=[b_den[ds]])
        sc.op("dve", lambda e: e.reciprocal(out=rec[ds], in_=den[ds]), reads=[b_den[ds]], writes=[b_rec[ds]])
        sc.op("dve", lambda e: e.tensor_tensor(out=YAST[s][:, i, :].rearrange("p (g n) -> p g n", g=4),
                                               in0=psO_v[:, :, 0:64],
                                               in1=rec[ds].unsqueeze(2).to_broadcast([128, 4, 64]), op=ALU.mult),
              reads=[b_psO, b_rec[ds]], writes=[b_yast[s]] if i == 0 else (), pwrites=() if i == 0 else [b_yast[s]])

    def store_ya(j):
        s = j % 2
        r_ = 512 * (j % 2)
        dst = xin_c[j // 2][r_:r_ + 512, 0:256].rearrange("(i p) c -> p i c", p=128)
        sc.dma("sp", ch_yast[s], lambda e: e.dma_start(out=dst, in_=YAST[s]), reads=[b_yast[s]], pwrites=[b_xchin[j // 2]])

    NT_RUN = int(os.environ.get("KNT", NT))
    if "loadx" not in SKIP:
        load_x(0)
        load_x(1)
    if "ss" not in SKIP:
        stage_ss(0)
    if "norm" not in SKIP:
        for i in range(4):
            stage_norm_block(0, i)
    for j in range(NT_RUN):
        nxt = j + 1 < NT
        if nxt:
            stage_ss(j + 1)
        items = [("Na", 0), ("fm", 4), ("fm", 5), ("fm", 6), ("Nb", 0), ("Na", 1), ("tm", 0), ("Sa", 0), ("tm", 1),
                 ("Nb", 1), ("Na", 2), ("fm", 0), ("Sb", 0), ("Sa", 1), ("tm", 2), ("Nb", 2), ("Na", 3), ("fm", 1),
                 ("Sb", 1), ("Sa", 2), ("tm", 3), ("Nb", 3), ("fm", 2), ("Sb", 2), ("Sa", 3), ("fm", 3), ("Sb", 3)]
        for kind, a in items:
            if kind == "fm":
                proj_fm(j, a)
            elif kind == "tm":
                proj_tm(j, a)
            elif kind == "Sa":
                swa_a(j, a)
            elif kind == "Sb":
                swa_b(j, a)
            elif kind == "Na" and nxt:
                norm_a(j + 1, a)
            elif kind == "Nb" and nxt:
                norm_b(j + 1, a)
        if "store" not in SKIP:
            store_ya(j)
        if j >= 2:
            emit_conv(1, pace_buf=b_yast[j % 2])
        if j + 2 < NT:
            load_x(j + 2)

    sc.barrier()
    A.set(P12_END)
    HI_BASE = 143 * 1024
    if phases >= 3:
        A.set(HI_BASE)
        cB = A.f32(CB_N)
        swag = cB[:, 0:512].rearrange("p (s n) -> p s n", s=2)
        gpost = cB[:, 512:1536]
        gffn = cB[:, 1536:2560]
        gpost2 = cB[:, 2560:3584]
        convw = cB[:, 3584:3776].rearrange("p (c k) -> p c k", k=3)
        convb = cB[:, 3776:3840]
        selsc = cB[:, 3840:3842]
        b_cB = Buf("cB")
        ch_cb = sc.chan("cb")
        sc.dma("sp", ch_cb, lambda e: e.dma_start(out=cB, in_=cb_d[:, :]), writes=[b_cB])
        Wout = A.bf(8, 1024)
        b_wout = Buf("wout")
        ch_wo = sc.chan("wout")
        sc.dma("pool", ch_wo,
               [lambda e, c=c: e.dma_start(out=Wout[:, c, :], in_=wout_d[c * 128:(c + 1) * 128, :]) for c in range(8)],
               writes=[b_wout])
        XO = [A.f32(4, 1024) for _ in range(2)]
        b_xo = [Buf("xo0"), Buf("xo1")]
        ch_xo = [sc.chan("xo0"), sc.chan("xo1")]
        def load_xo(t):
            s = t % 2
            if t < 0:
                src = x_halo[:, :]
                sc.dma("sp", ch_xo[s], lambda e: e.dma_start(out=XO[s][:, 0, :], in_=src), writes=[b_xo[s]])
            else:
                src = x_own[512 * t:512 * (t + 1), :].rearrange("(i p) d -> p i d", p=128)
                sc.dma("sp", ch_xo[s], lambda e: e.dma_start(out=XO[s], in_=src), writes=[b_xo[s]])

        load_xo(-1)
        load_xo(0)
        assert A.off <= ARENA_BYTES
        A.set(P12_END)
    PT = [A.bf(2, 512) for _ in range(3)]
    b_pt = [Buf("pt%d" % k) for k in range(3)]
    T1 = A.f32(4, 128)
    T2 = A.f32(4, 128)
    YY = A.f32(4, 128)
    SQ2 = A.f32(4, 128)
    b_t1, b_t2, b_yy, b_sq2 = Buf("t1"), Buf("t2"), Buf("yy"), Buf("sq2")
    YBST = [A.bf(4, 2, 128) for _ in range(2)]
    b_ybst = [Buf("ybst0"), Buf("ybst1")]
    ch_ybst = [sc.chan("ybst0"), sc.chan("ybst1")]
    ST2 = A.f32(32)
    recs = ST2[:, 0:8].rearrange("p (m q) -> p m q", m=2)
    ss2 = ST2[:, 8:12]
    rs2 = ST2[:, 12:16]
    b_recs, b_ss2, b_rs2 = Buf("recs"), Buf("ss2"), Buf("rs2")

    assert A.off <= HI_BASE, A.off
    psS2 = [ps[:, 0:1024].rearrange("p (m q) -> p m q", m=2), ps[:, 1024:2048].rearrange("p (m q) -> p m q", m=2)]
    b_psS2 = [Buf("psS2a"), Buf("psS2b")]
    psO2 = [ps[:, 2048:3072].rearrange("p (q n) -> p q n", q=4), ps[:, 3072:4096].rearrange("p (q n) -> p q n", q=4)]
    b_psO2 = Buf("psO2")

    units = []
    for p in range(NT):
        for hh in range(2):
            for kb in range(4 * p + 4):
                units.append((p, hh, kb))

    def rec_S(u):
        p, hh, kb = units[u]
        sb = u % 2
        jd = kb - 4 * p
        q0 = 128 * max(jd, 0)
        first = [True]

        def w():
            if first[0]:
                first[0] = False
                return dict(writes=[b_psS2[sb]])
            return dict(pwrites=[b_psS2[sb]])
        for m in range(2):
            r0 = 64 * m
            kk = KbT[r0:r0 + 64, hh, kb * 128:(kb + 1) * 128]
            sc.op("pe", lambda e, m=m, r0=r0, kk=kk: e.matmul(
                psS2[sb][:, m, q0:512], lhsT=kk, rhs=QbT[r0:r0 + 64, hh, 512 * p + q0:512 * (p + 1)],
                start=True, stop=True, tile_position=(r0, 0)), reads=[b_kqv], **w())

    def rec_E(u):
        p, hh, kb = units[u]
        sb = u % 2
        tb = u % 3
        jd = kb - 4 * p
        q0 = 128 * max(jd, 0)
        rel = 4 * p + 3 - kb
        sc.op("act", lambda e: e.activation(out=PT[tb][:, :, q0:512], in_=psS2[sb][:, :, q0:512], func=AF.Exp,
                                            bias=dbias[:, hh, rel:rel + 1], scale=0.125),
              reads=[b_psS2[sb], b_cA], writes=[b_pt[tb]])
        if jd >= 0:
            sc.op("dve", lambda e: e.tensor_tensor(out=PT[tb][:, :, q0:q0 + 128], in0=PT[tb][:, :, q0:q0 + 128],
                                                   in1=maskT.unsqueeze(1).to_broadcast([128, 2, 128]), op=ALU.mult),
                  reads=[b_pt[tb], b_cH], writes=[b_pt[tb]])

    def rec_PV(u):
        p, hh, kb = units[u]
        tb = u % 3
        jd = kb - 4 * p
        first = True
        for m in range(2):
            for qs in range(max(jd, 0), 4):
                sc.op("pe", lambda e, m=m, qs=qs: e.matmul(
                    psO2[m][:, qs, 0:129], lhsT=PT[tb][:, m, qs * 128:(qs + 1) * 128], rhs=Vb[:, kb, hh, 0:129],
                    start=(kb == 0 and qs % 2 == 0), stop=(kb == 4 * p + qs), skip_group_check=True),
                    reads=[b_pt[tb], b_kqv], writes=[b_psO2] if (first and kb == 0) else (),
                    pwrites=() if (first and kb == 0) else [b_psO2])
                first = False
        if kb == 4 * p + 3:
            epilogue(p, hh)

    def epilogue(p, hh):
        ys = p % 2
        for m in range(2):
            sc.op("dve", lambda e, m=m: e.reciprocal(out=recs[:, m, :], in_=psO2[m][:, :, 128]),
                  reads=[b_psO2], writes=[b_recs] if m == 0 else (), pwrites=() if m == 0 else [b_recs])
        sc.op("dve", lambda e: e.tensor_scalar(out=recs[:, 1, :], in0=recs[:, 1, :], scalar1=lam[:, 0:1], scalar2=None,
                                                op0=ALU.mult), reads=[b_recs, b_lam], writes=[b_recs])
        sc.op("dve", lambda e: e.tensor_tensor(out=T1, in0=psO2[0][:, :, 0:128],
                                               in1=recs[:, 0, :].unsqueeze(2).to_broadcast([128, 4, 128]), op=ALU.mult),
              reads=[b_psO2, b_recs], writes=[b_t1])
        sc.op("dve", lambda e: e.tensor_tensor(out=T2, in0=psO2[1][:, :, 0:128],
                                               in1=recs[:, 1, :].unsqueeze(2).to_broadcast([128, 4, 128]), op=ALU.mult),
              reads=[b_psO2, b_recs], writes=[b_t2])
        sc.op("dve", lambda e: e.tensor_tensor(out=YY, in0=T1, in1=T2, op=ALU.subtract), reads=[b_t1, b_t2], writes=[b_yy])
        sc.op("dve", lambda e: e.tensor_tensor(out=SQ2, in0=YY, in1=YY, op=ALU.mult), reads=[b_yy], writes=[b_sq2])
        sc.op("dve", lambda e: e.tensor_reduce(out=ss2, in_=SQ2, axis=AX.X, op=ALU.add), reads=[b_sq2], writes=[b_ss2])
        rsqrt_ops(rs2, ss2, 1.0 / 128, b_ss2, b_rs2, post_scale=1.0 - LAMBDA_INIT)
        sc.op("dve", lambda e: e.tensor_tensor(out=YY, in0=YY, in1=rs2.unsqueeze(2).to_broadcast([128, 4, 128]), op=ALU.mult),
              reads=[b_yy, b_rs2], writes=[b_yy])
        sc.op("dve", lambda e: e.tensor_tensor(out=YBST[ys][:, :, hh, :], in0=YY,
                                               in1=subg.unsqueeze(1).to_broadcast([128, 4, 128]), op=ALU.mult),
              reads=[b_yy, b_cA], writes=[b_ybst[ys]] if hh == 0 else (), pwrites=() if hh == 0 else [b_ybst[ys]])
        if hh == 1:
            r_ = 512 * (p % 2)
            dst = xin_c[p // 2][r_:r_ + 512, 256:512].rearrange("(i p) c -> p i c", p=128)
            sc.dma("sp", ch_ybst[ys], lambda e: e.dma_start(out=dst, in_=YBST[ys].rearrange("p q h n -> p q (h n)")),
                   reads=[b_ybst[ys]], pwrites=[b_xchin[p // 2]])
            if p >= 3:
                emit_conv(3, pace_buf=b_ybst[ys])
            if p % 2 == 1 and phases >= 3:
                k_ = p // 2
                sc.dma("pool", ch_cc[k_], lambda e: e.collective_compute(
                    "AllGather", ALU.bypass, replica_groups=[[0, 1], [2, 3], [4, 5], [6, 7]],
                    ins=[xin_t[k_].ap().opt()], outs=[xout_t[k_].ap().opt()]),
                    reads=[b_xchin[k_]], writes=[b_xchout[k_]])

    ch_cc = [sc.chan("cc%d" % k, unit=1) for k in range(8)]
    if phases >= 2:
        NU = len(units)
        for u in range(NU + 1):
            if u < NU:
                rec_S(u)
                rec_E(u)
            if u >= 1:
                rec_PV(u - 1)

    if debug:
        ch_dbg = sc.chan("dbg")
        sc.dma("sp", ch_dbg, [lambda e, k=k: e.dma_start(out=dbg[512 * k:512 * (k + 1), :],
                                                         in_=xin_c[k // 2][512 * (k % 2):512 * (k % 2) + 512, :])
                              for k in range(16)], reads=b_xchin)

    emit_conv(100)
    if phases >= 3:
        sc.barrier()
        A.set(G_END)
        MIXC = [A.bf(2, 2, 512) for _ in range(2)]
        b_mixc = [Buf("mixc0"), Buf("mixc1")]
        ch_mixc = [sc.chan("mixc0"), sc.chan("mixc1")]
        MIXN = [A.bf(2, 512) for _ in range(2)]
        b_mixn = [Buf("mixn0"), Buf("mixn1")]
        MIXT = A.bf(8, 512)
        b_mixt = Buf("mixt")
        TMPS = A.bf(2, 512)
        b_tmps = Buf("tmps")
        H2B = [A.bf(1024) for _ in range(2)]
        b_h2b = [Buf("h2b0"), Buf("h2b1")]
        H2T = A.bf(8, 512)
        b_h2t = Buf("h2t")
        H2TH = A.bf(8, 128)
        b_h2th = Buf("h2th")
        WUP = [A.bf(8, 2, 128) for _ in range(3)]
        b_wup = [Buf("wup%d" % k) for k in range(3)]
        ch_wup = [sc.chan("wup%d" % k) for k in range(3)]
        UB = [A.f32(516) for _ in range(4)]
        b_ub = [Buf("ub%d" % k) for k in range(4)]
        CBUF = [A.f32(512) for _ in range(4)]
        b_cbuf = [Buf("cbuf%d" % k) for k in range(4)]
        GG = [A.f32(512) for _ in range(2)]
        b_gg = [Buf("gg0"), Buf("gg1")]
        AT = A.bf(32, 512)
        b_at = [Buf("at%d" % g) for g in range(8)]
        NWD = 4
        WD = [A.bf(4, 512) for _ in range(NWD)]
        b_wd = [Buf("wd%d" % k) for k in range(NWD)]
        ch_wd = [sc.chan("wd%d" % k) for k in range(NWD)]
        FT = A.f32(4, 1024)
        b_ft = Buf("ft")
        ch_out = sc.chan("out")
        SQ3 = A.bf(1024)
        b_sq3 = Buf("sq3")
        TMP3 = A.f32(512)
        b_tmp3 = Buf("tmp3")
        ST3 = A.f32(64)
        ssa = ST3[:, 0:2]
        rsa = ST3[:, 2:4]
        ssw = ST3[:, 4:12].rearrange("p (i h) -> p i h", h=2)
        rsw = ST3[:, 12:16]
        ssh = ST3[:, 16:20]
        rsh = ST3[:, 20:24]
        ssf = ST3[:, 24:32].rearrange("p (i h) -> p i h", h=2)
        rsf = ST3[:, 32:36]
        b_ssa, b_rsa, b_ssw, b_rsw = Buf("ssa"), Buf("rsa"), Buf("ssw"), Buf("rsw")
        b_ssh, b_rsh, b_ssf, b_rsf = Buf("ssh"), Buf("rsh"), Buf("ssf"), Buf("rsf")

        assert A.off <= HI_BASE, A.off
        psT3 = bank(0).bitcast(BF16)
        b_psT3 = Buf("psT3")
        psM = ps[:, 512:1536]
        b_psM = Buf("psM")
        psU = [bank(1), bank(2), bank(3)]
        b_psU = [b_psM, Buf("psU1"), Buf("psU2")]
        b_psU[1] = b_psM
        b_psU = [Buf("bank1"), Buf("bank2"), Buf("bank3")]
        psD = [bank(4), bank(5), bank(6), bank(7)]
        b_psD = [Buf("psD%d" % k) for k in range(4)]
        sc.op("pool", lambda e: e.memset(uhalo, 0.0), writes=[b_uhalo])

        cnt3 = {"mixc": 0, "h2b": 0, "wup": 0, "psu": 0, "ub": 0, "cb": 0, "gg": 0, "wd": 0}

        mstate = {}

        def mix_A(t, i):
            ms = cnt3["mixc"] % 2
            cnt3["mixc"] += 1
            mstate[(t, i)] = {"ms": ms}
            fns = []
            if t < 0:
                fns.append(lambda e: e.dma_start(out=MIXC[ms][:, 1, :, :],
                                                 in_=xout_c[3][:, 896:1024, :].rearrange("s p c -> p s c")))
                rd = [b_xchout[3]]
            else:
                r0_ = 512 * t + 128 * i
                k0_, rr_ = r0_ // 1024, r0_ % 1024
                fns.append(lambda e: e.dma_start(out=MIXC[ms][:, 0, :, :],
                                                 in_=xout_c[k0_][:, rr_:rr_ + 128, :].rearrange("s p c -> p s c")))
                fns.append(lambda e: e.dma_start(out=MIXC[ms][:, 1, :, :],
                                                 in_=xout_c[4 + k0_][:, rr_:rr_ + 128, :].rearrange("s p c -> p s c")))
                rd = [b_xchout[k0_], b_xchout[4 + k0_]]
            sc.dma("sp", ch_mixc[ms], fns, reads=rd, writes=[b_mixc[ms]])
            if t < 0:
                sc.op("dve", lambda e: e.tensor_scalar(out=MIXN[ms], in0=MIXC[ms][:, 1, :, :], scalar1=selsc[:, 1:2],
                                                        scalar2=None, op0=ALU.mult),
                      reads=[b_mixc[ms], b_cB], writes=[b_mixn[ms]])
            else:
                sc.op("dve", lambda e: e.tensor_scalar(out=TMPS, in0=MIXC[ms][:, 1, :, :], scalar1=selsc[:, 1:2],
                                                        scalar2=None, op0=ALU.mult),
                      reads=[b_mixc[ms], b_cB], writes=[b_tmps])
                sc.op("dve", lambda e: e.scalar_tensor_tensor(out=MIXN[ms], in0=MIXC[ms][:, 0, :, :], scalar=selsc[:, 0:1],
                                                               in1=TMPS, op0=ALU.mult, op1=ALU.add),
                      reads=[b_mixc[ms], b_cB, b_tmps], writes=[b_mixn[ms]])
            sc.op("act", lambda e: e.activation(out=SQ3[:, 0:512].rearrange("p (s n) -> p s n", s=2),
                                                in_=MIXN[ms][:, :, 0:256], func=AF.Square, accum_out=ssa[:, ms:ms + 1]),
                  reads=[b_mixn[ms]], writes=[b_sq3, b_ssa])
            rsqrt_ops(rsa[:, ms:ms + 1], ssa[:, ms:ms + 1], 1.0 / 512, b_ssa, b_rsa)
            sc.op("dve", lambda e: e.scalar_tensor_tensor(out=MIXN[ms][:, :, 0:256], in0=MIXN[ms][:, :, 0:256],
                                                           scalar=rsa[:, ms:ms + 1], in1=swag, op0=ALU.mult, op1=ALU.mult),
                  reads=[b_mixn[ms], b_rsa, b_cB], writes=[b_mixn[ms]])

        def mix_B(t, i):
            ms = mstate[(t, i)]["ms"]
            mixn_flat = MIXN[ms].rearrange("p s n -> p (s n)")
            for c in range(8):
                sc.op("pe", lambda e, c=c: e.transpose(out=psT3[:, c * 128:(c + 1) * 128],
                                                       in_=mixn_flat[:, c * 128:(c + 1) * 128], identity=ident),
                      reads=[b_mixn[ms], b_cH], writes=[b_psT3] if c == 0 else (), pwrites=() if c == 0 else [b_psT3])
            evac(MIXT[:, :, i * 128:(i + 1) * 128], psT3.rearrange("p (c t) -> p c t", c=8),
                 reads=[b_psT3], writes=[b_mixt])

        def mix_C(t, i):
            s = t % 2
            for half in range(2):
                for c in range(8):
                    sc.op("pe", lambda e, c=c, half=half: e.matmul(
                        psM[:, 512 * half:512 * (half + 1)], lhsT=MIXT[:, c, i * 128:(i + 1) * 128],
                        rhs=Wout[:, c, 512 * half:512 * (half + 1)], start=(c == 0), stop=(c == 7)),
                        reads=[b_mixt, b_wout], writes=[b_psU[half]] if c == 0 else (),
                        pwrites=() if c == 0 else [b_psU[half]])
                sc.op("act", lambda e, half=half: e.activation(out=SQ3[:, 0:512], in_=psM[:, 512 * half:512 * (half + 1)],
                                                               func=AF.Square, accum_out=ssw[:, i, half:half + 1]),
                      reads=[b_psU[half]], writes=[b_sq3, b_ssw] if half == 0 else [b_sq3],
                      pwrites=[] if half == 0 else [b_ssw])
            sc.op("dve", lambda e: e.tensor_tensor(out=rsw[:, i:i + 1], in0=ssw[:, i, 0:1], in1=ssw[:, i, 1:2], op=ALU.add),
                  reads=[b_ssw], writes=[b_rsw])
            rsqrt_ops(rsw[:, i:i + 1], rsw[:, i:i + 1], 1.0 / D, b_rsw, b_rsw)
            for half in range(2):
                hsl = slice(512 * half, 512 * (half + 1))
                sc.op("dve", lambda e, hsl=hsl: e.scalar_tensor_tensor(out=TMP3, in0=psM[:, hsl], scalar=rsw[:, i:i + 1],
                                                                        in1=gpost[:, hsl], op0=ALU.mult, op1=ALU.mult),
                      reads=[b_psU[half], b_rsw, b_cB], writes=[b_tmp3])
                sc.op("dve", lambda e, hsl=hsl: e.tensor_tensor(out=XO[s][:, i, hsl], in0=TMP3, in1=XO[s][:, i, hsl], op=ALU.add),
                      reads=[b_tmp3, b_xo[s]], pwrites=[b_xo[s]])
            sc.op("act", lambda e: e.activation(out=SQ3, in_=XO[s][:, i, :], func=AF.Square, accum_out=ssh[:, i:i + 1]),
                  reads=[b_xo[s]], writes=[b_sq3, b_ssh])
            rsqrt_ops(rsh[:, i:i + 1], ssh[:, i:i + 1], 1.0 / D, b_ssh, b_rsh)
            hs = cnt3["h2b"] % 2
            cnt3["h2b"] += 1
            mstate[(t, i)]["hs"] = hs
            sc.op("dve", lambda e: e.scalar_tensor_tensor(out=H2B[hs], in0=XO[s][:, i, :], scalar=rsh[:, i:i + 1],
                                                           in1=gffn, op0=ALU.mult, op1=ALU.mult),
                  reads=[b_xo[s], b_rsh, b_cB], writes=[b_h2b[hs]])

        def mix_D(t, i):
            hs = mstate[(t, i)]["hs"]
            for c in range(8):
                sc.op("pe", lambda e, c=c: e.transpose(out=psT3[:, c * 128:(c + 1) * 128],
                                                       in_=H2B[hs][:, c * 128:(c + 1) * 128], identity=ident),
                      reads=[b_h2b[hs], b_cH], writes=[b_psT3] if c == 0 else (), pwrites=() if c == 0 else [b_psT3])
            if t < 0:
                evac(H2TH, psT3.rearrange("p (c t) -> p c t", c=8), reads=[b_psT3], writes=[b_h2th])
            else:
                evac(H2T[:, :, i * 128:(i + 1) * 128], psT3.rearrange("p (c t) -> p c t", c=8),
                     reads=[b_psT3], writes=[b_h2t] if i == 0 else (), pwrites=() if i == 0 else [b_h2t])

        MIX_FN = {"A": mix_A, "B": mix_B, "C": mix_C, "D": mix_D}
        MIX_HOOKS = {0: ["A0", "A1", "B0"], 1: ["C0"], 3: ["B1", "A2"], 4: ["C1"], 5: ["D0"], 6: ["B2", "A3"],
                     7: ["C2"], 8: ["D1"], 9: ["B3"], 10: ["C3"], 11: ["D2"], 14: ["D3"]}

        wup_state = {"next": 0}
        wup_sched = []

        def load_wup(idx):
            if idx >= len(wup_sched):
                return
            _, pair = wup_sched[idx]
            k = idx % 3
            src = wup_bf[128 * pair:128 * (pair + 1), :]
            sc.dma("sp", ch_wup[k], lambda e: e.dma_start(out=WUP[k].rearrange("p c t n -> p (c t n)"), in_=src),
                   reads=[b_wupbf], writes=[b_wup[k]])

        wd_sched = []

        def load_wd(idx):
            if idx >= len(wd_sched):
                return
            _, half, grp = wd_sched[idx]
            k = idx % NWD
            r_ = (half * 8 + grp) * 128
            src = wdn_bf[r_:r_ + 128, :]
            sc.dma("sp", ch_wd[k], lambda e: e.dma_start(out=WD[k].rearrange("p f n -> p (f n)"), in_=src),
                   reads=[b_wdnbf], writes=[b_wd[k]])

        for t in range(8):
            for pair in range(32):
                wup_sched.append((t, pair))
        for t in range(8):
            for half in range(2):
                for grp in range(8):
                    wd_sched.append((t, half, grp))
        wup_idx = {"i": 0}
        wd_idx = {"i": 0}

        def store_out_block(t, i):
            dst = out_d[512 * t + 128 * i:512 * t + 128 * (i + 1), :]
            sc.dma("sp", ch_out, lambda e: e.dma_start(out=dst, in_=FT[:, i, :]), reads=[b_ft])

        def load_xo_block(t, i):
            s = t % 2
            src = x_own[512 * t + 128 * i:512 * t + 128 * (i + 1), :]
            sc.dma("sp", ch_xo[s], lambda e: e.dma_start(out=XO[s][:, i, :], in_=src),
                   writes=[b_xo[s]] if i == 0 else (), pwrites=() if i == 0 else [b_xo[s]])

        STORE_AT = {3: 0, 9: 1, 15: 2, 21: 3}
        LOADX_AT = {6: 0, 12: 1, 18: 2, 24: 3}

        def ffn_up(t):
            psH = bank(0)
            for pair in range(32):
                idx = wup_idx["i"]
                wup_idx["i"] += 1
                load_wup(idx + 2)
                if t >= 1 and pair in STORE_AT:
                    store_out_block(t - 1, STORE_AT[pair])
                if t >= 1 and t + 1 < 8 and pair in LOADX_AT:
                    load_xo_block(t + 1, LOADX_AT[pair])
                k = idx % 3
                cbs = []
                for tt in range(2):
                    fc = tt * 32 + pair
                    if t == 0:
                        for d in range(8):
                            sc.op("pe", lambda e, d=d, tt=tt, fc=fc, k=k: e.matmul(
                                psH[:, 2 * fc:2 * fc + 2], lhsT=WUP[k][:, d, tt, :], rhs=H2TH[:, d, 126:128],
                                start=(d == 0), stop=(d == 7)),
                                reads=[b_wup[k], b_h2th], writes=[b_psT3] if d == 0 else (),
                                pwrites=() if d == 0 else [b_psT3])
                        sc.op("act", lambda e, fc=fc: e.copy(out=uhalo[:, fc, :], in_=psH[:, 2 * fc:2 * fc + 2]),
                              reads=[b_psT3], pwrites=[b_uhalo])
                    pu = cnt3["psu"] % 3
                    cnt3["psu"] += 1
                    for d in range(8):
                        sc.op("pe", lambda e, d=d, tt=tt, pu=pu, k=k: e.matmul(
                            psU[pu], lhsT=WUP[k][:, d, tt, :], rhs=H2T[:, d, :],
                            start=(d == 0), stop=(d == 7)),
                            reads=[b_wup[k], b_h2t], writes=[b_psU[pu]] if d == 0 else (),
                            pwrites=() if d == 0 else [b_psU[pu]])
                    ub = cnt3["ub"] % 4
                    cnt3["ub"] += 1
                    sc.op("pool", lambda e, fc=fc, ub=ub: e.tensor_copy(out=UB[ub][:, 0:2], in_=uhalo[:, fc, :]),
                          reads=[b_uhalo], writes=[b_ub[ub]])
                    sc.op("act", lambda e, pu=pu, ub=ub: e.copy(out=UB[ub][:, 2:514], in_=psU[pu]),
                          reads=[b_psU[pu]], pwrites=[b_ub[ub]])
                    sc.op("pool", lambda e, fc=fc, ub=ub: e.tensor_copy(out=uhalo[:, fc, :], in_=UB[ub][:, 512:514]),
                          reads=[b_ub[ub]], pwrites=[b_uhalo])
                    cbi = cnt3["cb"] % 4
                    cnt3["cb"] += 1
                    cbs.append(cbi)
                    sc.op("act", lambda e, fc=fc, pu=pu, cbi=cbi: e.activation(
                        out=CBUF[cbi], in_=psU[pu], func=AF.Identity, bias=convb[:, fc:fc + 1], scale=convw[:, fc, 2:3]),
                        reads=[b_psU[pu], b_cB], writes=[b_cbuf[cbi]])
                    sc.op("dve", lambda e, fc=fc, ub=ub, cbi=cbi: e.scalar_tensor_tensor(
                        out=CBUF[cbi], in0=UB[ub][:, 1:513], scalar=convw[:, fc, 1:2], in1=CBUF[cbi],
                        op0=ALU.mult, op1=ALU.add), reads=[b_ub[ub], b_cB, b_cbuf[cbi]], writes=[b_cbuf[cbi]])
                    sc.op("dve", lambda e, fc=fc, ub=ub, cbi=cbi: e.scalar_tensor_tensor(
                        out=CBUF[cbi], in0=UB[ub][:, 0:512], scalar=convw[:, fc, 0:1], in1=CBUF[cbi],
                        op0=ALU.mult, op1=ALU.add), reads=[b_ub[ub], b_cB, b_cbuf[cbi]], writes=[b_cbuf[cbi]])
                gi = cnt3["gg"] % 2
                cnt3["gg"] += 1
                sc.op("act", lambda e, gi=gi, c0=cbs[0]: e.activation(out=GG[gi], in_=CBUF[c0], func=AF.Gelu_apprx_tanh),
                      reads=[b_cbuf[cbs[0]]], writes=[b_gg[gi]])
                sc.op("dve", lambda e, gi=gi, c1=cbs[1], pair=pair: e.tensor_tensor(out=AT[:, pair, :], in0=GG[gi], in1=CBUF[c1],
                                                                                     op=ALU.mult),
                      reads=[b_gg[gi], b_cbuf[cbs[1]]], writes=[b_at[pair // 4]] if pair % 4 == 0 else (),
                      pwrites=() if pair % 4 == 0 else [b_at[pair // 4]])

        def ffn_down(t, nxt=None):
            s = t % 2
            hook = {(0, 1): 0, (0, 5): 1, (1, 1): 2, (1, 5): 3}
            for half in range(2):
                for grp in range(8):
                    idx = wd_idx["i"]
                    wd_idx["i"] += 1
                    load_wd(idx + NWD - 1)
                    k = idx % NWD
                    for fi in range(4):
                        f = grp * 4 + fi
                        for blk in range(4):
                            sc.op("pe", lambda e, f=f, fi=fi, blk=blk, k=k: e.matmul(
                                psD[blk], lhsT=AT[:, f, blk * 128:(blk + 1) * 128], rhs=WD[k][:, fi, :],
                                start=(f == 0), stop=(f == 31)),
                                reads=[b_at[grp], b_wd[k]], writes=[b_psD[blk]] if f == 0 else (),
                                pwrites=() if f == 0 else [b_psD[blk]])
                    if nxt is not None:
                        for st in MIX_HOOKS.get(half * 8 + grp, ()):
                            MIX_FN[st[0]](nxt, int(st[1]))
                hsl = slice(512 * half, 512 * (half + 1))
                for blk in range(4):
                    sc.op("act", lambda e, blk=blk, hsl=hsl: e.copy(out=FT[:, blk, hsl], in_=psD[blk]),
                          reads=[b_psD[blk]], writes=[b_ft] if (half == 0 and blk == 0) else (),
                          pwrites=() if (half == 0 and blk == 0) else [b_ft])
                for blk in range(4):
                    sc.op("act", lambda e, blk=blk, hsl=hsl, half=half: e.activation(
                        out=SQ3[:, 0:512], in_=FT[:, blk, hsl], func=AF.Square, accum_out=ssf[:, blk, half:half + 1]),
                        reads=[b_ft], writes=[b_sq3, b_ssf] if (half == 0 and blk == 0) else [b_sq3],
                        pwrites=[] if (half == 0 and blk == 0) else [b_ssf])
            sc.op("dve", lambda e: e.tensor_tensor(out=rsf, in0=ssf[:, :, 0], in1=ssf[:, :, 1], op=ALU.add),
                  reads=[b_ssf], writes=[b_rsf])
            rsqrt_ops(rsf, rsf, 1.0 / D, b_rsf, b_rsf)
            for blk in range(4):
                sc.op("dve", lambda e, blk=blk: e.scalar_tensor_tensor(out=FT[:, blk, :], in0=FT[:, blk, :],
                                                                        scalar=rsf[:, blk:blk + 1], in1=gpost2,
                                                                        op0=ALU.mult, op1=ALU.mult),
                      reads=[b_ft, b_rsf, b_cB], pwrites=[b_ft])
                sc.op("dve", lambda e, blk=blk: e.tensor_tensor(out=FT[:, blk, :], in0=FT[:, blk, :], in1=XO[s][:, blk, :],
                                                                 op=ALU.add),
                      reads=[b_ft, b_xo[s]], pwrites=[b_ft])
            if t == 7:
                for i_ in range(4):
                    store_out_block(t, i_)

        load_wup(0)
        load_wup(1)
        for k_ in range(NWD - 1):
            load_wd(k_)
        for st in ("A-", "A0", "B-", "A1", "C-", "X1", "B0", "D-", "C0", "B1", "A2", "C1", "D0", "B2", "A3", "C2", "D1",
                   "B3", "C3", "D2", "D3"):
            if st == "X1":
                load_xo(1)
            elif st[1] == "-":
                MIX_FN[st[0]](-1, 0)
            else:
                MIX_FN[st[0]](0, int(st[1]))
        for t in range(8):
            ffn_up(t)
            ffn_down(t, nxt=(t + 1 if t + 1 < 8 else None))
    elif debug:
        pass

    fin = Op("sp", None)
    chan_last = {}
    for o in sc.all_ops:
        if o.chan is not None:
            chan_last[id(o.chan)] = o
    for d in chan_last.values():
        fin.deps.append(d)
    for e in COMPUTE:
        cands = [o for o in sc.ops[e] if o.chan is None and o.fn is not None]
        if cands:
            cands[-1].signal = True
            cands[-1].sigto["sp"] = 0
            fin.deps.append(cands[-1])
    sc.ops["sp"].append(fin)

    sc.finalize()

    sem_ctx = []
    sems = {}
    for e in COMPUTE:
        c = nc.semaphore("s_" + e)
        sems[e] = c.__enter__()
        sem_ctx.append(c)
    for ch in sc.chans:
        c = nc.semaphore("c_" + ch.name)
        ch.sem = c.__enter__()
        sem_ctx.append(c)
    with nc.Block() as block:
        sc.emit(nc, block, sems)
    for c in reversed(sem_ctx):
        c.__exit__(None, None, None)
    ctx_ps.__exit__(None, None, None)
    ctx_arena.__exit__(None, None, None)
    return nc


def _alibi_slopes(n):
    def pow2(m):
        start = 2.0 ** (-8.0 / m)
        return [start ** (i + 1) for i in range(m)]
    if math.log2(n).is_integer():
        s = pow2(n)
    else:
        c = 2 ** int(math.floor(math.log2(n)))
        s = pow2(c) + pow2(2 * c)[0::2][: n - c]
    return np.array(sorted(s, reverse=True), dtype=np.float32)


def _const_tables(r):
    slopes = _alibi_slopes(12).astype(np.float64)
    swa_sl = slopes[:8][4 * r:4 * r + 4]
    dif_sl = slopes[8:][2 * r:2 * r + 2]
    k = np.arange(128, dtype=np.float64)
    dbias = np.zeros((128, 2, 64), np.float64)
    for h in range(2):
        for rel in range(64):
            dbias[:, h, rel] = dif_sl[h] * (k - 127.0 - 128.0 * rel)
    q = np.arange(128, dtype=np.float64)
    swab = np.zeros((128, 2, 4, 128), np.float64)
    for g in range(4):
        dprev = q[None, :] + 128.0 - k[:, None]
        swab[:, 0, g, :] = np.where(k[:, None] > q[None, :], -swa_sl[g] * dprev, NEG)
        dcur = q[None, :] - k[:, None]
        swab[:, 1, g, :] = np.where(k[:, None] <= q[None, :], -swa_sl[g] * dcur, NEG)
    maskT = np.where(k[:, None] <= q[None, :], 1.0, 0.0)
    return dbias.astype(np.float32), swab.astype(np.float32), maskT.astype(np.float32)


def _rep(v):
    return np.broadcast_to(np.asarray(v, np.float32).reshape(1, -1), (128, v.size))


_CACHE = {}


def _get_program():
    if "nc" not in _CACHE:
        _CACHE["nc"] = build_program(debug=False)
    return _CACHE["nc"]


def make_in_maps(x, attn_pre_g, w_in, swa_sinks, swa_out_g, diff_lq1, diff_lk1, diff_lq2, diff_lk2, diff_subln_g,
                 w_out, attn_post_g, ffn_pre_g, w_up, conv_w, conv_b, w_down, ffn_post_g):
    f32 = np.float32
    x = np.asarray(x, f32)
    w_in = np.asarray(w_in, f32)[0]
    w_out = np.asarray(w_out, f32)[0]
    w_up = np.asarray(w_up, f32)[0]
    w_down = np.asarray(w_down, f32)[0]
    conv_w = np.asarray(conv_w, f32)[0]
    conv_b = np.asarray(conv_b, f32)[0]
    wup_h = w_up.reshape(8, 128, 2, 32, 128).transpose(3, 1, 0, 2, 4).reshape(4096, 2048)
    wup_h = np.ascontiguousarray(wup_h)
    wdn_h = w_down.reshape(8, 4, 128, 2, 512).transpose(3, 0, 2, 1, 4).reshape(2048, 2048)
    wdn_h = np.ascontiguousarray(wdn_h)
    cw = conv_w.T.reshape(64, 128, 3).transpose(1, 0, 2).reshape(128, 192)
    cbias = conv_b.reshape(64, 128).T
    perm = np.concatenate([np.arange(0, 256), np.arange(512, 768), np.arange(256, 512), np.arange(768, 1024)])
    wout_h = np.ascontiguousarray(w_out[perm, :])
    swag_full = np.asarray(swa_out_g, f32)[0]
    eye = np.eye(128, dtype=f32)
    in_maps = []
    for c in range(8):
        b, r = c // 2, c % 2
        dbias, swab, maskT = _const_tables(r)
        qb = w_in[:, 768 + 256 * r:768 + 256 * r + 256]
        kb = w_in[:, 1280 + 256 * r:1280 + 256 * r + 256]
        qa = w_in[:, 256 * r:256 * r + 256]
        ka = w_in[:, 512 + 64 * r:512 + 64 * r + 64]
        vb = w_in[:, 1792 + 256 * r:1792 + 256 * r + 256]
        va = w_in[:, 640 + 64 * r:640 + 64 * r + 64]
        wsel = np.ascontiguousarray(np.concatenate([qb, kb, qa, ka, ka, vb, va], axis=1))
        assert wsel.shape[1] == NWSEL
        ca = np.zeros((128, CA_N), f32)
        ca[:, 0:128] = dbias.reshape(128, 128)
        ca[:, 128:1152] = swab.reshape(128, 2, 2, 2, 128).transpose(0, 3, 1, 2, 4).reshape(128, 1024)
        ca[:, 1152:2176] = _rep(np.asarray(attn_pre_g, f32)[0])
        ca[:, 2176:2304] = _rep(np.asarray(diff_subln_g, f32)[0])
        ca[:, 2304:2308] = _rep(np.asarray(swa_sinks, f32)[0][4 * r:4 * r + 4])
        lq = np.concatenate([np.asarray(diff_lq1, f32)[0], np.asarray(diff_lk1, f32)[0],
                             np.asarray(diff_lq2, f32)[0], np.asarray(diff_lk2, f32)[0]])
        ca[:, 2320:2576] = _rep(lq)
        cbk = np.zeros((128, CB_N), f32)
        cbk[:, 0:512] = _rep(swag_full)
        cbk[:, 512:1536] = _rep(np.asarray(attn_post_g, f32)[0])
        cbk[:, 1536:2560] = _rep(np.asarray(ffn_pre_g, f32)[0])
        cbk[:, 2560:3584] = _rep(np.asarray(ffn_post_g, f32)[0])
        cbk[:, 3584:3776] = cw
        cbk[:, 3776:3840] = cbias
        cbk[:, 3840] = 1.0 - r
        cbk[:, 3841] = float(r)
        chh = np.zeros((128, 512), f32)
        chh[:, 0:128] = eye
        chh[:, 128:256] = maskT
        chh[:, 256:384] = eye * (1.0 - r)
        chh[:, 384:512] = eye * float(r)
        xs = x[b]
        x_own = np.ascontiguousarray(xs[4096 * r:4096 * r + 4096])
        x_halo = np.ascontiguousarray(xs[3968:4096]) if r == 1 else np.zeros((128, D), f32)
        in_maps.append({
            "x_seq": np.ascontiguousarray(xs), "x_own": x_own, "x_halo": x_halo,
            "wsel": wsel, "wout": wout_h, "wup": wup_h, "wdn": wdn_h,
            "ca": ca, "cb": cbk, "ch": chh.astype(ml_dtypes.bfloat16),
        })
    return in_maps


def kernel(**inputs):
    in_maps = make_in_maps(**inputs)
    nc = _get_program()
    res = run_bass_kernel_spmd(nc, in_maps, core_ids=list(range(8)))
    out = np.zeros((4, S, D), np.float32)
    for c in range(8):
        b, r = c // 2, c % 2
        out[b, 4096 * r:4096 * r + 4096] = np.asarray(res.results[c]["out"], np.float32)
    return out
```

```python
import math
import numpy as np
import ml_dtypes
import concourse.bass as bass
import concourse.mybir as mybir
from concourse.bass_utils import run_bass_kernel_spmd

F32 = mybir.dt.float32
BF16 = mybir.dt.bfloat16
AF = mybir.ActivationFunctionType
ALU = mybir.AluOpType
AX = mybir.AxisListType

S = 8192
D = 1024
NEG = -30000.0
EPS = 1e-6
LAMBDA_INIT = 0.8 - 0.6 * math.exp(0.0)
NWSEL = 1216


class Buf:
    __slots__ = ("name", "w", "r")

    def __init__(self, name):
        self.name = name
        self.w = []
        self.r = []


class Chan:
    __slots__ = ("name", "sem", "count", "unit")

    def __init__(self, name, unit=16):
        self.name = name
        self.sem = None
        self.count = 0
        self.unit = unit


class Op:
    __slots__ = ("eng", "fn", "deps", "signal", "signo", "chan", "chval", "ndma", "pos", "sigto")

    def __init__(self, eng, fn):
        self.eng = eng
        self.fn = fn
        self.deps = []
        self.signal = False
        self.signo = 0
        self.chan = None
        self.chval = 0
        self.ndma = 0
        self.pos = 0
        self.sigto = {}


COMPUTE = ("pe", "act", "dve", "pool")


class Sched:
    def __init__(self):
        self.ops = {e: [] for e in ("pe", "act", "dve", "pool", "sp")}
        self.chans = []
        self.all_ops = []

    def chan(self, name, unit=16):
        c = Chan(name, unit)
        self.chans.append(c)
        return c

    def _add(self, op, reads, writes, pwrites):
        deps = []
        for b in reads:
            for w in b.w:
                deps.append((w, "raw"))
            if b.name.startswith("ps") or b.name.startswith("bank"):
                for r in b.r:
                    if r.eng != op.eng:
                        deps.append((r, "raw"))
        for b in writes:
            for w in b.w:
                deps.append((w, "waw"))
            for r in b.r:
                deps.append((r, "war"))
        for b in pwrites:
            for r in b.r:
                deps.append((r, "war"))
        best = {}
        for d, kind in deps:
            if d is op:
                continue
            if d.chan is None and d.eng == op.eng and op.chan is None:
                if op.eng == "pe":
                    continue
            if d.chan is not None:
                key = ("c", id(d.chan))
                if key not in best or best[key].chval < d.chval:
                    best[key] = d
            else:
                key = ("e", d.eng)
                if key not in best or best[key].pos < d.pos:
                    best[key] = d
        for d in best.values():
            op.deps.append(d)
            if d.chan is None:
                d.signal = True
                d.sigto[op.eng] = 0
        op.pos = len(self.ops[op.eng])
        for b in reads:
            b.r.append(op)
        for b in writes:
            b.w = [op]
            b.r = []
        for b in pwrites:
            b.w.append(op)
        self.ops[op.eng].append(op)
        self.all_ops.append(op)
        return op

    def op(self, eng, fn, reads=(), writes=(), pwrites=()):
        return self._add(Op(eng, fn), reads, writes, pwrites)

    def dma(self, eng, chan, fns, reads=(), writes=(), pwrites=()):
        if not isinstance(fns, (list, tuple)):
            fns = [fns]
        o = Op(eng, fns)
        o.chan = chan
        o.ndma = len(fns)
        chan.count += chan.unit * len(fns)
        o.chval = chan.count
        return self._add(o, reads, writes, pwrites)

    def barrier(self, bufs=()):
        lasts = []
        for e in COMPUTE:
            if self.ops[e]:
                cands = [o for o in self.ops[e] if o.chan is None and o.fn is not None]
                if cands:
                    lasts.append(cands[-1])
        chan_last = {}
        for o in self.all_ops:
            if o.chan is not None:
                chan_last[id(o.chan)] = o
        for e in ("pe", "act", "dve", "pool", "sp"):
            o = Op(e, None)
            for d in lasts:
                if d.eng != e:
                    o.deps.append(d)
                    d.signal = True
                    d.sigto[e] = 0
            for d in chan_last.values():
                o.deps.append(d)
            o.pos = len(self.ops[e])
            self.ops[e].append(o)
            self.all_ops.append(o)
        for b in bufs:
            b.w = []
            b.r = []

    def finalize(self):
        for e in COMPUTE:
            n = 0
            for o in self.ops[e]:
                if o.chan is None and o.signal:
                    n += 1
                    o.signo = n

    def emit(self, nc, block, sems):
        handles = {"pe": "tensor", "act": "scalar", "dve": "vector", "pool": "gpsimd", "sp": "sync"}
        sch = self

        def run(engname):
            def body(eng):
                seen = {}
                for o in sch.ops[engname]:
                    for d in o.deps:
                        if d.chan is not None:
                            key, sem, val = ("c", id(d.chan)), d.chan.sem, d.chval
                        else:
                            key, sem, val = ("e", d.eng), sems[d.eng], d.signo
                        if seen.get(key, 0) >= val:
                            continue
                        seen[key] = val
                        eng.wait_ge(sem, val)
                    if o.fn is None:
                        continue
                    if o.chan is not None:
                        for f in o.fn:
                            ins = f(eng)
                            if o.chan.unit == 16:
                                ins.then_inc(o.chan.sem, 16)
                            else:
                                ins.then_inc(o.chan.sem)
                    else:
                        ins = o.fn(eng)
                        if o.signal:
                            ins.then_inc(sems[engname], 1)
            return body

        block.tensor(run("pe"))
        block.scalar(run("act"))
        block.vector(run("dve"))
        block.gpsimd(run("pool"))
        block.sync(run("sp"))


class Arena:
    def __init__(self, ap_bf16, nbytes):
        self.ap = ap_bf16
        self.nbytes = nbytes
        self.off = 0

    def set(self, off):
        self.off = off

    def alloc(self, nbytes, dtype=BF16, shape=None):
        nbytes = (nbytes + 63) // 64 * 64
        assert self.off + nbytes <= self.nbytes, ("arena overflow", self.off, nbytes, self.nbytes)
        a = self.ap[:, self.off // 2:(self.off + nbytes) // 2]
        self.off += nbytes
        if dtype == F32:
            a = a.bitcast(F32)
        return a

    def f32(self, *shape):
        n = int(np.prod(shape))
        a = self.alloc(n * 4, F32)[:, 0:n]
        return _shape(a, shape)

    def bf(self, *shape):
        n = int(np.prod(shape))
        a = self.alloc(n * 2, BF16)[:, 0:n]
        return _shape(a, shape)


def _shape(a, shape):
    if len(shape) == 1:
        return a
    if len(shape) == 2:
        return a.rearrange("p (a b) -> p a b", a=shape[0])
    if len(shape) == 3:
        return a.rearrange("p (a b c) -> p a b c", a=shape[0], b=shape[1])
    raise ValueError(shape)


ARENA_BYTES = 207 * 1024
CA_N = 2576
CB_N = 3856


def build_program(debug=False, phases=3):
    nc = bass.Bass("TRN2", target_bir_lowering=False)
    sc = Sched()
    import os
    SKIP = set(os.environ.get("KSKIP", "").split(","))

    def dram_in(name, shape, dt=F32):
        return nc.dram_tensor(name, list(shape), dt, kind="ExternalInput").ap()

    x_seq = dram_in("x_seq", [S, D])
    x_own = dram_in("x_own", [4096, D])
    x_halo = dram_in("x_halo", [128, D])
    wsel_d = dram_in("wsel", [D, NWSEL])
    wout_d = dram_in("wout", [D, D])
    wup_d = dram_in("wup", [4096, 2048])
    wdn_d = dram_in("wdn", [2048, 2048])
    ca_d = dram_in("ca", [128, CA_N])
    cb_d = dram_in("cb", [128, CB_N])
    ch_d = dram_in("ch", [128, 512], BF16)
    out_d = nc.dram_tensor("out", [4096, D], F32, kind="ExternalOutput").ap()
    wup_bf = nc.dram_tensor("wup_bf", [4096, 2048], BF16)
    wdn_bf = nc.dram_tensor("wdn_bf", [2048, 2048], BF16)
    NCH = 8
    xin_t = [nc.dram_tensor("xch_in%d" % k, [1024, 512], BF16) for k in range(NCH)]
    xout_t = [nc.dram_tensor("xch_out%d" % k, [2048, 512], BF16) for k in range(NCH)]
    xin_c = [t.ap() for t in xin_t]
    xout_c = [t.ap().rearrange("(s t) c -> s t c", s=2) for t in xout_t]
    dbg = None
    if debug:
        dbg = nc.dram_tensor("dbg", [S, 512], BF16, kind="ExternalOutput").ap()

    ctx_arena = nc.sbuf_tensor("arena", [128, ARENA_BYTES // 2], BF16)
    ctx_ps = nc.psum_tensor("ps", [128, 4096], F32)
    arena_t = ctx_arena.__enter__()
    ps = ctx_ps.__enter__()
    A = Arena(arena_t[:, :], ARENA_BYTES)

    def rsqrt_ops(out, in_, inv_n, b_in, b_out, post_scale=1.0):
        sc.op("act", lambda e: e.activation(out=out, in_=in_, func=AF.Ln, bias=EPS, scale=inv_n),
              reads=[b_in], writes=[b_out])
        sc.op("act", lambda e: e.activation(out=out, in_=out, func=AF.Exp, bias=math.log(post_scale), scale=-0.5),
              reads=[b_out], writes=[b_out])

    def bank(b, n=1):
        return ps[:, 512 * b:512 * (b + n)]

    cH = A.bf(512)
    ident = cH[:, 0:128]
    maskT = cH[:, 128:256]
    sel0 = cH[:, 256:384]
    sel1 = cH[:, 384:512]
    uhalo = A.f32(64, 2)
    lam = A.f32(8)
    b_cH = Buf("cH")
    b_uhalo = Buf("uhalo")
    ch_c = sc.chan("consts")
    sc.dma("sp", ch_c, lambda e: e.dma_start(out=cH, in_=ch_d[:, :]), writes=[b_cH])

    ch_wconv = sc.chan("wconv")
    b_wupbf = Buf("wup_bf")
    b_wdnbf = Buf("wdn_bf")
    PACE = A.f32(2)
    b_pace = Buf("pace")
    conv_list = []
    for k in range(32):
        conv_list.append((lambda e, k=k: e.dma_start(out=wup_bf[128 * k:128 * (k + 1), :], in_=wup_d[128 * k:128 * (k + 1), :]),
                          b_wupbf))
    for k in range(16):
        conv_list.append((lambda e, k=k: e.dma_start(out=wdn_bf[128 * k:128 * (k + 1), :], in_=wdn_d[128 * k:128 * (k + 1), :]),
                          b_wdnbf))
    conv_pos = {"i": 0}

    def emit_conv(n, pace_buf=None):
        if conv_pos["i"] >= len(conv_list):
            return
        if pace_buf is not None:
            sc.op("pool", lambda e: e.memset(PACE, 0.0), reads=[pace_buf], writes=[b_pace])
        for _ in range(n):
            if conv_pos["i"] >= len(conv_list):
                return
            fn, buf = conv_list[conv_pos["i"]]
            conv_pos["i"] += 1
            sc.dma("pool", ch_wconv, fn, pwrites=[buf])

    G_END = A.off
    KbT = A.bf(2, S)
    QbT = A.bf(2, S)
    Vb = A.bf(64, 2, 130)
    cA = A.f32(CA_N)
    P12_END = A.off
    dbias = cA[:, 0:128].rearrange("p (h r) -> p h r", h=2)
    swab = cA[:, 128:1152].rearrange("p (k n) -> p k n", k=2)
    gpre = cA[:, 1152:2176]
    subg = cA[:, 2176:2304]
    sinks = cA[:, 2304:2308]
    lqk = cA[:, 2320:2576].rearrange("p (a n) -> p a n", a=4)
    b_cA = Buf("cA")
    ch_ca = sc.chan("ca")
    sc.dma("sp", ch_ca, lambda e: e.dma_start(out=cA, in_=ca_d[:, :]), writes=[b_cA])

    Wsel = A.bf(8, NWSEL)
    b_wsel = Buf("wsel")
    ch_ws = sc.chan("wsel")
    P1_BASE = A.off
    XT = [A.f32(4, 1024) for _ in range(2)]
    b_xt = [Buf("xt0"), Buf("xt1")]
    ch_xt = [sc.chan("xt0"), sc.chan("xt1")]
    HB = [A.bf(1024) for _ in range(2)]
    b_hb = [Buf("hb0"), Buf("hb1")]
    HT = [A.bf(8, 512) for _ in range(2)]
    b_ht = [Buf("ht0"), Buf("ht1")]
    QAT = [A.bf(2, 512) for _ in range(2)]
    b_qat = [Buf("qat0"), Buf("qat1")]
    KAT = [A.bf(512) for _ in range(2)]
    b_kat = [Buf("kat0"), Buf("kat1")]
    VA = [A.bf(4, 66) for _ in range(2)]
    b_va = [Buf("va0"), Buf("va1")]
    STMP = [A.f32(512) for _ in range(2)]
    b_stmp = [Buf("stmp0"), Buf("stmp1")]
    PTS = [[A.bf(512) for _ in range(2)] for _ in range(2)]
    b_pts = [[Buf("pts%d%d" % (a, b)) for b in range(2)] for a in range(2)]
    YAST = [A.bf(4, 256) for _ in range(2)]
    b_yast = [Buf("yast0"), Buf("yast1")]
    ch_yast = [sc.chan("yast0"), sc.chan("yast1")]
    SQJ = A.bf(1024)
    b_sqj = Buf("sqj")
    STAT = A.f32(64)
    ss1 = [STAT[:, 0:4], STAT[:, 4:8]]
    rstd1 = [STAT[:, 8:12], STAT[:, 12:16]]
    sinkexp = STAT[:, 16:20]
    den = [STAT[:, 20:24], STAT[:, 24:28]]
    rec = [STAT[:, 28:32], STAT[:, 32:36]]
    lsum = STAT[:, 40:44]
    b_ss1 = [Buf("ss1a"), Buf("ss1b")]
    b_rstd1 = [Buf("rstd1a"), Buf("rstd1b")]
    b_sinkexp = Buf("sinkexp")
    b_den = [Buf("den0"), Buf("den1")]
    b_rec = [Buf("rec0"), Buf("rec1")]
    b_lam = Buf("lam")
    b_lsum = Buf("lsum")
    LJ = A.f32(4, 64)
    P1_END = A.off

    b_xchin = [Buf("xch_in%d" % k) for k in range(8)]
    b_xchout = [Buf("xch_out%d" % k) for k in range(8)]
    b_kqv = Buf("kqv")
    if "memset4d" not in SKIP:
        sc.op("pool", lambda e: e.memset(Vb[:, :, :, 128:130], 1.0), pwrites=[b_kqv])
    sc.op("pool", lambda e: e.memset(VA[0][:, :, 64:66], 1.0), writes=[b_va[0]])
    sc.op("pool", lambda e: e.memset(VA[1][:, :, 64:66], 1.0), writes=[b_va[1]])
    if "wsel" not in SKIP:
        sc.dma("pool", ch_ws,
               [lambda e, c=c: e.dma_start(out=Wsel[:, c, :], in_=wsel_d[c * 128:(c + 1) * 128, :]) for c in range(8)],
               writes=[b_wsel])

    sc.op("act", lambda e: e.activation(out=sinkexp, in_=sinks, func=AF.Exp), reads=[b_cA], writes=[b_sinkexp])
    b_lj = Buf("lj")
    if "lam" not in SKIP:
      sc.op("dve", lambda e: e.tensor_tensor(out=LJ[:, 0:2, :], in0=lqk[:, 0:4:2, :], in1=lqk[:, 1:4:2, :], op=ALU.mult),
          reads=[b_cA], writes=[b_lj])
    sc.op("dve", lambda e: e.tensor_reduce(out=lsum[:, 0:2], in_=LJ[:, 0:2, :], axis=AX.X, op=ALU.add),
          reads=[b_lj], writes=[b_lsum])
    b_lexp = Buf("lexp")
    sc.op("act", lambda e: e.activation(out=lsum[:, 2:4], in_=lsum[:, 0:2], func=AF.Exp), reads=[b_lsum], writes=[b_lexp])
    sc.op("dve", lambda e: e.scalar_tensor_tensor(out=lam[:, 0:1], in0=lsum[:, 2:3], scalar=LAMBDA_INIT, in1=lsum[:, 3:4],
                                                   op0=ALU.add, op1=ALU.subtract),
          reads=[b_lexp], writes=[b_lam])

    psT = [bank(0).bitcast(BF16), bank(1).bitcast(BF16)]
    b_psT = [Buf("psT0"), Buf("psT1")]
    psF = [bank(2), bank(3)]
    b_psF = [Buf("psF0"), Buf("psF1")]
    psV = bank(4)
    b_psV = Buf("psV")
    psO = bank(5)
    b_psO = Buf("psO")
    psS = [bank(6), bank(7)]
    b_psS = [Buf("psS0"), Buf("psS1")]

    NT = S // 512

    def load_x(j):
        s = j % 2
        src = x_seq[512 * j:512 * (j + 1), :].rearrange("(i p) d -> p i d", p=128)
        sc.dma("sp", ch_xt[s], lambda e: e.dma_start(out=XT[s], in_=src), writes=[b_xt[s]])

    def stage_ss(j):
        s = j % 2
        for i in range(4):
            sc.op("act", lambda e, i=i: e.activation(out=SQJ, in_=XT[s][:, i, :], func=AF.Square,
                                                      accum_out=ss1[s][:, i:i + 1]),
                  reads=[b_xt[s]], writes=[b_sqj, b_ss1[s]] if i == 0 else [b_sqj], pwrites=[] if i == 0 else [b_ss1[s]])
        rsqrt_ops(rstd1[s], ss1[s], 1.0 / D, b_ss1[s], b_rstd1[s])

    cnt = {"hb": 0, "psT": 0, "psF": 0, "evac": 0}

    def evac(out, in_, reads, writes=(), pwrites=()):
        cnt["evac"] += 1
        use_act = False
        if "evacdve" in SKIP:
            use_act = False
        if "evacact" in SKIP:
            use_act = True
        if use_act:
            return sc.op("act", lambda e: e.copy(out=out, in_=in_), reads=reads, writes=writes, pwrites=pwrites)
        return sc.op("dve", lambda e: e.tensor_copy(out=out, in_=in_), reads=reads, writes=writes, pwrites=pwrites)

    nstate = {}

    def norm_a(j, i):
        s = j % 2
        hs = cnt["hb"] % 2
        cnt["hb"] += 1
        nstate[(j, i)] = hs
        sc.op("dve", lambda e: e.scalar_tensor_tensor(out=HB[hs], in0=XT[s][:, i, :], scalar=rstd1[s][:, i:i + 1],
                                                       in1=gpre, op0=ALU.mult, op1=ALU.mult),
              reads=[b_xt[s], b_rstd1[s], b_cA], writes=[b_hb[hs]])

    def norm_b(j, i):
        s = j % 2
        hs = nstate[(j, i)]
        tsl = cnt["psT"] % 2
        cnt["psT"] += 1
        for c in range(8):
            sc.op("pe", lambda e, c=c: e.transpose(out=psT[tsl][:, c * 128:(c + 1) * 128],
                                                   in_=HB[hs][:, c * 128:(c + 1) * 128], identity=ident),
                  reads=[b_hb[hs], b_cH], writes=[b_psT[tsl]] if c == 0 else (), pwrites=() if c == 0 else [b_psT[tsl]])
        evac(HT[s][:, :, i * 128:(i + 1) * 128], psT[tsl].rearrange("p (c t) -> p c t", c=8),
             reads=[b_psT[tsl]], writes=[b_ht[s]] if i == 0 else (), pwrites=() if i == 0 else [b_ht[s]])

    def stage_norm_block(j, i):
        norm_a(j, i)
        norm_b(j, i)

    def proj_fm(j, m):
        s = j % 2
        fs = cnt["psF"] % 2
        cnt["psF"] += 1
        for d in range(8):
            sc.op("pe", lambda e, d=d: e.matmul(psF[fs], lhsT=Wsel[:, d, m * 128:(m + 1) * 128], rhs=HT[s][:, d, :],
                                                start=(d == 0), stop=(d == 7)),
                  reads=[b_wsel, b_ht[s]], writes=[b_psF[fs]] if d == 0 else (), pwrites=() if d == 0 else [b_psF[fs]])
        cols = slice(512 * j, 512 * (j + 1))
        if m < 2:
            evac(QbT[:, m, cols], psF[fs], reads=[b_psF[fs]], pwrites=[b_kqv])
        elif m < 4:
            evac(KbT[:, m - 2, cols], psF[fs], reads=[b_psF[fs]], pwrites=[b_kqv])
        elif m < 6:
            evac(QAT[s][:, m - 4, :], psF[fs], reads=[b_psF[fs]],
                 writes=[b_qat[s]] if m == 4 else (), pwrites=() if m == 4 else [b_qat[s]])
        else:
            evac(KAT[s], psF[fs], reads=[b_psF[fs]], writes=[b_kat[s]])

    def proj_tm(j, i):
        s = j % 2
        for d in range(8):
            sc.op("pe", lambda e, d=d: e.matmul(psV[:, 0:320], lhsT=HT[s][:, d, i * 128:(i + 1) * 128],
                                                rhs=Wsel[:, d, 896:1216], start=(d == 0), stop=(d == 7)),
                  reads=[b_wsel, b_ht[s]], writes=[b_psV] if d == 0 else (), pwrites=() if d == 0 else [b_psV])
        sc.op("act", lambda e: e.copy(out=Vb[:, 4 * j + i, :, 0:128], in_=psV[:, 0:256].rearrange("p (h n) -> p h n", h=2)),
              reads=[b_psV], pwrites=[b_kqv])
        sc.op("act", lambda e: e.copy(out=VA[s][:, i, 0:64], in_=psV[:, 256:320]),
              reads=[b_psV], pwrites=[b_va[s]])

    swa_cnt = {"n": 0}

    sstate = {}

    def swa_a(j, i):
        s = j % 2
        n = 4 * j + i
        blocks = []
        if n > 0:
            blocks.append((0, (j if i > 0 else j - 1), (i - 1) % 4))
        blocks.append((1, j, i))
        pb = swa_cnt["n"] % 2
        swa_cnt["n"] += 1
        c0 = 0 if len(blocks) == 2 else 256
        firstw = [True, True]
        for (kbi, jj, ii) in blocks:
            ss_ = jj % 2
            for g in range(4):
                r0 = 64 * (g % 2)
                par = g % 2
                col = (kbi * 2 + g // 2) * 128
                sc.op("pe", lambda e, g=g, r0=r0, ss_=ss_, ii=ii, par=par, col=col: e.matmul(
                    psS[par][:, col:col + 128],
                    lhsT=KAT[ss_][r0:r0 + 64, ii * 128:(ii + 1) * 128],
                    rhs=QAT[s][r0:r0 + 64, g // 2, i * 128:(i + 1) * 128],
                    start=True, stop=True, tile_position=(r0, 0)),
                    reads=[b_kat[ss_], b_qat[s]], writes=[b_psS[par]] if firstw[par] else (),
                    pwrites=() if firstw[par] else [b_psS[par]])
                firstw[par] = False
        for par in range(2):
            sc.op("dve", lambda e, par=par: e.scalar_tensor_tensor(out=STMP[par][:, c0:512], in0=psS[par][:, c0:512], scalar=0.125,
                                                                    in1=swab[:, par, c0:512], op0=ALU.mult, op1=ALU.add),
                  reads=[b_psS[par], b_cA], writes=[b_stmp[par]])
            sc.op("act", lambda e, par=par: e.activation(out=PTS[par][pb][:, c0:512], in_=STMP[par][:, c0:512], func=AF.Exp),
                  reads=[b_stmp[par]], writes=[b_pts[par][pb]])
        sstate[(j, i)] = (blocks, pb)

    def swa_b(j, i):
        s = j % 2
        n = 4 * j + i
        blocks, pb = sstate[(j, i)]
        psO_v = psO[:, 0:264].rearrange("p (g n) -> p g n", g=4)
        first = True
        for g in range(4):
            for bi, (kbi, jj, ii) in enumerate(blocks):
                ss_ = jj % 2
                par = g % 2
                col = (kbi * 2 + g // 2) * 128
                sc.op("pe", lambda e, g=g, ss_=ss_, ii=ii, bi=bi, par=par, col=col: e.matmul(
                    psO_v[:, g, 0:65], lhsT=PTS[par][pb][:, col:col + 128], rhs=VA[ss_][:, ii, 0:65],
                    start=(bi == 0), stop=(bi == len(blocks) - 1)),
                    reads=[b_pts[par][pb], b_va[ss_]], writes=[b_psO] if first else (),
                    pwrites=() if first else [b_psO])
                first = False
        ds = n % 2
        sc.op("dve", lambda e: e.tensor_tensor(out=den[ds], in0=psO_v[:, :, 64], in1=sinkexp, op=ALU.add),
              reads=[b_psO, b_sinkexp], writes=[b_den[ds]])
        sc.op("dve", lambda e: e.reciprocal(out=rec[ds], in_=den[ds]), reads=[b_den[ds]], writes=[b_rec[ds]])
        sc.op("dve", lambda e: e.tensor_tensor(out=YAST[s][:, i, :].rearrange("p (g n) -> p g n", g=4),
                                               in0=psO_v[:, :, 0:64],
                                               in1=rec[ds].unsqueeze(2).to_broadcast([128, 4, 64]), op=ALU.mult),
              reads=[b_psO, b_rec[ds]], writes=[b_yast[s]] if i == 0 else (), pwrites=() if i == 0 else [b_yast[s]])

    def store_ya(j):
        s = j % 2
        r_ = 512 * (j % 2)
        dst = xin_c[j // 2][r_:r_ + 512, 0:256].rearrange("(i p) c -> p i c", p=128)
        sc.dma("sp", ch_yast[s], lambda e: e.dma_start(out=dst, in_=YAST[s]), reads=[b_yast[s]], pwrites=[b_xchin[j // 2]])

    NT_RUN = int(os.environ.get("KNT", NT))
    if "loadx" not in SKIP:
        load_x(0)
        load_x(1)
    if "ss" not in SKIP:
        stage_ss(0)
    if "norm" not in SKIP:
        for i in range(4):
            stage_norm_block(0, i)
    for j in range(NT_RUN):
        nxt = j + 1 < NT
        if nxt:
            stage_ss(j + 1)
        items = [("Na", 0), ("fm", 4), ("fm", 5), ("fm", 6), ("Nb", 0), ("Na", 1), ("tm", 0), ("Sa", 0), ("tm", 1),
                 ("Nb", 1), ("Na", 2), ("fm", 0), ("Sb", 0), ("Sa", 1), ("tm", 2), ("Nb", 2), ("Na", 3), ("fm", 1),
                 ("Sb", 1), ("Sa", 2), ("tm", 3), ("Nb", 3), ("fm", 2), ("Sb", 2), ("Sa", 3), ("fm", 3), ("Sb", 3)]
        for kind, a in items:
            if kind == "fm":
                proj_fm(j, a)
            elif kind == "tm":
                proj_tm(j, a)
            elif kind == "Sa":
                swa_a(j, a)
            elif kind == "Sb":
                swa_b(j, a)
            elif kind == "Na" and nxt:
                norm_a(j + 1, a)
            elif kind == "Nb" and nxt:
                norm_b(j + 1, a)
        if "store" not in SKIP:
            store_ya(j)
        if j >= 2:
            emit_conv(1, pace_buf=b_yast[j % 2])
        if j + 2 < NT:
            load_x(j + 2)

    sc.barrier()
    A.set(P12_END)
    HI_BASE = 143 * 1024
    if phases >= 3:
        A.set(HI_BASE)
        cB = A.f32(CB_N)
        swag = cB[:, 0:512].rearrange("p (s n) -> p s n", s=2)
        gpost = cB[:, 512:1536]
        gffn = cB[:, 1536:2560]
        gpost2 = cB[:, 2560:3584]
        convw = cB[:, 3584:3776].rearrange("p (c k) -> p c k", k=3)
        convb = cB[:, 3776:3840]
        selsc = cB[:, 3840:3842]
        b_cB = Buf("cB")
        ch_cb = sc.chan("cb")
        sc.dma("sp", ch_cb, lambda e: e.dma_start(out=cB, in_=cb_d[:, :]), writes=[b_cB])
        Wout = A.bf(8, 1024)
        b_wout = Buf("wout")
        ch_wo = sc.chan("wout")
        sc.dma("pool", ch_wo,
               [lambda e, c=c: e.dma_start(out=Wout[:, c, :], in_=wout_d[c * 128:(c + 1) * 128, :]) for c in range(8)],
               writes=[b_wout])
        XO = [A.f32(4, 1024) for _ in range(2)]
        b_xo = [Buf("xo0"), Buf("xo1")]
        ch_xo = [sc.chan("xo0"), sc.chan("xo1")]
        def load_xo(t):
            s = t % 2
            if t < 0:
                src = x_halo[:, :]
                sc.dma("sp", ch_xo[s], lambda e: e.dma_start(out=XO[s][:, 0, :], in_=src), writes=[b_xo[s]])
            else:
                src = x_own[512 * t:512 * (t + 1), :].rearrange("(i p) d -> p i d", p=128)
                sc.dma("sp", ch_xo[s], lambda e: e.dma_start(out=XO[s], in_=src), writes=[b_xo[s]])

        load_xo(-1)
        load_xo(0)
        assert A.off <= ARENA_BYTES
        A.set(P12_END)
    PT = [A.bf(2, 512) for _ in range(4)]
    b_pt = [Buf("pt%d" % k) for k in range(4)]
    T1 = A.f32(4, 128)
    T2 = A.f32(4, 128)
    YY = A.f32(4, 128)
    SQ2 = A.f32(4, 128)
    b_t1, b_t2, b_yy, b_sq2 = Buf("t1"), Buf("t2"), Buf("yy"), Buf("sq2")
    YBST = [A.bf(4, 2, 128) for _ in range(2)]
    b_ybst = [Buf("ybst0"), Buf("ybst1")]
    ch_ybst = [sc.chan("ybst0"), sc.chan("ybst1")]
    ST2 = A.f32(32)
    recs = ST2[:, 0:8].rearrange("p (m q) -> p m q", m=2)
    ss2 = ST2[:, 8:12]
    rs2 = ST2[:, 12:16]
    b_recs, b_ss2, b_rs2 = Buf("recs"), Buf("ss2"), Buf("rs2")

    assert A.off <= HI_BASE, A.off
    psS2 = [ps[:, 0:1024].rearrange("p (m q) -> p m q", m=2), ps[:, 1024:2048].rearrange("p (m q) -> p m q", m=2)]
    b_psS2 = [Buf("psS2a"), Buf("psS2b")]
    psO2 = [ps[:, 2048:3072].rearrange("p (q n) -> p q n", q=4), ps[:, 3072:4096].rearrange("p (q n) -> p q n", q=4)]
    b_psO2 = Buf("psO2")

    units = []
    for p in range(NT):
        for hh in range(2):
            for kb in range(4 * p + 4):
                units.append((p, hh, kb))

    def rec_S(u):
        p, hh, kb = units[u]
        sb = u % 2
        jd = kb - 4 * p
        q0 = 128 * max(jd, 0)
        first = [True]

        def w():
            if first[0]:
                first[0] = False
                return dict(writes=[b_psS2[sb]])
            return dict(pwrites=[b_psS2[sb]])
        for m in range(2):
            r0 = 64 * m
            kk = KbT[r0:r0 + 64, hh, kb * 128:(kb + 1) * 128]
            sc.op("pe", lambda e, m=m, r0=r0, kk=kk: e.matmul(
                psS2[sb][:, m, q0:512], lhsT=kk, rhs=QbT[r0:r0 + 64, hh, 512 * p + q0:512 * (p + 1)],
                start=True, stop=True, tile_position=(r0, 0)), reads=[b_kqv], **w())

    def rec_E(u):
        p, hh, kb = units[u]
        sb = u % 2
        tb = u % 4
        jd = kb - 4 * p
        q0 = 128 * max(jd, 0)
        rel = 4 * p + 3 - kb
        sc.op("act", lambda e: e.activation(out=PT[tb][:, :, q0:512], in_=psS2[sb][:, :, q0:512], func=AF.Exp,
                                            bias=dbias[:, hh, rel:rel + 1], scale=0.125),
              reads=[b_psS2[sb], b_cA], writes=[b_pt[tb]])
        if jd >= 0:
            sc.op("dve", lambda e: e.tensor_tensor(out=PT[tb][:, :, q0:q0 + 128], in0=PT[tb][:, :, q0:q0 + 128],
                                                   in1=maskT.unsqueeze(1).to_broadcast([128, 2, 128]), op=ALU.mult),
                  reads=[b_pt[tb], b_cH], writes=[b_pt[tb]])

    def rec_PV(u):
        p, hh, kb = units[u]
        tb = u % 4
        jd = kb - 4 * p
        first = True
        for m in range(2):
            for qs in range(max(jd, 0), 4):
                sc.op("pe", lambda e, m=m, qs=qs: e.matmul(
                    psO2[m][:, qs, 0:129], lhsT=PT[tb][:, m, qs * 128:(qs + 1) * 128], rhs=Vb[:, kb, hh, 0:129],
                    start=(kb == 0 and qs % 2 == 0), stop=(kb == 4 * p + qs), skip_group_check=True),
                    reads=[b_pt[tb], b_kqv], writes=[b_psO2] if (first and kb == 0) else (),
                    pwrites=() if (first and kb == 0) else [b_psO2])
                first = False
        if kb == 4 * p + 3:
            epilogue(p, hh)

    def epilogue(p, hh):
        ys = p % 2
        for m in range(2):
            sc.op("dve", lambda e, m=m: e.reciprocal(out=recs[:, m, :], in_=psO2[m][:, :, 128]),
                  reads=[b_psO2], writes=[b_recs] if m == 0 else (), pwrites=() if m == 0 else [b_recs])
        sc.op("dve", lambda e: e.tensor_scalar(out=recs[:, 1, :], in0=recs[:, 1, :], scalar1=lam[:, 0:1], scalar2=None,
                                                op0=ALU.mult), reads=[b_recs, b_lam], writes=[b_recs])
        sc.op("dve", lambda e: e.tensor_tensor(out=T1, in0=psO2[0][:, :, 0:128],
                                               in1=recs[:, 0, :].unsqueeze(2).to_broadcast([128, 4, 128]), op=ALU.mult),
              reads=[b_psO2, b_recs], writes=[b_t1])
        sc.op("dve", lambda e: e.tensor_tensor(out=T2, in0=psO2[1][:, :, 0:128],
                                               in1=recs[:, 1, :].unsqueeze(2).to_broadcast([128, 4, 128]), op=ALU.mult),
              reads=[b_psO2, b_recs], writes=[b_t2])
        sc.op("dve", lambda e: e.tensor_tensor(out=YY, in0=T1, in1=T2, op=ALU.subtract), reads=[b_t1, b_t2], writes=[b_yy])
        sc.op("dve", lambda e: e.tensor_tensor(out=SQ2, in0=YY, in1=YY, op=ALU.mult), reads=[b_yy], writes=[b_sq2])
        sc.op("dve", lambda e: e.tensor_reduce(out=ss2, in_=SQ2, axis=AX.X, op=ALU.add), reads=[b_sq2], writes=[b_ss2])
        rsqrt_ops(rs2, ss2, 1.0 / 128, b_ss2, b_rs2, post_scale=1.0 - LAMBDA_INIT)
        sc.op("dve", lambda e: e.tensor_tensor(out=YY, in0=YY, in1=rs2.unsqueeze(2).to_broadcast([128, 4, 128]), op=ALU.mult),
              reads=[b_yy, b_rs2], writes=[b_yy])
        sc.op("dve", lambda e: e.tensor_tensor(out=YBST[ys][:, :, hh, :], in0=YY,
                                               in1=subg.unsqueeze(1).to_broadcast([128, 4, 128]), op=ALU.mult),
              reads=[b_yy, b_cA], writes=[b_ybst[ys]] if hh == 0 else (), pwrites=() if hh == 0 else [b_ybst[ys]])
        if hh == 1:
            r_ = 512 * (p % 2)
            dst = xin_c[p // 2][r_:r_ + 512, 256:512].rearrange("(i p) c -> p i c", p=128)
            sc.dma("sp", ch_ybst[ys], lambda e: e.dma_start(out=dst, in_=YBST[ys].rearrange("p q h n -> p q (h n)")),
                   reads=[b_ybst[ys]], pwrites=[b_xchin[p // 2]])
            if p >= 3:
                emit_conv(3, pace_buf=b_ybst[ys])
            if p % 2 == 1 and phases >= 3:
                k_ = p // 2
                sc.dma("pool", ch_cc[k_], lambda e: e.collective_compute(
                    "AllGather", ALU.bypass, replica_groups=[[0, 1], [2, 3], [4, 5], [6, 7]],
                    ins=[xin_t[k_].ap().opt()], outs=[xout_t[k_].ap().opt()]),
                    reads=[b_xchin[k_]], writes=[b_xchout[k_]])

    ch_cc = [sc.chan("cc%d" % k, unit=1) for k in range(8)]
    if phases >= 2:
        NU = len(units)
        for u in range(NU + 1):
            if u < NU:
                rec_S(u)
                rec_E(u)
            if u >= 1:
                rec_PV(u - 1)

    if debug:
        ch_dbg = sc.chan("dbg")
        sc.dma("sp", ch_dbg, [lambda e, k=k: e.dma_start(out=dbg[512 * k:512 * (k + 1), :],
                                                         in_=xin_c[k // 2][512 * (k % 2):512 * (k % 2) + 512, :])
                              for k in range(16)], reads=b_xchin)

    emit_conv(100)
    if phases >= 3:
        sc.barrier()
        A.set(G_END)
        MIXC = [A.bf(2, 2, 512) for _ in range(2)]
        b_mixc = [Buf("mixc0"), Buf("mixc1")]
        ch_mixc = [sc.chan("mixc0"), sc.chan("mixc1")]
        MIXN = [A.bf(2, 512) for _ in range(2)]
        b_mixn = [Buf("mixn0"), Buf("mixn1")]
        MIXT = A.bf(8, 512)
        b_mixt = Buf("mixt")
        TMPS = A.bf(2, 512)
        b_tmps = Buf("tmps")
        H2B = [A.bf(1024) for _ in range(2)]
        b_h2b = [Buf("h2b0"), Buf("h2b1")]
        H2T = A.bf(8, 512)
        b_h2t = Buf("h2t")
        H2TH = A.bf(8, 128)
        b_h2th = Buf("h2th")
        WUP = [A.bf(8, 2, 128) for _ in range(3)]
        b_wup = [Buf("wup%d" % k) for k in range(3)]
        ch_wup = [sc.chan("wup%d" % k) for k in range(3)]
        UB = [A.f32(516) for _ in range(4)]
        b_ub = [Buf("ub%d" % k) for k in range(4)]
        CBUF = [A.f32(512) for _ in range(4)]
        b_cbuf = [Buf("cbuf%d" % k) for k in range(4)]
        GG = [A.f32(512) for _ in range(2)]
        b_gg = [Buf("gg0"), Buf("gg1")]
        AT = A.bf(32, 512)
        b_at = [Buf("at%d" % g) for g in range(8)]
        NWD = 4
        WD = [A.bf(4, 512) for _ in range(NWD)]
        b_wd = [Buf("wd%d" % k) for k in range(NWD)]
        ch_wd = [sc.chan("wd%d" % k) for k in range(NWD)]
        FT = A.f32(4, 1024)
        b_ft = Buf("ft")
        ch_out = sc.chan("out")
        SQ3 = A.bf(1024)
        b_sq3 = Buf("sq3")
        TMP3 = A.f32(512)
        b_tmp3 = Buf("tmp3")
        ST3 = A.f32(64)
        ssa = ST3[:, 0:2]
        rsa = ST3[:, 2:4]
        ssw = ST3[:, 4:12].rearrange("p (i h) -> p i h", h=2)
        rsw = ST3[:, 12:16]
        ssh = ST3[:, 16:20]
        rsh = ST3[:, 20:24]
        ssf = ST3[:, 24:32].rearrange("p (i h) -> p i h", h=2)
        rsf = ST3[:, 32:36]
        b_ssa, b_rsa, b_ssw, b_rsw = Buf("ssa"), Buf("rsa"), Buf("ssw"), Buf("rsw")
        b_ssh, b_rsh, b_ssf, b_rsf = Buf("ssh"), Buf("rsh"), Buf("ssf"), Buf("rsf")

        assert A.off <= HI_BASE, A.off
        psT3 = bank(0).bitcast(BF16)
        b_psT3 = Buf("psT3")
        psM = ps[:, 512:1536]
        b_psM = Buf("psM")
        psU = [bank(1), bank(2), bank(3), bank(0)]
        b_psU = [b_psM, Buf("psU1"), Buf("psU2")]
        b_psU[1] = b_psM
        b_psU = [Buf("bank1"), Buf("bank2"), Buf("bank3"), b_psT3]
        psD = [bank(4), bank(5), bank(6), bank(7)]
        b_psD = [Buf("psD%d" % k) for k in range(4)]
        sc.op("pool", lambda e: e.memset(uhalo, 0.0), writes=[b_uhalo])

        cnt3 = {"mixc": 0, "h2b": 0, "wup": 0, "psu": 0, "ub": 0, "cb": 0, "gg": 0, "wd": 0}

        mstate = {}

        def mix_A(t, i):
            ms = cnt3["mixc"] % 2
            cnt3["mixc"] += 1
            mstate[(t, i)] = {"ms": ms}
            fns = []
            if t < 0:
                fns.append(lambda e: e.dma_start(out=MIXC[ms][:, 1, :, :],
                                                 in_=xout_c[3][:, 896:1024, :].rearrange("s p c -> p s c")))
                rd = [b_xchout[3]]
            else:
                r0_ = 512 * t + 128 * i
                k0_, rr_ = r0_ // 1024, r0_ % 1024
                fns.append(lambda e: e.dma_start(out=MIXC[ms][:, 0, :, :],
                                                 in_=xout_c[k0_][:, rr_:rr_ + 128, :].rearrange("s p c -> p s c")))
                fns.append(lambda e: e.dma_start(out=MIXC[ms][:, 1, :, :],
                                                 in_=xout_c[4 + k0_][:, rr_:rr_ + 128, :].rearrange("s p c -> p s c")))
                rd = [b_xchout[k0_], b_xchout[4 + k0_]]
            sc.dma("sp", ch_mixc[ms], fns, reads=rd, writes=[b_mixc[ms]])
            if t < 0:
                sc.op("dve", lambda e: e.tensor_scalar(out=MIXN[ms], in0=MIXC[ms][:, 1, :, :], scalar1=selsc[:, 1:2],
                                                        scalar2=None, op0=ALU.mult),
                      reads=[b_mixc[ms], b_cB], writes=[b_mixn[ms]])
            else:
                sc.op("dve", lambda e: e.tensor_scalar(out=TMPS, in0=MIXC[ms][:, 1, :, :], scalar1=selsc[:, 1:2],
                                                        scalar2=None, op0=ALU.mult),
                      reads=[b_mixc[ms], b_cB], writes=[b_tmps])
                sc.op("dve", lambda e: e.scalar_tensor_tensor(out=MIXN[ms], in0=MIXC[ms][:, 0, :, :], scalar=selsc[:, 0:1],
                                                               in1=TMPS, op0=ALU.mult, op1=ALU.add),
                      reads=[b_mixc[ms], b_cB, b_tmps], writes=[b_mixn[ms]])
            sc.op("act", lambda e: e.activation(out=SQ3[:, 0:512].rearrange("p (s n) -> p s n", s=2),
                                                in_=MIXN[ms][:, :, 0:256], func=AF.Square, accum_out=ssa[:, ms:ms + 1]),
                  reads=[b_mixn[ms]], writes=[b_sq3, b_ssa])
            rsqrt_ops(rsa[:, ms:ms + 1], ssa[:, ms:ms + 1], 1.0 / 512, b_ssa, b_rsa)
            sc.op("dve", lambda e: e.scalar_tensor_tensor(out=MIXN[ms][:, :, 0:256], in0=MIXN[ms][:, :, 0:256],
                                                           scalar=rsa[:, ms:ms + 1], in1=swag, op0=ALU.mult, op1=ALU.mult),
                  reads=[b_mixn[ms], b_rsa, b_cB], writes=[b_mixn[ms]])

        def mix_B(t, i):
            ms = mstate[(t, i)]["ms"]
            mixn_flat = MIXN[ms].rearrange("p s n -> p (s n)")
            for c in range(8):
                sc.op("pe", lambda e, c=c: e.transpose(out=psT3[:, c * 128:(c + 1) * 128],
                                                       in_=mixn_flat[:, c * 128:(c + 1) * 128], identity=ident),
                      reads=[b_mixn[ms], b_cH], writes=[b_psT3] if c == 0 else (), pwrites=() if c == 0 else [b_psT3])
            evac(MIXT[:, :, i * 128:(i + 1) * 128], psT3.rearrange("p (c t) -> p c t", c=8),
                 reads=[b_psT3], writes=[b_mixt])

        def mix_C(t, i):
            s = t % 2
            for half in range(2):
                for c in range(8):
                    sc.op("pe", lambda e, c=c, half=half: e.matmul(
                        psM[:, 512 * half:512 * (half + 1)], lhsT=MIXT[:, c, i * 128:(i + 1) * 128],
                        rhs=Wout[:, c, 512 * half:512 * (half + 1)], start=(c == 0), stop=(c == 7)),
                        reads=[b_mixt, b_wout], writes=[b_psU[half]] if c == 0 else (),
                        pwrites=() if c == 0 else [b_psU[half]])
                sc.op("act", lambda e, half=half: e.activation(out=SQ3[:, 0:512], in_=psM[:, 512 * half:512 * (half + 1)],
                                                               func=AF.Square, accum_out=ssw[:, i, half:half + 1]),
                      reads=[b_psU[half]], writes=[b_sq3, b_ssw] if half == 0 else [b_sq3],
                      pwrites=[] if half == 0 else [b_ssw])
            sc.op("dve", lambda e: e.tensor_tensor(out=rsw[:, i:i + 1], in0=ssw[:, i, 0:1], in1=ssw[:, i, 1:2], op=ALU.add),
                  reads=[b_ssw], writes=[b_rsw])
            rsqrt_ops(rsw[:, i:i + 1], rsw[:, i:i + 1], 1.0 / D, b_rsw, b_rsw)
            for half in range(2):
                hsl = slice(512 * half, 512 * (half + 1))
                sc.op("dve", lambda e, hsl=hsl: e.scalar_tensor_tensor(out=TMP3, in0=psM[:, hsl], scalar=rsw[:, i:i + 1],
                                                                        in1=gpost[:, hsl], op0=ALU.mult, op1=ALU.mult),
                      reads=[b_psU[half], b_rsw, b_cB], writes=[b_tmp3])
                sc.op("dve", lambda e, hsl=hsl: e.tensor_tensor(out=XO[s][:, i, hsl], in0=TMP3, in1=XO[s][:, i, hsl], op=ALU.add),
                      reads=[b_tmp3, b_xo[s]], pwrites=[b_xo[s]])
            sc.op("act", lambda e: e.activation(out=SQ3, in_=XO[s][:, i, :], func=AF.Square, accum_out=ssh[:, i:i + 1]),
                  reads=[b_xo[s]], writes=[b_sq3, b_ssh])
            rsqrt_ops(rsh[:, i:i + 1], ssh[:, i:i + 1], 1.0 / D, b_ssh, b_rsh)
            hs = cnt3["h2b"] % 2
            cnt3["h2b"] += 1
            mstate[(t, i)]["hs"] = hs
            sc.op("dve", lambda e: e.scalar_tensor_tensor(out=H2B[hs], in0=XO[s][:, i, :], scalar=rsh[:, i:i + 1],
                                                           in1=gffn, op0=ALU.mult, op1=ALU.mult),
                  reads=[b_xo[s], b_rsh, b_cB], writes=[b_h2b[hs]])

        def mix_D(t, i):
            hs = mstate[(t, i)]["hs"]
            for c in range(8):
                sc.op("pe", lambda e, c=c: e.transpose(out=psT3[:, c * 128:(c + 1) * 128],
                                                       in_=H2B[hs][:, c * 128:(c + 1) * 128], identity=ident),
                      reads=[b_h2b[hs], b_cH], writes=[b_psT3] if c == 0 else (), pwrites=() if c == 0 else [b_psT3])
            if t < 0:
                evac(H2TH, psT3.rearrange("p (c t) -> p c t", c=8), reads=[b_psT3], writes=[b_h2th])
            else:
                evac(H2T[:, :, i * 128:(i + 1) * 128], psT3.rearrange("p (c t) -> p c t", c=8),
                     reads=[b_psT3], writes=[b_h2t] if i == 0 else (), pwrites=() if i == 0 else [b_h2t])

        MIX_FN = {"A": mix_A, "B": mix_B, "C": mix_C, "D": mix_D}
        MIX_HOOKS = {0: ["A0", "A1", "B0"], 1: ["C0"], 3: ["B1", "A2"], 4: ["C1"], 5: ["D0"], 6: ["B2", "A3"],
                     7: ["C2"], 8: ["D1"], 9: ["B3"], 10: ["C3"], 11: ["D2"], 14: ["D3"]}

        wup_state = {"next": 0}
        wup_sched = []

        def load_wup(idx):
            if idx >= len(wup_sched):
                return
            _, pair = wup_sched[idx]
            k = idx % 3
            src = wup_bf[128 * pair:128 * (pair + 1), :]
            sc.dma("sp", ch_wup[k], lambda e: e.dma_start(out=WUP[k].rearrange("p c t n -> p (c t n)"), in_=src),
                   reads=[b_wupbf], writes=[b_wup[k]])

        wd_sched = []

        def load_wd(idx):
            if idx >= len(wd_sched):
                return
            _, half, grp = wd_sched[idx]
            k = idx % NWD
            r_ = (half * 8 + grp) * 128
            src = wdn_bf[r_:r_ + 128, :]
            sc.dma("sp", ch_wd[k], lambda e: e.dma_start(out=WD[k].rearrange("p f n -> p (f n)"), in_=src),
                   reads=[b_wdnbf], writes=[b_wd[k]])

        for t in range(8):
            for pair in range(32):
                wup_sched.append((t, pair))
        for t in range(8):
            for half in range(2):
                for grp in range(8):
                    wd_sched.append((t, half, grp))
        wup_idx = {"i": 0}
        wd_idx = {"i": 0}

        def store_out_block(t, i):
            dst = out_d[512 * t + 128 * i:512 * t + 128 * (i + 1), :]
            sc.dma("sp", ch_out, lambda e: e.dma_start(out=dst, in_=FT[:, i, :]), reads=[b_ft])

        def load_xo_block(t, i):
            s = t % 2
            src = x_own[512 * t + 128 * i:512 * t + 128 * (i + 1), :]
            sc.dma("sp", ch_xo[s], lambda e: e.dma_start(out=XO[s][:, i, :], in_=src),
                   writes=[b_xo[s]] if i == 0 else (), pwrites=() if i == 0 else [b_xo[s]])

        STORE_AT = {3: 0, 9: 1, 15: 2, 21: 3}
        LOADX_AT = {6: 0, 12: 1, 18: 2, 24: 3}

        def ffn_up(t):
            psH = bank(0)
            for pair in range(32):
                idx = wup_idx["i"]
                wup_idx["i"] += 1
                load_wup(idx + 2)
                if t >= 1 and pair in STORE_AT:
                    store_out_block(t - 1, STORE_AT[pair])
                if t >= 1 and t + 1 < 8 and pair in LOADX_AT:
                    load_xo_block(t + 1, LOADX_AT[pair])
                k = idx % 3
                cbs = []
                for tt in range(2):
                    fc = tt * 32 + pair
                    if t == 0:
                        for d in range(8):
                            sc.op("pe", lambda e, d=d, tt=tt, fc=fc, k=k: e.matmul(
                                psH[:, 2 * fc:2 * fc + 2], lhsT=WUP[k][:, d, tt, :], rhs=H2TH[:, d, 126:128],
                                start=(d == 0), stop=(d == 7)),
                                reads=[b_wup[k], b_h2th], writes=[b_psT3] if d == 0 else (),
                                pwrites=() if d == 0 else [b_psT3])
                        sc.op("act", lambda e, fc=fc: e.copy(out=uhalo[:, fc, :], in_=psH[:, 2 * fc:2 * fc + 2]),
                              reads=[b_psT3], pwrites=[b_uhalo])
                    pu = cnt3["psu"] % (3 if t == 0 else 4)
                    cnt3["psu"] += 1
                    for d in range(8):
                        sc.op("pe", lambda e, d=d, tt=tt, pu=pu, k=k: e.matmul(
                            psU[pu], lhsT=WUP[k][:, d, tt, :], rhs=H2T[:, d, :],
                            start=(d == 0), stop=(d == 7)),
                            reads=[b_wup[k], b_h2t], writes=[b_psU[pu]] if d == 0 else (),
                            pwrites=() if d == 0 else [b_psU[pu]])
                    ub = cnt3["ub"] % 4
                    cnt3["ub"] += 1
                    sc.op("pool", lambda e, fc=fc, ub=ub: e.tensor_copy(out=UB[ub][:, 0:2], in_=uhalo[:, fc, :]),
                          reads=[b_uhalo], writes=[b_ub[ub]])
                    sc.op("act", lambda e, pu=pu, ub=ub: e.copy(out=UB[ub][:, 2:514], in_=psU[pu]),
                          reads=[b_psU[pu]], pwrites=[b_ub[ub]])
                    sc.op("pool", lambda e, fc=fc, ub=ub: e.tensor_copy(out=uhalo[:, fc, :], in_=UB[ub][:, 512:514]),
                          reads=[b_ub[ub]], pwrites=[b_uhalo])
                    cbi = cnt3["cb"] % 4
                    cnt3["cb"] += 1
                    cbs.append(cbi)
                    sc.op("act", lambda e, fc=fc, pu=pu, cbi=cbi: e.activation(
                        out=CBUF[cbi], in_=psU[pu], func=AF.Identity, bias=convb[:, fc:fc + 1], scale=convw[:, fc, 2:3]),
                        reads=[b_psU[pu], b_cB], writes=[b_cbuf[cbi]])
                    sc.op("dve", lambda e, fc=fc, ub=ub, cbi=cbi: e.scalar_tensor_tensor(
                        out=CBUF[cbi], in0=UB[ub][:, 1:513], scalar=convw[:, fc, 1:2], in1=CBUF[cbi],
                        op0=ALU.mult, op1=ALU.add), reads=[b_ub[ub], b_cB, b_cbuf[cbi]], writes=[b_cbuf[cbi]])
                    sc.op("dve", lambda e, fc=fc, ub=ub, cbi=cbi: e.scalar_tensor_tensor(
                        out=CBUF[cbi], in0=UB[ub][:, 0:512], scalar=convw[:, fc, 0:1], in1=CBUF[cbi],
                        op0=ALU.mult, op1=ALU.add), reads=[b_ub[ub], b_cB, b_cbuf[cbi]], writes=[b_cbuf[cbi]])
                gi = cnt3["gg"] % 2
                cnt3["gg"] += 1
                sc.op("act", lambda e, gi=gi, c0=cbs[0]: e.activation(out=GG[gi], in_=CBUF[c0], func=AF.Gelu_apprx_tanh),
                      reads=[b_cbuf[cbs[0]]], writes=[b_gg[gi]])
                sc.op("dve", lambda e, gi=gi, c1=cbs[1], pair=pair: e.tensor_tensor(out=AT[:, pair, :], in0=GG[gi], in1=CBUF[c1],
                                                                                     op=ALU.mult),
                      reads=[b_gg[gi], b_cbuf[cbs[1]]], writes=[b_at[pair // 4]] if pair % 4 == 0 else (),
                      pwrites=() if pair % 4 == 0 else [b_at[pair // 4]])

        def ffn_down(t, nxt=None):
            s = t % 2
            hook = {(0, 1): 0, (0, 5): 1, (1, 1): 2, (1, 5): 3}
            for half in range(2):
                for grp in range(8):
                    idx = wd_idx["i"]
                    wd_idx["i"] += 1
                    load_wd(idx + NWD - 1)
                    k = idx % NWD
                    for fi in range(4):
                        f = grp * 4 + fi
                        for blk in range(4):
                            sc.op("pe", lambda e, f=f, fi=fi, blk=blk, k=k: e.matmul(
                                psD[blk], lhsT=AT[:, f, blk * 128:(blk + 1) * 128], rhs=WD[k][:, fi, :],
                                start=(f == 0), stop=(f == 31)),
                                reads=[b_at[grp], b_wd[k]], writes=[b_psD[blk]] if f == 0 else (),
                                pwrites=() if f == 0 else [b_psD[blk]])
                    if nxt is not None:
                        for st in MIX_HOOKS.get(half * 8 + grp, ()):
                            MIX_FN[st[0]](nxt, int(st[1]))
                hsl = slice(512 * half, 512 * (half + 1))
                for blk in range(4):
                    sc.op("act", lambda e, blk=blk, hsl=hsl: e.copy(out=FT[:, blk, hsl], in_=psD[blk]),
                          reads=[b_psD[blk]], writes=[b_ft] if (half == 0 and blk == 0) else (),
                          pwrites=() if (half == 0 and blk == 0) else [b_ft])
                for blk in range(4):
                    sc.op("act", lambda e, blk=blk, hsl=hsl, half=half: e.activation(
                        out=SQ3[:, 0:512], in_=FT[:, blk, hsl], func=AF.Square, accum_out=ssf[:, blk, half:half + 1]),
                        reads=[b_ft], writes=[b_sq3, b_ssf] if (half == 0 and blk == 0) else [b_sq3],
                        pwrites=[] if (half == 0 and blk == 0) else [b_ssf])
            sc.op("dve", lambda e: e.tensor_tensor(out=rsf, in0=ssf[:, :, 0], in1=ssf[:, :, 1], op=ALU.add),
                  reads=[b_ssf], writes=[b_rsf])
            rsqrt_ops(rsf, rsf, 1.0 / D, b_rsf, b_rsf)
            for blk in range(4):
                sc.op("dve", lambda e, blk=blk: e.scalar_tensor_tensor(out=FT[:, blk, :], in0=FT[:, blk, :],
                                                                        scalar=rsf[:, blk:blk + 1], in1=gpost2,
                                                                        op0=ALU.mult, op1=ALU.mult),
                      reads=[b_ft, b_rsf, b_cB], pwrites=[b_ft])
                sc.op("dve", lambda e, blk=blk: e.tensor_tensor(out=FT[:, blk, :], in0=FT[:, blk, :], in1=XO[s][:, blk, :],
                                                                 op=ALU.add),
                      reads=[b_ft, b_xo[s]], pwrites=[b_ft])
            if t == 7:
                for i_ in range(4):
                    store_out_block(t, i_)

        load_wup(0)
        load_wup(1)
        for k_ in range(NWD - 1):
            load_wd(k_)
        for st in ("A-", "A0", "B-", "A1", "C-", "X1", "B0", "D-", "C0", "B1", "A2", "C1", "D0", "B2", "A3", "C2", "D1",
                   "B3", "C3", "D2", "D3"):
            if st == "X1":
                load_xo(1)
            elif st[1] == "-":
                MIX_FN[st[0]](-1, 0)
            else:
                MIX_FN[st[0]](0, int(st[1]))
        for t in range(8):
            ffn_up(t)
            ffn_down(t, nxt=(t + 1 if t + 1 < 8 else None))
    elif debug:
        pass

    fin = Op("sp", None)
    chan_last = {}
    for o in sc.all_ops:
        if o.chan is not None:
            chan_last[id(o.chan)] = o
    for d in chan_last.values():
        fin.deps.append(d)
    for e in COMPUTE:
        cands = [o for o in sc.ops[e] if o.chan is None and o.fn is not None]
        if cands:
            cands[-1].signal = True
            cands[-1].sigto["sp"] = 0
            fin.deps.append(cands[-1])
    sc.ops["sp"].append(fin)

    sc.finalize()

    sem_ctx = []
    sems = {}
    for e in COMPUTE:
        c = nc.semaphore("s_" + e)
        sems[e] = c.__enter__()
        sem_ctx.append(c)
    for ch in sc.chans:
        c = nc.semaphore("c_" + ch.name)
        ch.sem = c.__enter__()
        sem_ctx.append(c)
    with nc.Block() as block:
        sc.emit(nc, block, sems)
    for c in reversed(sem_ctx):
        c.__exit__(None, None, None)
    ctx_ps.__exit__(None, None, None)
    ctx_arena.__exit__(None, None, None)
    return nc


def _alibi_slopes(n):
    def pow2(m):
        start = 2.0 ** (-8.0 / m)
        return [start ** (i + 1) for i in range(m)]
    if math.log2(n).is_integer():
        s = pow2(n)
    else:
        c = 2 ** int(math.floor(math.log2(n)))
        s = pow2(c) + pow2(2 * c)[0::2][: n - c]
    return np.array(sorted(s, reverse=True), dtype=np.float32)


def _const_tables(r):
    slopes = _alibi_slopes(12).astype(np.float64)
    swa_sl = slopes[:8][4 * r:4 * r + 4]
    dif_sl = slopes[8:][2 * r:2 * r + 2]
    k = np.arange(128, dtype=np.float64)
    dbias = np.zeros((128, 2, 64), np.float64)
    for h in range(2):
        for rel in range(64):
            dbias[:, h, rel] = dif_sl[h] * (k - 127.0 - 128.0 * rel)
    q = np.arange(128, dtype=np.float64)
    swab = np.zeros((128, 2, 4, 128), np.float64)
    for g in range(4):
        dprev = q[None, :] + 128.0 - k[:, None]
        swab[:, 0, g, :] = np.where(k[:, None] > q[None, :], -swa_sl[g] * dprev, NEG)
        dcur = q[None, :] - k[:, None]
        swab[:, 1, g, :] = np.where(k[:, None] <= q[None, :], -swa_sl[g] * dcur, NEG)
    maskT = np.where(k[:, None] <= q[None, :], 1.0, 0.0)
    return dbias.astype(np.float32), swab.astype(np.float32), maskT.astype(np.float32)


def _rep(v):
    return np.broadcast_to(np.asarray(v, np.float32).reshape(1, -1), (128, v.size))


_CACHE = {}


def _get_program():
    if "nc" not in _CACHE:
        _CACHE["nc"] = build_program(debug=False)
    return _CACHE["nc"]


def make_in_maps(x, attn_pre_g, w_in, swa_sinks, swa_out_g, diff_lq1, diff_lk1, diff_lq2, diff_lk2, diff_subln_g,
                 w_out, attn_post_g, ffn_pre_g, w_up, conv_w, conv_b, w_down, ffn_post_g):
    f32 = np.float32
    x = np.asarray(x, f32)
    w_in = np.asarray(w_in, f32)[0]
    w_out = np.asarray(w_out, f32)[0]
    w_up = np.asarray(w_up, f32)[0]
    w_down = np.asarray(w_down, f32)[0]
    conv_w = np.asarray(conv_w, f32)[0]
    conv_b = np.asarray(conv_b, f32)[0]
    wup_h = w_up.reshape(8, 128, 2, 32, 128).transpose(3, 1, 0, 2, 4).reshape(4096, 2048)
    wup_h = np.ascontiguousarray(wup_h)
    wdn_h = w_down.reshape(8, 4, 128, 2, 512).transpose(3, 0, 2, 1, 4).reshape(2048, 2048)
    wdn_h = np.ascontiguousarray(wdn_h)
    cw = conv_w.T.reshape(64, 128, 3).transpose(1, 0, 2).reshape(128, 192)
    cbias = conv_b.reshape(64, 128).T
    perm = np.concatenate([np.arange(0, 256), np.arange(512, 768), np.arange(256, 512), np.arange(768, 1024)])
    wout_h = np.ascontiguousarray(w_out[perm, :])
    swag_full = np.asarray(swa_out_g, f32)[0]
    eye = np.eye(128, dtype=f32)
    in_maps = []
    for c in range(8):
        b, r = c // 2, c % 2
        dbias, swab, maskT = _const_tables(r)
        qb = w_in[:, 768 + 256 * r:768 + 256 * r + 256]
        kb = w_in[:, 1280 + 256 * r:1280 + 256 * r + 256]
        qa = w_in[:, 256 * r:256 * r + 256]
        ka = w_in[:, 512 + 64 * r:512 + 64 * r + 64]
        vb = w_in[:, 1792 + 256 * r:1792 + 256 * r + 256]
        va = w_in[:, 640 + 64 * r:640 + 64 * r + 64]
        wsel = np.ascontiguousarray(np.concatenate([qb, kb, qa, ka, ka, vb, va], axis=1))
        assert wsel.shape[1] == NWSEL
        ca = np.zeros((128, CA_N), f32)
        ca[:, 0:128] = dbias.reshape(128, 128)
        ca[:, 128:1152] = swab.reshape(128, 2, 2, 2, 128).transpose(0, 3, 1, 2, 4).reshape(128, 1024)
        ca[:, 1152:2176] = _rep(np.asarray(attn_pre_g, f32)[0])
        ca[:, 2176:2304] = _rep(np.asarray(diff_subln_g, f32)[0])
        ca[:, 2304:2308] = _rep(np.asarray(swa_sinks, f32)[0][4 * r:4 * r + 4])
        lq = np.concatenate([np.asarray(diff_lq1, f32)[0], np.asarray(diff_lk1, f32)[0],
                             np.asarray(diff_lq2, f32)[0], np.asarray(diff_lk2, f32)[0]])
        ca[:, 2320:2576] = _rep(lq)
        cbk = np.zeros((128, CB_N), f32)
        cbk[:, 0:512] = _rep(swag_full)
        cbk[:, 512:1536] = _rep(np.asarray(attn_post_g, f32)[0])
        cbk[:, 1536:2560] = _rep(np.asarray(ffn_pre_g, f32)[0])
        cbk[:, 2560:3584] = _rep(np.asarray(ffn_post_g, f32)[0])
        cbk[:, 3584:3776] = cw
        cbk[:, 3776:3840] = cbias
        cbk[:, 3840] = 1.0 - r
        cbk[:, 3841] = float(r)
        chh = np.zeros((128, 512), f32)
        chh[:, 0:128] = eye
        chh[:, 128:256] = maskT
        chh[:, 256:384] = eye * (1.0 - r)
        chh[:, 384:512] = eye * float(r)
        xs = x[b]
        x_own = np.ascontiguousarray(xs[4096 * r:4096 * r + 4096])
        x_halo = np.ascontiguousarray(xs[3968:4096]) if r == 1 else np.zeros((128, D), f32)
        in_maps.append({
            "x_seq": np.ascontiguousarray(xs), "x_own": x_own, "x_halo": x_halo,
            "wsel": wsel, "wout": wout_h, "wup": wup_h, "wdn": wdn_h,
            "ca": ca, "cb": cbk, "ch": chh.astype(ml_dtypes.bfloat16),
        })
    return in_maps


def kernel(**inputs):
    in_maps = make_in_maps(**inputs)
    nc = _get_program()
    res = run_bass_kernel_spmd(nc, in_maps, core_ids=list(range(8)))
    out = np.zeros((4, S, D), np.float32)
    for c in range(8):
        b, r = c // 2, c % 2
        out[b, 4096 * r:4096 * r + 4096] = np.asarray(res.results[c]["out"], np.float32)
    return out
```
